# Optimizing a Trainium2 kernel written in Bass

```python
import jax, jax.numpy as jnp
from jax import lax
import numpy as np

D_MODEL = 1024
BATCH = 2
SEQ = 8192
DEPTH = 2

CTX_LEN = 256
GRID_W = 64
HEAD_DIM = 64
ROPE_BASE = 10000.0
ATT_HEADS = 8
ATT_KV_HEADS = 2
ATT_GROUP = ATT_HEADS // ATT_KV_HEADS
ATT_WIDTH = ATT_HEADS * HEAD_DIM
KV_WIDTH = ATT_KV_HEADS * HEAD_DIM
WINDOW = 128
BLOCK = 128
ATT_SCALE = HEAD_DIM ** -0.5
NEG_INF = -1e30
FNET_GROUPS = 4
FNET_WIDTH = FNET_GROUPS * HEAD_DIM
GLA_HEADS = 4
GLA_WIDTH = GLA_HEADS * HEAD_DIM
GLA_GATE_RANK = 16
GLA_TAU = 16.0
GLA_CHUNK = 64
GLA_SCALE = HEAD_DIM ** -0.5
MIX_WIDTH = ATT_WIDTH + FNET_WIDTH + GLA_WIDTH
_O1 = ATT_WIDTH
_O2 = _O1 + KV_WIDTH
_O3 = _O2 + KV_WIDTH
_O4 = _O3 + FNET_WIDTH
_O5 = _O4 + GLA_WIDTH
_O6 = _O5 + GLA_WIDTH
_O7 = _O6 + GLA_WIDTH
_O8 = _O7 + GLA_WIDTH
IN_WIDTH = _O8 + 2 * GLA_GATE_RANK
IN_SPLITS = (_O1, _O2, _O3, _O4, _O5, _O6, _O7, _O8)
FFN_HIDDEN = ((8 * D_MODEL // 3 + 255) // 256) * 256

kernel_name = "hybrid_prefix_dit_gqa_fnet_gla"


def rms_norm(x, g, eps=1e-6):
    xf = x.astype(jnp.float32)
    y = xf * lax.rsqrt(jnp.mean(xf * xf, axis=-1, keepdims=True) + eps)
    return (y * g.astype(jnp.float32)).astype(x.dtype)


def split_heads(t, nh):
    return t.reshape(t.shape[0], t.shape[1], nh, HEAD_DIM)


def axial_rope_tables(n):
    rows = n // GRID_W
    row = jnp.repeat(jnp.arange(rows, dtype=jnp.float32), GRID_W)
    col = jnp.tile(jnp.arange(GRID_W, dtype=jnp.float32), rows)
    axis_dim = HEAD_DIM // 2
    inv_freq = ROPE_BASE ** (-jnp.arange(0, axis_dim, 2, dtype=jnp.float32) / axis_dim)
    ang_r = row[:, None] * inv_freq[None, :]
    ang_c = col[:, None] * inv_freq[None, :]
    return (jnp.cos(ang_r), jnp.sin(ang_r), jnp.cos(ang_c), jnp.sin(ang_c))


def _rotate(x, cos, sin):
    m = x.shape[-1] // 2
    x1, x2 = x[..., :m], x[..., m:]
    c = cos[:, None, :]
    s = sin[:, None, :]
    return jnp.concatenate([x1 * c - x2 * s, x2 * c + x1 * s], axis=-1)


def apply_axial_rope(x, rope):
    cos_r, sin_r, cos_c, sin_c = rope
    h = HEAD_DIM // 2
    xf = x.astype(jnp.float32)
    y = jnp.concatenate([_rotate(xf[..., :h], cos_r, sin_r), _rotate(xf[..., h:], cos_c, sin_c)], axis=-1)
    return y.astype(x.dtype)


def context_attention(qc, kc, vc, sink):
    B, L = qc.shape[0], qc.shape[1]
    q = qc.reshape(B, L, ATT_KV_HEADS, ATT_GROUP, HEAD_DIM)
    s = jnp.einsum('bqkgd,bskd->bkgqs', q, kc).astype(jnp.float32) * ATT_SCALE
    sk = sink.astype(jnp.float32).reshape(1, ATT_KV_HEADS, ATT_GROUP, 1, 1)
    m = jnp.maximum(s.max(-1, keepdims=True), sk)
    p = jnp.exp(s - m)
    p = p / (p.sum(-1, keepdims=True) + jnp.exp(sk - m))
    o = jnp.einsum('bkgqs,bskd->bqkgd', p.astype(vc.dtype), vc)
    return o.reshape(B, L, ATT_WIDTH)


def window_attention(q, k, v, kc, vc, sink):
    B, n = q.shape[0], q.shape[1]
    nb = n // BLOCK
    nbr = WINDOW // BLOCK
    span = BLOCK + 2 * WINDOW
    qb = q.reshape(B, nb, BLOCK, ATT_KV_HEADS, ATT_GROUP, HEAD_DIM)
    pad = ((0, 0), (WINDOW, WINDOW), (0, 0), (0, 0))
    kp = jnp.pad(k, pad).reshape(B, nb + 2 * nbr, BLOCK, ATT_KV_HEADS, HEAD_DIM)
    vp = jnp.pad(v, pad).reshape(B, nb + 2 * nbr, BLOCK, ATT_KV_HEADS, HEAD_DIM)
    kb = jnp.concatenate([kp[:, i:i + nb] for i in range(2 * nbr + 1)], axis=2)
    vb = jnp.concatenate([vp[:, i:i + nb] for i in range(2 * nbr + 1)], axis=2)
    qi = jnp.arange(BLOCK)[:, None]
    kj = jnp.arange(span)[None, :]
    band = jnp.abs(qi + WINDOW - kj) <= WINDOW
    kpos = jnp.arange(nb)[:, None] * BLOCK - WINDOW + jnp.arange(span)[None, :]
    valid = band[None] & ((kpos >= 0) & (kpos < n))[:, None, :]
    s_loc = jnp.einsum('bnqkgd,bnskd->bnkgqs', qb, kb).astype(jnp.float32) * ATT_SCALE
    s_loc = jnp.where(valid[None, :, None, None], s_loc, NEG_INF)
    s_ctx = jnp.einsum('bnqkgd,bskd->bnkgqs', qb, kc).astype(jnp.float32) * ATT_SCALE
    sk = sink.astype(jnp.float32).reshape(1, 1, ATT_KV_HEADS, ATT_GROUP, 1, 1)
    m = jnp.maximum(jnp.maximum(s_loc.max(-1, keepdims=True), s_ctx.max(-1, keepdims=True)), sk)
    p_loc = jnp.exp(s_loc - m)
    p_ctx = jnp.exp(s_ctx - m)
    inv = 1.0 / (p_loc.sum(-1, keepdims=True) + p_ctx.sum(-1, keepdims=True) + jnp.exp(sk - m))
    o = (jnp.einsum('bnkgqs,bnskd->bnqkgd', (p_loc * inv).astype(v.dtype), vb)
         + jnp.einsum('bnkgqs,bskd->bnqkgd', (p_ctx * inv).astype(vc.dtype), vc))
    return o.reshape(B, n, ATT_WIDTH)


def fourier_mix(u, w_f):
    B, n = u.shape[0], u.shape[1]
    ug = u.reshape(B, n, FNET_GROUPS, HEAD_DIM).astype(jnp.float32)
    f = jnp.fft.fft2(ug, axes=(1, 3), norm='ortho').real
    y = jnp.einsum('bngc,gce->bnge', f.astype(u.dtype), w_f)
    return y.reshape(B, n, FNET_WIDTH)


def gla_chunked(q, k, v, log_a, s0):
    B, T, H, dk = q.shape
    dv = v.shape[-1]
    nc = T // GLA_CHUNK
    rs = lambda t: t.reshape(B, nc, GLA_CHUNK, H, t.shape[-1]).astype(jnp.float32)
    qc, kc, vc = rs(q), rs(k), rs(v)
    bcum = jnp.cumsum(rs(log_a), axis=2)
    btot = bcum[:, :, -1]
    q_in = qc * jnp.exp(bcum)
    k_in = kc * jnp.exp(-bcum)
    k_out = kc * jnp.exp(btot[:, :, None] - bcum)
    lower = jnp.tril(jnp.ones((GLA_CHUNK, GLA_CHUNK), dtype=bool))
    att = jnp.einsum('bnthd,bnshd->bnhts', q_in, k_in)
    att = jnp.where(lower, att, 0.0)
    o_intra = jnp.einsum('bnhts,bnshe->bnthe', att, vc)
    u = jnp.einsum('bnshd,bnshe->bnhde', k_out, vc)
    decay = jnp.exp(btot)

    def step(S, inp):
        d, uu = inp
        return d[..., None] * S + uu, S

    s_final, s_starts = lax.scan(step, s0.astype(jnp.float32),
                                 (jnp.moveaxis(decay, 1, 0), jnp.moveaxis(u, 1, 0)))
    s_starts = jnp.moveaxis(s_starts, 0, 1)
    o_inter = jnp.einsum('bnthd,bnhde->bnthe', q_in, s_starts)
    o = (o_intra + o_inter).reshape(B, T, H, dv)
    return o, s_final


def gla_gates(z_lr, w_gate, b_gate):
    z = (z_lr @ w_gate + b_gate).astype(jnp.float32)
    la = jax.nn.log_sigmoid(z) / GLA_TAU
    return la.reshape(la.shape[0], la.shape[1], GLA_HEADS, HEAD_DIM)


def gla_mixer(lat, ctxp, wgf, bgf, wgb, bgb, norm_g, need_ctx):
    def prep(q, k, v, r, z):
        q = split_heads(q, GLA_HEADS) * GLA_SCALE
        k = split_heads(k, GLA_HEADS)
        v = split_heads(v, GLA_HEADS)
        laf = gla_gates(z[..., :GLA_GATE_RANK], wgf, bgf)
        lab = gla_gates(z[..., GLA_GATE_RANK:], wgb, bgb)
        return q, k, v, laf, lab

    ql, kl, vl, lfl, lbl = prep(*lat)
    qc, kc, vc, lfc, lbc = prep(*ctxp)
    B = ql.shape[0]
    s0 = jnp.zeros((B, GLA_HEADS, HEAD_DIM, HEAD_DIM), jnp.float32)
    flip = lambda t: t[:, ::-1]
    o_cf, s_cf = gla_chunked(qc, kc, vc, lfc, s0)
    o_cb, s_cb = gla_chunked(flip(qc), flip(kc), flip(vc), flip(lbc), s0)
    o_lf, _ = gla_chunked(ql, kl, vl, lfl, s_cf)
    o_lb, _ = gla_chunked(flip(ql), flip(kl), flip(vl), flip(lbl), s_cb)

    def finish(o, r):
        y = rms_norm(o, norm_g).reshape(o.shape[0], o.shape[1], GLA_WIDTH).astype(r.dtype)
        return y * jax.nn.silu(r)

    y_l = finish(o_lf + flip(o_lb), lat[3])
    y_c = finish(o_cf + flip(o_cb), ctxp[3]) if need_ctx else None
    return y_l, y_c


def token_mixers(h, hc, w_in, q_g, k_g, sink, w_f, wgf, bgf, wgb, bgb, gla_g, rope, need_ctx):
    aq, ak, av, fu, gq, gk, gv, gr, gz = jnp.split(h @ w_in, IN_SPLITS, axis=-1)
    caq, cak, cav, cfu, cgq, cgk, cgv, cgr, cgz = jnp.split(hc @ w_in, IN_SPLITS, axis=-1)
    q = apply_axial_rope(rms_norm(split_heads(aq, ATT_HEADS), q_g), rope)
    k = apply_axial_rope(rms_norm(split_heads(ak, ATT_KV_HEADS), k_g), rope)
    v = split_heads(av, ATT_KV_HEADS)
    kc = rms_norm(split_heads(cak, ATT_KV_HEADS), k_g)
    vc = split_heads(cav, ATT_KV_HEADS)
    att = window_attention(q, k, v, kc, vc, sink)
    four = fourier_mix(fu, w_f)
    gla, gla_c = gla_mixer((gq, gk, gv, gr, gz), (cgq, cgk, cgv, cgr, cgz), wgf, bgf, wgb, bgb, gla_g, need_ctx)
    mix = jnp.concatenate([att, four, gla], axis=-1)
    mix_c = None
    if need_ctx:
        qc = rms_norm(split_heads(caq, ATT_HEADS), q_g)
        att_c = context_attention(qc, kc, vc, sink)
        four_c = fourier_mix(cfu, w_f)
        mix_c = jnp.concatenate([att_c, four_c, gla_c], axis=-1)
    return mix, mix_c


def swiglu(h, w_ffn_in, w_ffn_out):
    gate, up = jnp.split(h @ w_ffn_in, 2, axis=-1)
    return (jax.nn.silu(gate) * up) @ w_ffn_out


def setup_inputs(seed: int = 0) -> dict:
    key = jax.random.key(seed)
    ks = jax.random.split(key, 24)
    nrm = lambda k, shape, scale: jax.random.normal(k, shape, jnp.float32) * scale
    return {
        'x': nrm(ks[0], (BATCH, SEQ, D_MODEL), 1.0),
        'c': nrm(ks[1], (BATCH, D_MODEL), 1.0),
        'ctx': nrm(ks[2], (BATCH, CTX_LEN, D_MODEL), 1.0),
        'c_ctx': nrm(ks[3], (D_MODEL,), 1.0),
        'w_mod': nrm(ks[4], (DEPTH, D_MODEL, 6 * D_MODEL), 0.5 * D_MODEL ** -0.5),
        'b_mod': nrm(ks[5], (DEPTH, 6 * D_MODEL), 0.02),
        'g_norm1': 1.0 + nrm(ks[6], (DEPTH, D_MODEL), 0.02),
        'w_in': nrm(ks[7], (DEPTH, D_MODEL, IN_WIDTH), D_MODEL ** -0.5),
        'q_norm_g': 1.0 + nrm(ks[8], (DEPTH, HEAD_DIM), 0.02),
        'k_norm_g': 1.0 + nrm(ks[9], (DEPTH, HEAD_DIM), 0.02),
        'attn_sink': nrm(ks[10], (DEPTH, ATT_HEADS), 0.5),
        'w_fourier': nrm(ks[11], (DEPTH, FNET_GROUPS, HEAD_DIM, HEAD_DIM), HEAD_DIM ** -0.5),
        'gla_w_gate_f': nrm(ks[12], (DEPTH, GLA_GATE_RANK, GLA_WIDTH), GLA_GATE_RANK ** -0.5),
        'gla_b_gate_f': nrm(ks[13], (DEPTH, GLA_WIDTH), 0.1),
        'gla_w_gate_b': nrm(ks[14], (DEPTH, GLA_GATE_RANK, GLA_WIDTH), GLA_GATE_RANK ** -0.5),
        'gla_b_gate_b': nrm(ks[15], (DEPTH, GLA_WIDTH), 0.1),
        'gla_norm_g': 1.0 + nrm(ks[16], (DEPTH, HEAD_DIM), 0.02),
        'w_out': nrm(ks[17], (DEPTH, MIX_WIDTH, D_MODEL), MIX_WIDTH ** -0.5),
        'g_norm2': 1.0 + nrm(ks[18], (DEPTH, D_MODEL), 0.02),
        'w_ffn_in': nrm(ks[19], (DEPTH, D_MODEL, 2 * FFN_HIDDEN), D_MODEL ** -0.5),
        'w_ffn_out': nrm(ks[20], (DEPTH, FFN_HIDDEN, D_MODEL), FFN_HIDDEN ** -0.5),
    }


def reference(x, c, ctx, c_ctx, w_mod, b_mod, g_norm1, w_in, q_norm_g, k_norm_g, attn_sink, w_fourier,
              gla_w_gate_f, gla_b_gate_f, gla_w_gate_b, gla_b_gate_b, gla_norm_g, w_out, g_norm2,
              w_ffn_in, w_ffn_out):
    n = x.shape[1]
    rope = axial_rope_tables(n)
    xc = ctx
    for l in range(DEPTH):
        need_ctx = l < DEPTH - 1
        mod_l = (jax.nn.silu(c) @ w_mod[l] + b_mod[l])[:, None, :]
        mod_c = (jax.nn.silu(c_ctx) @ w_mod[l] + b_mod[l])[None, None, :]
        sh1, sc1, gt1, sh2, sc2, gt2 = jnp.split(mod_l, 6, axis=-1)
        csh1, csc1, cgt1, csh2, csc2, cgt2 = jnp.split(mod_c, 6, axis=-1)
        h = rms_norm(x, g_norm1[l]) * (1.0 + sc1) + sh1
        hc = rms_norm(xc, g_norm1[l]) * (1.0 + csc1) + csh1
        mix, mix_c = token_mixers(h, hc, w_in[l], q_norm_g[l], k_norm_g[l], attn_sink[l], w_fourier[l],
                                  gla_w_gate_f[l], gla_b_gate_f[l], gla_w_gate_b[l], gla_b_gate_b[l],
                                  gla_norm_g[l], rope, need_ctx)
        x = x + gt1 * (mix @ w_out[l])
        h2 = rms_norm(x, g_norm2[l]) * (1.0 + sc2) + sh2
        x = x + gt2 * swiglu(h2, w_ffn_in[l], w_ffn_out[l])
        if need_ctx:
            xc = xc + cgt1 * (mix_c @ w_out[l])
            hc2 = rms_norm(xc, g_norm2[l]) * (1.0 + csc2) + csh2
            xc = xc + cgt2 * swiglu(hc2, w_ffn_in[l], w_ffn_out[l])
    return x
```

```python
from contextlib import ExitStack
import numpy as np
import ml_dtypes
import concourse.bass as bass
import concourse.mybir as mybir
from concourse.bass_utils import run_bass_kernel_spmd

F32 = mybir.dt.float32
BF16 = mybir.dt.bfloat16
AF = mybir.ActivationFunctionType
ALU = mybir.AluOpType
AX = mybir.AxisListType
NPBF = ml_dtypes.bfloat16

D = 1024
NB = 2
SEQ = 8192
DEPTH = 2
CTX = 256
TOK = 2048
NTL = TOK // 128
NTC = CTX // 128
NT = NTL + NTC
NTOK = NT * 128
NTLF = SEQ // 128
NTF = NTLF + NTC
NTOKF = NTF * 128
HID = 2816
NCH = HID // 128
INW = 2080
EPS = 1e-6
GRID_W = 64
GCH = 128
DBG = {}


class Buf:
    __slots__ = ("w", "r", "excl")

    def __init__(self, excl=False):
        self.w = []
        self.r = []
        self.excl = excl


class Sched:
    def __init__(self, nc, n_dma_sems=10):
        self.nc = nc
        self.eng = {"pe": nc.tensor, "dve": nc.vector, "act": nc.scalar, "pool": nc.gpsimd, "sp": nc.sync}
        self.sems = {}
        self.cnt = {}
        self.seen = {e: {} for e in self.eng}
        for e in self.eng:
            self.sems[e] = nc.alloc_semaphore("s_" + e)
            self.cnt[e] = 0
        self.dsem = {}
        self.dpos = {}
        for q in ("sp", "act", "pool"):
            self.dsem[q] = []
            for i in range(n_dma_sems):
                k = "d_%s_%d" % (q, i)
                self.sems[k] = nc.alloc_semaphore(k)
                self.cnt[k] = 0
                self.dsem[q].append(k)
            self.dpos[q] = 0
        self.all_out = []
        self.rec = None

    def _wait(self, e, deps):
        best = {}
        for (k, v) in deps:
            if v > best.get(k, 0):
                best[k] = v
        for k, v in best.items():
            if e == "pe" and k == "pe":
                continue
            if self.seen[e].get(k, 0) < v:
                self.eng[e].wait_ge(self.sems[k], v)
                self.seen[e][k] = v

    @staticmethod
    def _deps(reads, writes):
        deps = []
        for b in reads:
            deps += b.w
            if b.excl:
                deps += b.r
        for b in writes:
            deps += b.w
            deps += b.r
        return deps

    @staticmethod
    def _commit(tick, reads, writes):
        for b in reads:
            b.r.append(tick)
            if len(b.r) > 64:
                best = {}
                for (k, v) in b.r:
                    if v > best.get(k, 0):
                        best[k] = v
                b.r = list(best.items())
        for b in writes:
            b.w = [tick]
            b.r = []

    def op(self, e, fn, reads=(), writes=()):
        if self.rec is not None:
            self.rec.append(("op", e, fn, tuple(reads), tuple(writes)))
            return None
        self._wait(e, self._deps(reads, writes))
        inst = fn(self.eng[e])
        self.cnt[e] += 1
        inst.then_inc(self.sems[e], 1)
        tick = (e, self.cnt[e])
        self._commit(tick, reads, writes)
        return tick

    def play(self, lists):
        assert self.rec is None
        n = max(len(l) for l in lists)
        for i in range(n):
            for l in lists:
                if i < len(l):
                    it = l[i]
                    if it[0] == "mark":
                        continue
                    if it[0] == "op":
                        self.op(it[1], it[2], it[3], it[4])
                    else:
                        self.dma(it[1], it[2], it[3], it[4], it[5], **it[6])

    def dma(self, q, out, in_, reads=(), writes=(), **kw):
        if self.rec is not None:
            self.rec.append(("dma", q, out, in_, tuple(reads), tuple(writes), kw))
            return None
        k = self.dsem[q][self.dpos[q] % len(self.dsem[q])]
        self.dpos[q] += 1
        deps = []
        for b in reads:
            deps += b.w
        for b in writes:
            deps += [w for w in b.w if not w[0].startswith("d_")]
            deps += b.r
        if self.cnt[k] > 0:
            deps.append((k, self.cnt[k]))
        self._wait(q, deps)
        inst = self.eng[q].dma_start(out=out, in_=in_, **kw)
        self.cnt[k] += 16
        inst.then_inc(self.sems[k], 16)
        tick = (k, self.cnt[k])
        for b in reads:
            b.r.append(tick)
        for b in writes:
            best = {}
            for (kk, v) in b.w + [tick]:
                if kk.startswith("d_") and v > best.get(kk, 0):
                    best[kk] = v
            b.w = list(best.items())
            b.r = []
        return tick

    def barrier(self):
        deps = [(k, v) for k, v in self.cnt.items() if v > 0]
        for e in self.eng:
            self._wait(e, [d for d in deps if d[0] != e])

    def finish(self, bufs):
        deps = []
        for b in bufs:
            deps += b.w
        self._wait("sp", deps)


class SBAlloc:
    LO0 = 16512
    HI0 = 229344

    def __init__(self, nc):
        self.nc = nc
        self.lo = self.LO0
        self.hi = self.HI0
        self.n = 0

    @staticmethod
    def _size(shape, dt):
        n = 1
        for d in shape[1:]:
            n *= d
        b = n * (2 if dt == BF16 else 4)
        return (b + 31) // 32 * 32

    def alloc(self, name, shape, dt, high=False):
        sz = self._size(shape, dt)
        if high:
            self.hi -= sz
            off = self.hi
        else:
            off = self.lo
            self.lo += sz
        assert self.lo <= self.hi, "SBUF overflow allocating %s: lo=%d hi=%d" % (name, self.lo, self.hi)
        self.n += 1
        return self.nc.alloc_sbuf_tensor_at("sb%d_%s" % (self.n, name), list(shape), dt, offset=off)

    def mark(self):
        return (self.lo, self.hi)

    def release(self, m):
        self.lo, self.hi = m


class Ring:
    def __init__(self, alloc, name, shape, dtype, n=2):
        self.items = [(alloc("%s_%d" % (name, i), shape, dtype), Buf()) for i in range(n)]
        self.pos = 0

    def get(self):
        it = self.items[self.pos % len(self.items)]
        self.pos += 1
        return it


def _host_consts():
    c = {}
    c["ident_f"] = np.eye(128, dtype=np.float32)
    s = np.arange(128)[:, None]
    t = np.arange(128)[None, :]
    same = (s // GCH) == (t // GCH)
    cm = np.zeros((128, 4, 128), np.float32)
    cm[:, 0, :] = (same & (s <= t)) * (-1.0 / 16)
    cm[:, 1, :] = (same & (s > t)) * (-1.0 / 16)
    cm[:, 2, :] = (same & (s >= t)) * (-1.0 / 16)
    cm[:, 3, :] = (same & (s < t)) * (-1.0 / 16)
    c["cm"] = cm
    ci = np.zeros((128, 2), np.float32)
    ci[:GCH, 0] = -1.0 / 16
    ci[GCH:, 1] = -1.0 / 16
    c["ci"] = ci
    gm = np.zeros((128, 2, 128), np.float32)
    gm[:, 0, :] = same & (s <= t)
    gm[:, 1, :] = same & (s >= t)
    c["gm"] = gm
    c["bd"] = ((s // 64) == (t // 64)).astype(np.float32)
    c["mp"] = (s >= t).astype(np.float32)
    c["mn"] = (s <= t).astype(np.float32)
    c["cbc"] = _cb_const()
    return c


def _rope_tables(pos0, n):
    pos = np.arange(pos0, pos0 + n)
    row = (pos // GRID_W).astype(np.float32)
    col = (pos % GRID_W).astype(np.float32)
    inv = (10000.0 ** (-np.arange(0, 32, 2, dtype=np.float32) / 32.0)).astype(np.float32)
    ar = row[:, None] * inv[None, :]
    ac = col[:, None] * inv[None, :]
    cr, sr, cc, sc = np.cos(ar), np.sin(ar), np.cos(ac), np.sin(ac)
    cos64 = np.concatenate([cr, cr, cc, cc], axis=1).astype(np.float32)
    sin64 = np.concatenate([-sr, sr, -sc, sc], axis=1).astype(np.float32)
    return np.stack([cos64, sin64], axis=0)


def _dft_tables(s0, ns, n):
    t = np.arange(n, dtype=np.int64)[:, None]
    s = (s0 + np.arange(ns, dtype=np.int64))[None, :]
    ang = (2.0 * np.pi / n) * ((t * s) % n).astype(np.float64)
    out = []
    for f in (np.cos, np.sin):
        m = (f(ang) / np.sqrt(n)).astype(np.float32)
        blk = min(512, ns)
        m = m.reshape(n // 128, 128, ns // blk, blk).transpose(2, 0, 1, 3)
        out.append(m.astype(NPBF))
    return np.stack(out, axis=0)


class Prog:
    def __init__(self, stage):
        self.stage = stage
        self.nc = bass.Bass("TRN2", target_bir_lowering=False)
        self.S = Sched(self.nc)
        self.din = {}
        self.dout = {}
        self.dbuf = {}
        nc = self.nc
        self.sb = SBAlloc(nc)
        self.alloc = self.sb.alloc
        self.banks = [(nc.alloc_psum_tensor("ps%d" % i, [128, 512], F32), Buf(excl=True)) for i in range(8)]
        self.bpos = 0
        self.brange = (0, 8)
        self.pinned = set()
        self.mod_done = False
        self.modv = {}
        self.aout = {}

    def dram(self, name, shape, dt=F32):
        t = self.nc.dram_tensor(name, list(shape), dt, kind="Internal").ap()
        self.dbuf[name] = Buf()
        return t

    def inp(self, name, shape, dt=F32):
        if name in self.din:
            return self.din[name]
        t = self.nc.dram_tensor(name, list(shape), dt, kind="ExternalInput").ap()
        self.din[name] = t
        self.dbuf[name] = Buf()
        return t

    def outp(self, name, shape, dt=F32):
        t = self.nc.dram_tensor(name, list(shape), dt, kind="ExternalOutput").ap()
        self.dout[name] = t
        self.dbuf[name] = Buf()
        return t

    def bank(self, pin=False):
        lo, hi = self.brange
        while True:
            idx = lo + self.bpos % (hi - lo)
            self.bpos += 1
            if idx not in self.pinned:
                break
        if pin:
            self.pinned.add(idx)
        return self.banks[idx]

    def unpin(self, bk):
        for i, (t, b) in enumerate(self.banks):
            if t is bk:
                self.pinned.discard(i)

    def mm(self, out, lhsT, rhs, start, stop, reads, writes):
        return self.S.op("pe", lambda e: e.matmul(out, lhsT, rhs, start=start, stop=stop), reads, writes)

    def tr(self, out, in_, ident, reads, writes):
        return self.S.op("pe", lambda e: e.transpose(out, in_, ident), reads, writes)

    def act(self, out, in_, func, reads, writes, **kw):
        return self.S.op("act", lambda e: e.activation(out=out, in_=in_, func=func, **kw), reads, writes)

    def tt(self, eng, out, in0, in1, op, reads, writes):
        return self.S.op(eng, lambda e: e.tensor_tensor(out=out, in0=in0, in1=in1, op=op), reads, writes)

    def ts(self, eng, out, in0, s1, s2, op0, op1, reads, writes):
        if op1 is None:
            return self.S.op(eng, lambda e: e.tensor_scalar(out=out, in0=in0, scalar1=s1, scalar2=None, op0=op0), reads, writes)
        return self.S.op(eng, lambda e: e.tensor_scalar(out=out, in0=in0, scalar1=s1, scalar2=s2, op0=op0, op1=op1), reads, writes)

    def stt(self, out, in0, scalar, in1, op0, op1, reads, writes):
        return self.S.op("dve", lambda e: e.scalar_tensor_tensor(out=out, in0=in0, scalar=scalar, in1=in1, op0=op0, op1=op1), reads, writes)

    def cp(self, eng, out, in_, reads, writes):
        if eng == "act":
            return self.S.op("act", lambda e: e.copy(out=out, in_=in_), reads, writes)
        return self.S.op(eng, lambda e: e.tensor_copy(out=out, in_=in_), reads, writes)

    def dma(self, q, out, in_, reads, writes, **kw):
        return self.S.dma(q, out, in_, reads, writes, **kw)

    def pipeline(self, tile_fn, tiles):
        S = self.S
        prev = None
        for t in tiles:
            S.rec = []
            tile_fn(t)
            L = S.rec
            S.rec = None
            h = [i for i, it in enumerate(L) if it[0] == "mark"]
            h = h[0] if h else len(L) // 2
            if prev is None:
                S.play([L[:h]])
            else:
                S.play([prev, L[:h]])
            prev = L[h:]
        if prev:
            S.play([prev])

    def recip(self, out, in_, reads, writes):
        return self.S.op("dve", lambda e: e.reciprocal(out=out, in_=in_), reads, writes)

    def reduce_x(self, out, in_, reads, writes):
        return self.S.op("dve", lambda e: e.tensor_reduce(out=out, in_=in_, axis=AX.X, op=ALU.add), reads, writes)

    def rstd(self, out, ss, scale, reads, writes, tmp, tmpb):
        self.act(tmp, ss, AF.Ln, reads, [tmpb], scale=scale, bias=self.eps_t[:, 0:1])
        self.act(out, tmp, AF.Exp, [tmpb], writes, scale=-0.5)

    def setup_common(self):
        nc, S = self.nc, self.S
        A = self.alloc
        self.eps_t = A("eps_t", [128, 4], F32)
        self.eps_b = Buf()
        S.op("dve", lambda e: e.memset(self.eps_t[:, 0:1], EPS), [], [self.eps_b])
        S.op("dve", lambda e: e.memset(self.eps_t[:, 1:2], 1.0), [], [self.eps_b])
        S.op("dve", lambda e: e.memset(self.eps_t[:, 2:3], float(np.log(0.125))), [], [self.eps_b])
        S.op("dve", lambda e: e.memset(self.eps_t[:, 3:4], 0.0), [], [self.eps_b])
        self.ident_f = A("ident_f", [128, 128], F32)
        self.ident_b = A("ident_b", [128, 128], BF16)
        self.ones_r = A("ones_r", [1, 128], F32)
        self.cb = Buf()
        d = self.inp("ident_f", [128, 128])
        self.dma("sp", self.ident_f[:], d, [], [self.cb])
        self.dma("pool", self.ident_b[:], d, [], [self.cb])
        S.op("dve", lambda e: e.memset(self.ones_r[:], 1.0), [], [self.cb])

    def bcast_row(self, row_ap, rowb, n):
        outs = []
        for c0 in range(0, n, 512):
            c1 = min(n, c0 + 512)
            bk, bb = self.bank()
            self.mm(bk[:, 0:c1 - c0], self.ones_r[0:1, :], row_ap[0:1, c0:c1], True, True, [self.cb, rowb], [bb])
            outs.append((bk, bb, c0, c1))
        return outs

    def A_mod(self):
        nc, S, A = self.nc, self.S, self.alloc
        sfx = "_A"
        db = self.dbuf
        cT_d = self.inp("cT", [128, 8, 2])
        wmod_d = self.inp("wmod" + sfx, [D, 6 * D])
        bmod_d = self.inp("bmod" + sfx, [6 * D])
        MODV = self.outp("MODV" + sfx, [2, 6 * D])
        mk = self.sb.mark()
        cTs = A("cTs", [128, 8, 2], F32)
        cTb = Buf()
        self.dma("sp", cTs[:], cT_d, [], [cTb])
        scT = A("scT", [128, 8, 64], BF16)
        scb = Buf()
        S.op("dve", lambda e: e.memset(scT[:], 0.0), [], [scb])
        sil = A("sil", [128, 8, 2], F32)
        silb = Buf()
        self.act(sil[:], cTs[:], AF.Exp, [cTb], [silb], scale=-1.0)
        self.ts("dve", sil[:], sil[:], 1.0, None, ALU.add, None, [silb], [silb])
        S.op("dve", lambda e: e.reciprocal(out=sil[:], in_=sil[:]), [silb], [silb])
        self.tt("dve", sil[:], sil[:], cTs[:], ALU.mult, [silb, cTb], [silb])
        self.cp("dve", scT[:, :, 0:1], sil[:, :, 0:1], [silb], [scb])
        self.cp("dve", scT[:, :, 32:33], sil[:, :, 1:2], [silb], [scb])
        wm_ring = Ring(A, "wm", [128, 8, 512], BF16, 2)
        bm_ring = Ring(A, "bm", [64, 512], F32, 2)
        mr_ring = Ring(A, "mr", [64, 512], F32, 2)
        for cbk in range(12):
            wm, wmb = wm_ring.get()
            self.dma("pool", wm[:], wmod_d[:, cbk * 512:(cbk + 1) * 512].rearrange("(k p) c -> p k c", p=128), [], [wmb])
            bm, bmb = bm_ring.get()
            self.dma("sp", bm[:], bmod_d[cbk * 512:(cbk + 1) * 512].partition_broadcast(64), [], [bmb])
            bk, bb = self.bank()
            for k in range(8):
                self.mm(bk[0:64, :], scT[:, k, :], wm[:, k, :], k == 0, k == 7, [scb, wmb], [bb])
            mr, mrb = mr_ring.get()
            self.tt("dve", mr[:], bk[0:64, :], bm[:], ALU.add, [bb, bmb], [mrb])
            self.dma("sp", MODV[0:1, cbk * 512:(cbk + 1) * 512], mr[0:1, :], [mrb], [db["MODV" + sfx]])
            self.dma("sp", MODV[1:2, cbk * 512:(cbk + 1) * 512], mr[32:33, :], [mrb], [db["MODV" + sfx]])

        S.barrier()
        self.sb.release(mk)
        self.mod_done = True

    def phase_A(self, l, xin, xin_b, first_stage):
        nc, S, A = self.nc, self.S, self.alloc
        sfx = "_A"
        g1_d = self.inp("g1" + sfx, [D])
        win_d = self.inp("win" + sfx, [D, INW])
        qg_d = self.inp("qg" + sfx, [64])
        kg_d = self.inp("kg" + sfx, [64])
        wg_d = self.inp("wg" + sfx, [2, 17, 256])
        rope_d = self.inp("rope", [2, TOK, 64])
        cm_d = self.inp("cm", [128, 4, 128])
        ci_d = self.inp("ci", [128, 2])
        gm_d = self.inp("gm", [128, 2, 128])
        bd_d = self.inp("bd", [128, 128])
        if not self.mod_done:
            self.A_mod()
        MODV = self.dout["MODV" + sfx]
        QT = self.outp("QT", [128, 4, NTOK], BF16)
        KT = self.outp("KT", [128, NTOK], BF16)
        V = self.outp("V", [NTOK, 128], BF16)
        FU = self.outp("FU", [NTOK, 256], BF16)
        OLOC = self.outp("OLOC", [NTOK, 256])
        QTIL = self.outp("QTIL", [2, 128, 2, NTOK], BF16)
        SR = self.outp("SR", [NTOK, 256], BF16)
        DS = self.outp("DS", [128, 2, 2])
        SLOC = self.outp("SLOC", [128, 2, 2, 128])
        SCTX = self.outp("SCTX", [128, 2, 2, 128])
        db = self.dbuf
        cb = self.cb

        cm = A("cm", [128, 4, 128], F32)
        ci = A("ci", [128, 2], F32)
        gm = A("gm", [128, 2, 128], BF16)
        bd = A("bd", [128, 128], F32)
        self.dma("sp", cm[:], cm_d, [], [cb])
        self.dma("sp", ci[:], ci_d, [], [cb])
        self.dma("pool", gm[:], gm_d, [], [cb])
        self.dma("sp", bd[:], bd_d, [], [cb])
        gqk = A("gqk", [128, 10, 64], F32)
        for hh in range(10):
            src = (qg_d if hh < 8 else kg_d).partition_broadcast(128)
            self.dma("sp", gqk[:, hh, :], src, [], [cb])
        wg = A("wg", [17, 2, 256], F32)
        for dr in range(2):
            self.dma("sp", wg[:, dr, :], wg_d[dr], [], [cb])
        zT = A("zT", [17, 2, 128], F32)
        zTb = Buf()
        S.op("dve", lambda e: e.memset(zT[:], 1.0), [], [zTb])

        W = A("w_in", [128, 8, INW], BF16)
        Wb = Buf()
        for k in range(8):
            for c0 in (0, 1040):
                self.dma("pool", W[:, k, c0:c0 + 1040], win_d[k * 128:(k + 1) * 128, c0:c0 + 1040], [], [Wb])

        MODB = A("modb", [128, 4, D], F32)
        MODBb = Buf()
        mk = self.sb.mark()
        rows = A("rowsA", [1, 3, D], F32)
        rowsb = Buf()
        for var in range(2):
            self.dma("sp", rows[0:1, 0, :], g1_d.rearrange("(o n) -> o n", o=1), [], [rowsb])
            self.dma("sp", rows[0:1, 1, :], MODV[var:var + 1, D:2 * D], [db["MODV" + sfx]], [rowsb])
            self.dma("sp", rows[0:1, 2, :], MODV[var:var + 1, 0:D], [db["MODV" + sfx]], [rowsb])
            self.stt(rows[0:1, 1, :], rows[0:1, 1, :], 1.0, rows[0:1, 0, :], ALU.add, ALU.mult, [rowsb], [rowsb])
            for (ri, slot) in ((1, 2 * var), (2, 2 * var + 1)):
                for (bk, bb, c0, c1) in self.bcast_row(rows[0:1, ri, :], rowsb, D):
                    self.cp("act", MODB[:, slot, c0:c1], bk[:, 0:c1 - c0], [bb], [MODBb])

        S.barrier()
        self.sb.release(mk)
        if DBG.get("stop") == "mod":
            return
        KOUT = A("kout", [128, NT, 2, 256], BF16)
        VG = A("vg", [128, NT, 256], BF16)
        QINT = A("qint", [128, NT, 4, 128], BF16)
        OL = A("ol", [128, NT, 256], F32)
        DEC = A("dec", [128, NT, 8], F32)
        tb = [dict(kout=Buf(), vg=Buf(), qint=Buf(), ol=Buf(), dec=Buf()) for _ in range(NT)]

        xt_r = Ring(A, "xt", [128, D], F32, 2)
        junk_r = Ring(A, "junk", [128, D], BF16, 1)
        st_r = Ring(A, "st", [128, 8], F32, 2)
        tmp_r = Ring(A, "tmpA", [128, D], F32, 1)
        h_r = Ring(A, "h", [128, D], BF16, 2)
        hT_r = Ring(A, "hT", [128, 8, 128], BF16, 2)
        qk_r = Ring(A, "qk", [128, 10, 64], F32, 1)
        sq_r = Ring(A, "sq", [128, 10, 64], F32, 1)
        sm_r = Ring(A, "sm", [128, 3, 16], F32, 2)
        qn_r = Ring(A, "qn", [128, 10, 64], F32, 1)
        t1_r = Ring(A, "t1", [128, 10, 64], F32, 1)
        t2_r = Ring(A, "t2", [128, 10, 64], F32, 1)
        qr_r = Ring(A, "qr", [128, 10, 64], BF16, 2)
        qkT_r = Ring(A, "qkT", [128, 5, 128], BF16, 2)
        rp_r = Ring(A, "rp", [128, 2, 64], F32, 2)
        vf_r = Ring(A, "vf", [128, 384], BF16, 2)
        z_r = Ring(A, "z", [128, 32], F32, 2)
        L_r = Ring(A, "L", [128, 2, 256], F32, 2)
        E_r = Ring(A, "E", [128, 3, 512], F32, 1)
        qi_r = Ring(A, "qi", [128, 2, 2, 256], BF16, 2)
        aT_r = Ring(A, "aT", [128, 8, 128], BF16, 2)
        sg_r = Ring(A, "sg", [128, 256], F32, 1)
        sr_r = Ring(A, "srt", [128, 256], BF16, 2)

        for t in range(DBG.get("ntiles", NT)):
            is_ctx = t >= NTL
            var = 1 if is_ctx else 0
            xt, xtb = xt_r.get()
            self.dma("sp", xt[:], xin[t * 128:(t + 1) * 128, :], [xin_b], [xtb])
            st, stb = st_r.get()
            junk, junkb = junk_r.get()
            self.act(junk[:], xt[:], AF.Square, [xtb], [junkb, stb], accum_out=st[:, 0:1])
            self.rstd(st[:, 2:3], st[:, 0:1], 1.0 / D, [stb, self.eps_b], [stb], st[:, 1:2], stb)
            tmp, tmpb = tmp_r.get()
            self.stt(tmp[:], xt[:], st[:, 2:3], MODB[:, 2 * var, :], ALU.mult, ALU.mult, [xtb, stb, MODBb], [tmpb])
            h, hb = h_r.get()
            self.tt("pool", h[:], tmp[:], MODB[:, 2 * var + 1, :], ALU.add, [tmpb, MODBb], [hb])
            if DBG.get('tstop') == 1:
                continue
            bk, bb = self.bank()
            bkb = bk[:].bitcast(BF16)
            for k in range(8):
                self.tr(bkb[:, k * 128:(k + 1) * 128], h[:, k * 128:(k + 1) * 128], self.ident_b[:], [hb, cb], [bb])
            hT, hTb = hT_r.get()
            self.cp("act", hT[:].rearrange("p k t -> p (k t)"), bkb[:, :], [bb], [hTb])
            if DBG.get('tstop') == 2:
                continue
            pbk = []
            for (c0, c1) in ((0, 512), (512, 1024), (1024, 1536), (1536, 2048), (2048, 2080)):
                bk, bb = self.bank(pin=True)
                for k in range(8):
                    self.mm(bk[:, 0:c1 - c0], hT[:, k, :], W[:, k, c0:c1], k == 0, k == 7, [hTb, Wb], [bb])
                pbk.append((bk, bb))
            (Pq, Pqb), (Pk, Pkb), (Pg1, Pg1b), (Pg2, Pg2b), (Pz, Pzb) = pbk
            if DBG.get('tstop') == 3:
                continue
            qk, qkb = qk_r.get()
            sq, sqb = sq_r.get()
            qkf = qk[:].rearrange("p h d -> p (h d)")
            sqf = sq[:].rearrange("p h d -> p (h d)")
            self.cp("act", qkf[:, 0:512], Pq[:, 0:512], [Pqb], [qkb])
            self.cp("act", qkf[:, 512:640], Pk[:, 0:128], [Pkb], [qkb])
            self.unpin(Pq)
            vf, vfb = vf_r.get()
            self.cp("dve", vf[:], Pk[:, 128:512], [Pkb], [vfb])
            self.unpin(Pk)
            self.dma("sp", V[t * 128:(t + 1) * 128, :], vf[:, 0:128], [vfb], [db["V"]])
            self.dma("sp", FU[t * 128:(t + 1) * 128, :], vf[:, 128:384], [vfb], [db["FU"]])
            z, zb_ = z_r.get()
            self.cp("act", z[:], Pz[:, 0:32], [Pzb], [zb_])
            self.unpin(Pz)
            self.act(sqf[:, :], qkf[:, :], AF.Square, [qkb], [sqb])
            sm, smb = sm_r.get()
            S.op("dve", lambda e: e.tensor_reduce(out=sm[:, 0, 0:10], in_=sq[:], axis=AX.X, op=ALU.add), [sqb], [smb])
            self.rstd(sm[:, 2, 0:10], sm[:, 0, 0:10], 1.0 / 64, [smb, self.eps_b], [smb], sm[:, 1, 0:10], smb)
            qn, qnb = qn_r.get()
            self.tt("dve", qn[:], qk[:], sm[:, 2, 0:10].unsqueeze(2).broadcast_to([128, 10, 64]), ALU.mult, [qkb, smb], [qnb])
            self.tt("pool", qn[:], qn[:], gqk[:], ALU.mult, [qnb, cb], [qnb])
            if DBG.get('tstop') == 4:
                continue
            qr, qrb = qr_r.get()
            if not is_ctx:
                rp, rpb = rp_r.get()
                self.dma("sp", rp[:], rope_d[:, t * 128:(t + 1) * 128, :].rearrange("c t d -> t c d"), [], [rpb])
                t1, t1b = t1_r.get()
                t2, t2b = t2_r.get()
                self.tt("dve", t1[:], qn[:], rp[:, 0, :].unsqueeze(1).broadcast_to([128, 10, 64]), ALU.mult, [qnb, rpb], [t1b])
                qn5 = qn[:].rearrange("p h (a s c) -> p (h a) s c", a=2, s=2)
                t25 = t2[:].rearrange("p h (a s c) -> p (h a) s c", a=2, s=2)
                self.cp("pool", t25[:, :, 0, :], qn5[:, :, 1, :], [qnb], [t2b])
                self.cp("pool", t25[:, :, 1, :], qn5[:, :, 0, :], [qnb], [t2b])
                self.tt("pool", t2[:], t2[:], rp[:, 1, :].unsqueeze(1).broadcast_to([128, 10, 64]), ALU.mult, [t2b, rpb], [t2b])
                self.tt("dve", qr[:], t1[:], t2[:], ALU.add, [t1b, t2b], [qrb])
            else:
                self.cp("dve", qr[:], qn[:], [qnb], [qrb])
            if DBG.get('tstop') == 5:
                continue
            bk, bb = self.bank()
            bkb = bk[:].bitcast(BF16)
            qrf = qr[:].rearrange("p h d -> p (h d)")
            for j in range(5):
                self.tr(bkb[:, j * 128:(j + 1) * 128], qrf[:, j * 128:(j + 1) * 128], self.ident_b[:], [qrb, cb], [bb])
            qkT, qkTb = qkT_r.get()
            self.cp("act", qkT[:].rearrange("p j t -> p (j t)"), bkb[:, 0:640], [bb], [qkTb])
            self.dma("sp", QT[:, :, t * 128:(t + 1) * 128], qkT[:, 0:4, :], [qkTb], [db["QT"]])
            self.dma("sp", KT[:, t * 128:(t + 1) * 128], qkT[:, 4, :], [qkTb], [db["KT"]])
            if DBG.get('tstop') == 6:
                continue
            if DBG.get('tstop') == 7:
                continue
            bk, bb = self.bank()
            for dr in range(2):
                self.tr(bk[0:16, dr * 128:(dr + 1) * 128], z[:, dr * 16:(dr + 1) * 16], self.ident_f[:], [zb_, cb], [bb])
            self.cp("dve", zT[0:16, :, :].rearrange("p a t -> p (a t)"), bk[0:16, 0:256], [bb], [zTb])
            bk, bb = self.bank()
            for dr in range(2):
                self.mm(bk[:, dr * 256:(dr + 1) * 256], zT[:, dr, :], wg[:, dr, :], True, True, [zTb, cb], [bb])
            Lt, Lb = L_r.get()
            Lf = Lt[:].rearrange("p a c -> p (a c)")
            self.act(Lf, bk[:, :], AF.Exp, [bb], [Lb], scale=-1.0)
            self.act(Lf, Lf, AF.Ln, [Lb, self.eps_b], [Lb], bias=self.eps_t[:, 1:2])
            if DBG.get('tstop') == 8:
                continue
            c1k, c1b = self.bank()
            c2k, c2b = self.bank()
            self.mm(c1k[:, 0:256], cm[:, 0, :], Lt[:, 0, :], True, True, [cb, Lb], [c1b])
            self.mm(c1k[:, 256:512], cm[:, 2, :], Lt[:, 1, :], True, True, [cb, Lb], [c1b])
            self.mm(c2k[:, 0:256], cm[:, 1, :], Lt[:, 0, :], True, True, [cb, Lb], [c2b])
            self.mm(c2k[:, 256:512], cm[:, 3, :], Lt[:, 1, :], True, True, [cb, Lb], [c2b])
            dk, dkb = self.bank()
            for dr in range(2):
                for pr in range(2):
                    i0 = (dr * 2 + pr) * 2
                    self.mm(dk[:, i0:i0 + 2], Lt[:, dr, pr * 128:(pr + 1) * 128], ci[:, :], True, True, [Lb, cb], [dkb])
            self.act(DEC[:, t, :], dk[:, 0:8], AF.Exp, [dkb], [tb[t]["dec"]])
            if DBG.get('tstop') == 9:
                continue
            E, Eb = E_r.get()
            self.act(E[:, 0, :], c1k[:, :], AF.Exp, [c1b, self.eps_b], [Eb], bias=self.eps_t[:, 2:3])
            self.act(E[:, 1, :], c1k[:, :], AF.Exp, [c1b], [Eb], scale=-1.0)
            self.act(E[:, 2, :], c2k[:, :], AF.Exp, [c2b], [Eb])
            qi, qib = qi_r.get()
            gq_b = Pg1[:, 0:256].unsqueeze(1).broadcast_to([128, 2, 256])
            gk_b = Pg1[:, 256:512].unsqueeze(1).broadcast_to([128, 2, 256])
            self.tt("dve", qi[:, 0, :, :], gq_b, E[:, 0, :].rearrange("p (a c) -> p a c", a=2), ALU.mult, [Pg1b, Eb], [qib])
            self.tt("dve", qi[:, 1, :, :], gk_b, E[:, 1, :].rearrange("p (a c) -> p a c", a=2), ALU.mult, [Pg1b, Eb], [qib])
            self.tt("dve", KOUT[:, t, :, :], gk_b, E[:, 2, :].rearrange("p (a c) -> p a c", a=2), ALU.mult, [Pg1b, Eb], [tb[t]["kout"]])
            self.cp("act", VG[:, t, :], Pg2[:, 0:256], [Pg2b], [tb[t]["vg"]])
            if DBG.get('tstop') == 10:
                continue
            sg, sgb = sg_r.get()
            self.act(sg[:], Pg2[:, 256:512], AF.Exp, [Pg2b], [sgb], scale=-1.0)
            self.ts("pool", sg[:], sg[:], 1.0, None, ALU.add, None, [sgb], [sgb])
            S.op("dve", lambda e: e.reciprocal(out=sg[:], in_=sg[:]), [sgb], [sgb])
            srt, srb = sr_r.get()
            self.tt("dve", srt[:], sg[:], Pg2[:, 256:512], ALU.mult, [sgb, Pg2b], [srb])
            self.dma("sp", SR[t * 128:(t + 1) * 128, :], srt[:], [srb], [db["SR"]])
            self.unpin(Pg1)
            self.unpin(Pg2)
            if DBG.get('tstop') == 11:
                continue
            bk, bb = self.bank()
            bkb = bk[:].bitcast(BF16)
            for wh in range(2):
                for dr in range(2):
                    for pr in range(2):
                        i0 = wh * 4 + dr * 2 + pr
                        self.tr(bkb[:, i0 * 128:(i0 + 1) * 128], qi[:, wh, dr, pr * 128:(pr + 1) * 128], self.ident_b[:], [qib, cb], [bb])
            kiT, kiTb = hT_r.get()
            if DBG.get("v") != "noact":
                self.cp("act", QINT[:, t, :, :].rearrange("p a t -> p (a t)"), bkb[:, 0:512], [bb], [tb[t]["qint"]])
            if DBG.get("v") != "nodve":
                self.cp(DBG.get("kieng", "dve"), kiT[:, 0:4, :].rearrange("p a t -> p (a t)"), bkb[:, 512:1024], [bb], [kiTb])
            if DBG.get('tstop') == 12:
                continue
            a1k, a1b = self.bank()
            a2k, a2b = self.bank()
            for half in range(2):
                ak, ab = (a1k, a1b) if half == 0 else (a2k, a2b)
                p0 = half * 64
                for dr in range(2):
                    for pr in range(2):
                        c0 = (dr * 2 + pr) * 128
                        self.mm(ak[:, c0:c0 + 128], kiT[p0:p0 + 64, dr * 2 + pr, :], QINT[p0:p0 + 64, t, dr * 2 + pr, :],
                                True, True, [kiTb, tb[t]["qint"]], [ab])
            aT, aTb = aT_r.get()
            aT5 = aT[:].rearrange("p (d r f) t -> p d r f t", d=2, r=2, f=2)
            for half in range(2):
                ak, ab = (a1k, a1b) if half == 0 else (a2k, a2b)
                for dr in range(2):
                    self.tt("dve", aT5[:, dr, :, half, :], ak[:, dr * 256:(dr + 1) * 256].rearrange("p (r t) -> p r t", r=2),
                            gm[:, dr, :].unsqueeze(1).broadcast_to([128, 2, 128]), ALU.mult, [ab, cb], [aTb])
            if DBG.get('tstop') == 13:
                continue
            ok, ob = self.bank()
            for hd in range(4):
                for dr in range(2):
                    self.mm(ok[:, hd * 64:(hd + 1) * 64], aT[:, dr * 4 + hd, :], VG[:, t, hd * 64:(hd + 1) * 64], dr == 0, dr == 1,
                            [aTb, tb[t]["vg"]], [ob])
            self.cp("act", OL[:, t, :], ok[:, 0:256], [ob], [tb[t]["ol"]])

        if DBG.get("dbgA"):
            OLI = self.outp("OLI", [NTOK, 256])
            QID = self.outp("QID", [NT, 128, 4, 128], BF16)
            for t in range(NT):
                self.dma("sp", OLI[t * 128:(t + 1) * 128, :], OL[:, t, :], [tb[t]["ol"]], [db["OLI"]])
                self.dma("sp", QID[t], QINT[:, t, :, :], [tb[t]["qint"]], [db["QID"]])
        if DBG.get("stop") == "tiles":
            return
        Sst = A("Sst", [128, 2, 2, 128], F32)
        Sbf = A("Sbf", [128, 2, 2, 128], BF16)
        G = A("G", [128, 4], F32)
        sb_ = [[Buf() for _ in range(2)] for _ in range(2)]
        sbb = [[Buf() for _ in range(2)] for _ in range(2)]
        gb = [[Buf() for _ in range(2)] for _ in range(2)]
        um_r = Ring(A, "um", [128, 128], F32, 3)
        qt_r = Ring(A, "qtl", [128, 64], BF16, 4)

        def scan(tiles, final_S_dram, final_D_dram):
            for dr in range(2):
                order = tiles if dr == 0 else tiles[::-1]
                for pr in range(2):
                    S.op("pool", lambda e: e.memset(Sst[:, dr, pr, :], 0.0), [], [sb_[dr][pr]])
                    S.op("pool", lambda e: e.memset(Sbf[:, dr, pr, :], 0.0), [], [sbb[dr][pr]])
                    S.op("pool", lambda e: e.memset(G[:, dr * 2 + pr:dr * 2 + pr + 1], 1.0), [], [gb[dr][pr]])
                for t in order:
                    for ch in ((0, 1) if dr == 0 else (1, 0)):
                        r0 = ch * 64
                        for pr in range(2):
                            ip = dr * 2 + pr
                            qt, qtb = qt_r.get()
                            self.ts("pool", qt[:], QINT[:, t, ip, r0:r0 + 64], G[:, ip:ip + 1], None, ALU.mult, None,
                                    [tb[t]["qint"], gb[dr][pr]], [qtb])
                            self.dma("sp", QTIL[dr, :, pr, t * 128 + r0:t * 128 + r0 + 64], qt[:], [qtb], [db["QTIL"]])
                            ik, ib = self.bank()
                            self.mm(ik[:, 0:128], QINT[:, t, ip, :], Sbf[:, dr, pr, :], True, True, [tb[t]["qint"], sbb[dr][pr]], [ib])
                            self.tt("dve", OL[r0:r0 + 64, t, pr * 128:(pr + 1) * 128], OL[r0:r0 + 64, t, pr * 128:(pr + 1) * 128],
                                    ik[r0:r0 + 64, 0:128], ALU.add, [tb[t]["ol"], ib], [tb[t]["ol"]])
                            uk, ub = self.bank()
                            self.mm(uk[:, 0:128], KOUT[r0:r0 + 64, t, dr, pr * 128:(pr + 1) * 128], VG[r0:r0 + 64, t, pr * 128:(pr + 1) * 128],
                                    True, True, [tb[t]["kout"], tb[t]["vg"]], [ub])
                            um, umb = um_r.get()
                            self.tt("dve", um[:], uk[:, 0:128], bd[:], ALU.mult, [ub, cb], [umb])
                            dcol = DEC[:, t, ip * 2 + ch:ip * 2 + ch + 1]
                            self.stt(Sst[:, dr, pr, :], Sst[:, dr, pr, :], dcol, um[:], ALU.mult, ALU.add,
                                     [sb_[dr][pr], tb[t]["dec"], umb], [sb_[dr][pr]])
                            self.cp("act", Sbf[:, dr, pr, :], Sst[:, dr, pr, :], [sb_[dr][pr]], [sbb[dr][pr]])
                            self.ts("pool", G[:, ip:ip + 1], G[:, ip:ip + 1], dcol, None, ALU.mult, None, [gb[dr][pr], tb[t]["dec"]], [gb[dr][pr]])
            allS = [sb_[a][b] for a in range(2) for b in range(2)]
            self.dma("sp", final_S_dram, Sst[:], allS, [db["SLOC"], db["SCTX"]])
            if final_D_dram is not None:
                allG = [gb[a][b] for a in range(2) for b in range(2)]
                self.dma("sp", final_D_dram.rearrange("p a b -> p (a b)"), G[:], allG, [db["DS"]])

        scan(list(range(NTL)), SLOC, DS)
        scan(list(range(NTL, NT)), SCTX, None)
        for t in range(NT):
            self.dma("sp", OLOC[t * 128:(t + 1) * 128, :], OL[:, t, :], [tb[t]["ol"]], [db["OLOC"]])

    def phase_B(self, l, xin, xin_b, with_ctx):
        nc, S, A = self.nc, self.S, self.alloc
        db, cb = self.dbuf, self.cb
        tiles = list(range(NT if with_ctx else NTL))
        nvar = 2 if with_ctx else 1
        I = lambda name, shape, dt=F32: self.inp("i_" + name, shape, dt)
        MODV = I("MODV", [2, 6 * D])
        QT = I("QT", [128, 4, NTOK], BF16)
        KTE = I("KTE", [128, 20 * 128], BF16)
        VE = I("VE", [20 * 128, 128], BF16)
        UALL = I("UALL", [SEQ, 256], BF16)
        FUC = I("FUC", [CTX, 256], BF16)
        OLOC = I("OLOC", [NTOK, 256])
        QTIL = I("QTIL", [2, 128, 2, NTOK], BF16)
        SR = I("SR", [NTOK, 256], BF16)
        DSg = I("DSg", [128, 4, 2, 2])
        SLOCg = I("SLOCg", [128, 4, 2, 2, 128])
        SCTX = I("SCTX", [128, 2, 2, 128])
        rmask_d = I("rmask", [128, 4, 2])
        amask_d = I("amask", [128, 4, 128])
        TAB = I("TAB", [2, 4, 64, 128, 512], BF16)
        TABC = I("TABC", [2, 1, 2, 128, 256], BF16)
        CB_d = I("CB", [128, 2, 128])
        wf_d = I("wf", [4, 64, 64])
        glag_d = I("glag", [64])
        sink_d = I("sink", [8])
        wout_d = I("wout", [D, D])
        g2_d = I("g2", [D])
        wffi_d = I("wffi", [D, 2 * HID])
        wffo_d = I("wffo", [HID, D])
        XO = self.outp("XO", [NTOK, D])
        FOURT = nc.dram_tensor("FOURT", [2, 128, NTOK], BF16, kind="Internal").ap()
        H2T = nc.dram_tensor("H2T", [NT, 128, 8, 128], BF16, kind="Internal").ap()
        fourb = Buf()
        h2tb = [Buf() for _ in range(NT)]
        xob = [Buf() for _ in range(NT)]
        pmark = self.sb.mark()

        Wi = A("w_ffi", [128, 8, 2 * HID], BF16, high=True)
        Wib = Buf()
        for k in range(8):
            for c0 in range(0, 2 * HID, 1408):
                self.dma("pool", Wi[:, k, c0:c0 + 1408], wffi_d[k * 128:(k + 1) * 128, c0:c0 + 1408], [], [Wib])

        amask = A("amask", [128, 4, 128], BF16)
        self.dma("pool", amask[:], amask_d, [], [cb])
        exs = A("exs", [128, 8], F32)
        self.dma("sp", exs[:], sink_d.partition_broadcast(128), [], [cb])
        self.act(exs[:], exs[:], AF.Exp, [cb], [cb])
        glag = A("glag", [128, 64], F32)
        self.dma("sp", glag[:], glag_d.partition_broadcast(128), [], [cb])
        MODB = A("modbB", [128, nvar, 4, D], F32)
        MODBb = Buf()
        mk = self.sb.mark()
        rows = A("rowsB", [1, 5, D], F32)
        rowsb = Buf()
        for var in range(nvar):
            self.dma("sp", rows[0:1, 0, :], g2_d.rearrange("(o n) -> o n", o=1), [], [rowsb])
            for (ri, c0) in ((1, 2 * D), (2, 4 * D), (3, 3 * D), (4, 5 * D)):
                self.dma("sp", rows[0:1, ri, :], MODV[var:var + 1, c0:c0 + D], [], [rowsb])
            self.stt(rows[0:1, 2, :], rows[0:1, 2, :], 1.0, rows[0:1, 0, :], ALU.add, ALU.mult, [rowsb], [rowsb])
            for (ri, slot) in ((1, 0), (2, 1), (3, 2), (4, 3)):
                for (bk, bb, c0, c1) in self.bcast_row(rows[0:1, ri, :], rowsb, D):
                    self.cp("act", MODB[:, var, slot, c0:c1], bk[:, 0:c1 - c0], [bb], [MODBb])
        Sin = A("Sin", [128, 2, 2, 128], BF16, high=True)
        Sinb = Buf()
        dsg = A("dsg", [128, 4, 2, 2], F32)
        slg = A("slg", [128, 4, 2, 2, 128], F32)
        sct = A("sct", [128, 2, 2, 128], F32)
        rmk = A("rmk", [128, 4, 2], F32)
        sttmp = A("sttmp", [128, 2, 128], F32)
        gb_ = Buf()
        self.dma("sp", dsg[:], DSg, [], [gb_])
        self.dma("sp", slg[:], SLOCg, [], [gb_])
        self.dma("sp", sct[:], SCTX, [], [gb_])
        self.dma("sp", rmk[:], rmask_d, [], [gb_])
        for dr in range(2):
            for r in (range(4) if dr == 0 else range(3, -1, -1)):
                self.tt("dve", sttmp[:], sct[:, dr, :, :], dsg[:, r, dr, :].unsqueeze(2).broadcast_to([128, 2, 128]), ALU.mult, [gb_], [gb_])
                self.tt("dve", sttmp[:], sttmp[:], slg[:, r, dr, :, :], ALU.add, [gb_], [gb_])
                self.tt("dve", sttmp[:], sttmp[:], sct[:, dr, :, :], ALU.subtract, [gb_], [gb_])
                self.stt(sct[:, dr, :, :], sttmp[:], rmk[:, r, dr:dr + 1], sct[:, dr, :, :], ALU.mult, ALU.add, [gb_], [gb_])
        self.cp("dve", Sin[:], sct[:], [gb_], [Sinb])
        S.barrier()
        self.sb.release(mk)

        fmark = self.sb.mark()
        U = A("uall", [128, 64, 256], BF16)
        Ub = Buf()
        for q4 in range(4):
            self.dma("sp", U[:, q4 * 16:(q4 + 1) * 16, :], UALL[q4 * 2048:(q4 + 1) * 2048, :].rearrange("(k p) c -> p k c", p=128), [], [Ub])
        CBs = A("CBs", [128, 2, 128], F32)
        self.dma("sp", CBs[:], CB_d, [], [cb])
        wblk = A("wblk", [128, 2, 128], F32)
        wblkb = Buf()
        S.op("dve", lambda e: e.memset(wblk[:], 0.0), [], [wblkb])
        for g in range(4):
            hf, gi = g // 2, g % 2
            self.dma("sp", wblk[gi * 64:(gi + 1) * 64, hf, gi * 64:(gi + 1) * 64], wf_d[g], [wblkb], [wblkb])
        Mblk = A("Mblk", [128, 2, 2, 128], BF16)
        Mb = Buf()
        for hf in range(2):
            for cs in range(2):
                bk, bb = self.bank()
                self.mm(bk[:, 0:128], CBs[:, cs, :], wblk[:, hf, :], True, True, [cb, wblkb], [bb])
                self.cp("act", Mblk[:, hf, cs, :], bk[:, 0:128], [bb], [Mb])
        tab_r = Ring(A, "tab", [128, 8, 512], BF16, 2)
        abt_r = Ring(A, "abt", [128, 2, 2, 512], BF16, 1)
        ft_r = Ring(A, "ft", [128, 512], BF16, 2)

        def fnet(Usb, Usbb, ntc, tabd, nblk, blk, col0):
            tg = min(8, ntc)
            for b in range(nblk):
                abt, abtb = abt_r.get()
                for cs in range(2):
                    acc = [self.bank(), self.bank()]
                    for g0 in range(0, ntc, tg):
                        tab, tabb = tab_r.get()
                        self.dma("sp", tab[:, 0:tg, 0:blk], tabd[cs, b, g0:g0 + tg].rearrange("k p c -> p k c"), [], [tabb])
                        for tc in range(tg):
                            for hf in range(2):
                                self.mm(acc[hf][0][:, 0:blk], Usb[:, g0 + tc, hf * 128:(hf + 1) * 128], tab[:, tc, 0:blk],
                                        g0 + tc == 0, g0 + tc == ntc - 1, [Usbb, tabb], [acc[hf][1]])
                    for hf in range(2):
                        self.cp("act" if hf == 0 else "dve", abt[:, hf, cs, 0:blk], acc[hf][0][:, 0:blk], [acc[hf][1]], [abtb])
                for hf in range(2):
                    bk, bb = self.bank()
                    for cs in range(2):
                        self.mm(bk[:, 0:blk], Mblk[:, hf, cs, :], abt[:, hf, cs, 0:blk], cs == 0, cs == 1, [Mb, abtb], [bb])
                    ft, ftb = ft_r.get()
                    self.cp("act", ft[:, 0:blk], bk[:, 0:blk], [bb], [ftb])
                    self.dma("sp", FOURT[hf, :, col0 + b * blk:col0 + (b + 1) * blk], ft[:, 0:blk], [ftb], [fourb])

        fnet(U, Ub, 64, TAB, 4, 512, 0)
        if with_ctx:
            Uc = A("uc", [128, 2, 256], BF16)
            Ucb = Buf()
            self.dma("sp", Uc[:], FUC.rearrange("(k p) c -> p k c", p=128), [], [Ucb])
            fnet(Uc, Ucb, 2, TABC, 1, 256, TOK)
        S.barrier()
        self.sb.release(fmark)

        b1mark = self.sb.mark()
        Wo = A("w_out", [128, 8, D], BF16)
        Wob = Buf()
        for k in range(8):
            self.dma("pool", Wo[:, k, :], wout_d[k * 128:(k + 1) * 128, :], [], [Wob])
        KTs = A("kte", [128, 20, 128], BF16)
        self.dma("sp", KTs[:].rearrange("p k t -> p (k t)"), KTE, [], [cb])
        VEa = A("vea", [128, 20, 2, 65], BF16)
        S.op("pool", lambda e: e.memset(VEa[:, :, :, 64:65], 1.0), [], [cb])
        for g in range(2):
            self.dma("sp", VEa[:, :, g, 0:64], VE[:, g * 64:(g + 1) * 64].rearrange("(k p) d -> p k d", p=128), [], [cb])
        xt_r = Ring(A, "xtB", [128, D], F32, 2)
        qt_r = Ring(A, "qtB", [128, 4, 128], BF16, 2)
        pT_r = Ring(A, "pT", [128, 5, 512], BF16, 2)
        den_r = Ring(A, "den", [128, 8], F32, 2)
        att_r = Ring(A, "att", [128, 8, 64], BF16, 2)
        mix_r = Ring(A, "mixT", [128, 8, 128], BF16, 2)
        ol_r = Ring(A, "olB", [128, 256], F32, 2)
        srr = Ring(A, "srB", [128, 256], BF16, 2)
        qtl_r = Ring(A, "qtlB", [128, 2, 2, 128], BF16, 2)
        o_r = Ring(A, "oB", [128, 4, 64], F32, 1)
        sq_r = Ring(A, "sqB", [128, 4, 64], F32, 1)
        sm_r = Ring(A, "smB", [128, 3, 4], F32, 2)
        y_r = Ring(A, "yB", [128, 4, 64], BF16, 2)
        tmp_r = Ring(A, "tmpB", [128, D], F32, 1)
        xn_r = Ring(A, "xnB", [128, D], F32, 2)
        junk_r = Ring(A, "junkB", [128, D], BF16, 1)
        st_r = Ring(A, "stB", [128, 4], F32, 2)
        h2_r = Ring(A, "h2B", [128, D], BF16, 2)
        h2T_r = Ring(A, "h2TB", [128, 8, 128], BF16, 2)
        for t in tiles:
            is_ctx = t >= NTL
            var = 1 if is_ctx else 0
            if is_ctx:
                kts = [18, 19]
                msk = [None, None]
            else:
                kts = [t, t + 1, t + 2, 18, 19]
                msk = [0 if t == 0 else 1, None, 3 if t == NTL - 1 else 2, None, None]
            nk = len(kts)
            qt, qtb = qt_r.get()
            self.dma("sp", qt[:], QT[:, :, t * 128:(t + 1) * 128], [], [qtb])
            xt, xtb = xt_r.get()
            self.dma("sp", xt[:], xin[t * 128:(t + 1) * 128, :], [xin_b], [xtb])
            ol, olb = ol_r.get()
            self.dma("sp", ol[:], OLOC[t * 128:(t + 1) * 128, :], [], [olb])
            sr, srb = srr.get()
            self.dma("sp", sr[:], SR[t * 128:(t + 1) * 128, :], [], [srb])
            mix, mixb = mix_r.get()
            self.dma("sp", mix[:, 4:6, :], FOURT[:, :, t * 128:(t + 1) * 128].rearrange("h p t -> p h t"), [fourb], [mixb])
            pTs = []
            for g in range(2):
                pT, pTb = pT_r.get()
                for i, kt in enumerate(kts):
                    bk, bb = self.bank()
                    self.mm(bk[:, :], KTs[g * 64:(g + 1) * 64, kt, :], qt[g * 64:(g + 1) * 64, :, :].rearrange("p j t -> p (j t)"),
                            True, True, [cb, qtb], [bb])
                    self.act(pT[:, i, :], bk[:, :], AF.Exp, [bb], [pTb], scale=0.125)
                    if msk[i] is not None:
                        self.tt("pool" if g == 0 else "dve", pT[:, i, :].rearrange("p (j t) -> p j t", j=4), pT[:, i, :].rearrange("p (j t) -> p j t", j=4),
                                amask[:, msk[i], :].unsqueeze(1).broadcast_to([128, 4, 128]), ALU.mult, [pTb, cb], [pTb])
                pTs.append((pT, pTb))
            pv = [self.bank(), self.bank()]
            for g in range(2):
                pT, pTb = pTs[g]
                for j in range(4):
                    for i, kt in enumerate(kts):
                        self.mm(pv[g][0][:, j * 65:(j + 1) * 65], pT[:, i, j * 128:(j + 1) * 128], VEa[:, kt, g, :], i == 0, i == nk - 1,
                                [pTb, cb], [pv[g][1]])
            den, denb = den_r.get()
            att, attb = att_r.get()
            for g in range(2):
                pvv = pv[g][0][:, 0:260].rearrange("p (j c) -> p j c", c=65)
                self.tt("dve", den[:, g * 4:(g + 1) * 4].unsqueeze(2), pvv[:, :, 64:65], exs[:, g * 4:(g + 1) * 4].unsqueeze(2), ALU.add,
                        [pv[g][1], cb], [denb])
            S.op("dve", lambda e: e.reciprocal(out=den[:], in_=den[:]), [denb], [denb])
            for g in range(2):
                pvv = pv[g][0][:, 0:260].rearrange("p (j c) -> p j c", c=65)
                self.tt("dve", att[:, g * 4:(g + 1) * 4, :], pvv[:, :, 0:64], den[:, g * 4:(g + 1) * 4].unsqueeze(2).broadcast_to([128, 4, 64]),
                        ALU.mult, [pv[g][1], denb], [attb])
            o, ob = o_r.get()
            of = o[:].rearrange("p h e -> p (h e)")
            if not is_ctx:
                qtl, qtlb = qtl_r.get()
                for dr in range(2):
                    self.dma("sp", qtl[:, dr, :, :], QTIL[dr, :, :, t * 128:(t + 1) * 128], [], [qtlb])
                bk, bb = self.bank()
                for pr in range(2):
                    for dr in range(2):
                        self.mm(bk[:, pr * 128:(pr + 1) * 128], qtl[:, dr, pr, :], Sin[:, dr, pr, :], dr == 0, dr == 1, [qtlb, Sinb], [bb])
                self.tt("dve", of, bk[:, 0:256], ol[:], ALU.add, [bb, olb], [ob])
            else:
                self.cp("dve", of, ol[:], [olb], [ob])
            sq, sqb = sq_r.get()
            self.act(sq[:].rearrange("p h e -> p (h e)"), of, AF.Square, [ob], [sqb])
            sm, smb = sm_r.get()
            S.op("dve", lambda e: e.tensor_reduce(out=sm[:, 0, :], in_=sq[:], axis=AX.X, op=ALU.add), [sqb], [smb])
            self.rstd(sm[:, 2, :], sm[:, 0, :], 1.0 / 64, [smb, self.eps_b], [smb], sm[:, 1, :], smb)
            self.tt("dve", o[:], o[:], sm[:, 2, :].unsqueeze(2).broadcast_to([128, 4, 64]), ALU.mult, [ob, smb], [ob])
            self.tt("pool", o[:], o[:], glag[:].unsqueeze(1).broadcast_to([128, 4, 64]), ALU.mult, [ob, cb], [ob])
            y, yb = y_r.get()
            self.tt("dve", y[:].rearrange("p h e -> p (h e)"), of, sr[:], ALU.mult, [ob, srb], [yb])
            bk, bb = self.bank()
            bkb = bk[:].bitcast(BF16)
            attf = att[:].rearrange("p h e -> p (h e)")
            yf = y[:].rearrange("p h e -> p (h e)")
            for k in range(4):
                self.tr(bkb[:, k * 128:(k + 1) * 128], attf[:, k * 128:(k + 1) * 128], self.ident_b[:], [attb, cb], [bb])
            for k in range(2):
                self.tr(bkb[:, (4 + k) * 128:(5 + k) * 128], yf[:, k * 128:(k + 1) * 128], self.ident_b[:], [yb, cb], [bb])
            self.cp("act", mix[:, 0:4, :].rearrange("p k t -> p (k t)"), bkb[:, 0:512], [bb], [mixb])
            self.cp("act", mix[:, 6:8, :].rearrange("p k t -> p (k t)"), bkb[:, 512:768], [bb], [mixb])
            if DBG.get("mixout"):
                if t == tiles[0]:
                    self.MIXD = self.outp("MIXD", [NT, 128, 8, 128], BF16)
                self.dma("sp", self.MIXD[t], mix[:], [mixb], [db["MIXD"]])
            xn, xnb = xn_r.get()
            tmp, tmpb = tmp_r.get()
            for hfc in range(2):
                bk, bb = self.bank()
                for k in range(8):
                    self.mm(bk[:, :], mix[:, k, :], Wo[:, k, hfc * 512:(hfc + 1) * 512], k == 0, k == 7, [mixb, Wob], [bb])
                self.tt("dve", tmp[:, hfc * 512:(hfc + 1) * 512], bk[:, :], MODB[:, var, 0, hfc * 512:(hfc + 1) * 512], ALU.mult, [bb, MODBb], [tmpb])
            self.tt("pool", xn[:], tmp[:], xt[:], ALU.add, [tmpb, xtb], [xnb])
            self.dma("sp", XO[t * 128:(t + 1) * 128, :], xn[:], [xnb], [xob[t]])
            st, stb = st_r.get()
            junk, junkb = junk_r.get()
            self.act(junk[:], xn[:], AF.Square, [xnb], [junkb, stb], accum_out=st[:, 0:1])
            self.rstd(st[:, 2:3], st[:, 0:1], 1.0 / D, [stb, self.eps_b], [stb], st[:, 1:2], stb)
            tmp, tmpb = tmp_r.get()
            self.stt(tmp[:], xn[:], st[:, 2:3], MODB[:, var, 1, :], ALU.mult, ALU.mult, [xnb, stb, MODBb], [tmpb])
            h2, h2b = h2_r.get()
            self.tt("pool", h2[:], tmp[:], MODB[:, var, 2, :], ALU.add, [tmpb, MODBb], [h2b])
            bk, bb = self.bank()
            bkb = bk[:].bitcast(BF16)
            for k in range(8):
                self.tr(bkb[:, k * 128:(k + 1) * 128], h2[:, k * 128:(k + 1) * 128], self.ident_b[:], [h2b, cb], [bb])
            h2T, h2Tb = h2T_r.get()
            self.cp("act", h2T[:].rearrange("p k t -> p (k t)"), bkb[:, :], [bb], [h2Tb])
            self.dma("sp", H2T[t], h2T[:], [h2Tb], [h2tb[t]])
        S.barrier()
        self.sb.release(b1mark)

        Wf = A("w_ffo", [128, NCH, D], BF16, high=True)
        Wfb = Buf()
        for j in range(NCH):
            self.dma("pool", Wf[:, j, :], wffo_d[j * 128:(j + 1) * 128, :], [], [Wfb])
        hp_r = Ring(A, "hp", [128, 8, 256], BF16, 2)
        xp_r = Ring(A, "xp", [128, 2, D], F32, 2)
        sg_r = Ring(A, "sgF", [128, 256], BF16, 2)
        ac_r = Ring(A, "acF", [128, 256], BF16, 3)
        tm_r = Ring(A, "tmF", [128, D], F32, 1)
        xo_r = Ring(A, "xoF", [128, D], F32, 2)
        accs = self.banks[0:4]
        self.brange = (4, 8)
        for p in range(len(tiles) // 2):
            t0 = 2 * p
            var = 1 if t0 >= NTL else 0
            hp, hpb = hp_r.get()
            xp, xpb = xp_r.get()
            for i in range(2):
                self.dma("sp", hp[:, :, i * 128:(i + 1) * 128], H2T[t0 + i], [h2tb[t0 + i]], [hpb])
                self.dma("sp", xp[:, i, :], XO[(t0 + i) * 128:(t0 + i + 1) * 128, :], [xob[t0 + i]], [xpb])
            for j in range(NCH):
                gk, gbk = self.bank()
                uk, ubk = self.bank()
                for k in range(8):
                    self.mm(gk[:, 0:256], Wi[:, k, j * 128:(j + 1) * 128], hp[:, k, :], k == 0, k == 7, [Wib, hpb], [gbk])
                for k in range(8):
                    self.mm(uk[:, 0:256], Wi[:, k, HID + j * 128:HID + (j + 1) * 128], hp[:, k, :], k == 0, k == 7, [Wib, hpb], [ubk])
                sg, sgb = sg_r.get()
                self.act(sg[:], gk[:, 0:256], AF.Silu, [gbk], [sgb])
                ac, acb = ac_r.get()
                self.tt("dve", ac[:], sg[:], uk[:, 0:256], ALU.mult, [sgb, ubk], [acb])
                for i in range(2):
                    for hfc in range(2):
                        a_, ab_ = accs[i * 2 + hfc]
                        self.mm(a_[:, :], ac[:, i * 128:(i + 1) * 128], Wf[:, j, hfc * 512:(hfc + 1) * 512], j == 0, j == NCH - 1, [acb, Wfb], [ab_])
            for i in range(2):
                tm, tmb = tm_r.get()
                for hfc in range(2):
                    a_, ab_ = accs[i * 2 + hfc]
                    self.tt("dve", tm[:, hfc * 512:(hfc + 1) * 512], a_[:, :], MODB[:, var, 3, hfc * 512:(hfc + 1) * 512], ALU.mult, [ab_, MODBb], [tmb])
                xo, xo_b = xo_r.get()
                self.tt("pool", xo[:], tm[:], xp[:, i, :], ALU.add, [tmb, xpb], [xo_b])
                self.dma("sp", XO[(t0 + i) * 128:(t0 + i + 1) * 128, :], xo[:], [xo_b], [xob[t0 + i], db["XO"]])
        self.brange = (0, 8)
        S.barrier()
        self.sb.release(pmark)

    def A_mod_f(self, l):
        nc, S, A = self.nc, self.S, self.alloc
        db = self.dbuf
        cT_d = self.inp("cT", [128, 8, 2])
        wmod_d = self.inp("wmod%d" % l, [D, 6 * D])
        bmod_d = self.inp("bmod%d" % l, [6 * D])
        MODV = self.dram("MODV%d" % l, [2, 6 * D])
        self.modv[l] = MODV
        mk = self.sb.mark()
        cTs = A("cTs", [128, 8, 2], F32)
        cTb = Buf()
        self.dma("sp", cTs[:], cT_d, [], [cTb])
        scT = A("scT", [128, 8, 64], BF16)
        scb = Buf()
        S.op("dve", lambda e: e.memset(scT[:], 0.0), [], [scb])
        sil = A("sil", [128, 8, 2], F32)
        silb = Buf()
        self.act(sil[:], cTs[:], AF.Exp, [cTb], [silb], scale=-1.0)
        self.ts("dve", sil[:], sil[:], 1.0, None, ALU.add, None, [silb], [silb])
        S.op("dve", lambda e: e.reciprocal(out=sil[:], in_=sil[:]), [silb], [silb])
        self.tt("dve", sil[:], sil[:], cTs[:], ALU.mult, [silb, cTb], [silb])
        self.cp("dve", scT[:, :, 0:1], sil[:, :, 0:1], [silb], [scb])
        self.cp("dve", scT[:, :, 32:33], sil[:, :, 1:2], [silb], [scb])
        wm_ring = Ring(A, "wm", [128, 8, 512], BF16, 2)
        bm_ring = Ring(A, "bm", [64, 512], F32, 2)
        mr_ring = Ring(A, "mr", [64, 512], F32, 2)
        mvb = db["MODV%d" % l]
        for cbk in range(12):
            wm, wmb = wm_ring.get()
            self.dma("pool", wm[:], wmod_d[:, cbk * 512:(cbk + 1) * 512].rearrange("(k p) c -> p k c", p=128), [], [wmb])
            bm, bmb = bm_ring.get()
            self.dma("sp", bm[:], bmod_d[cbk * 512:(cbk + 1) * 512].partition_broadcast(64), [], [bmb])
            bk, bb = self.bank()
            for k in range(8):
                self.mm(bk[0:64, :], scT[:, k, :], wm[:, k, :], k == 0, k == 7, [scb, wmb], [bb])
            mr, mrb = mr_ring.get()
            self.tt("dve", mr[:], bk[0:64, :], bm[:], ALU.add, [bb, bmb], [mrb])
            self.dma("sp", MODV[0:1, cbk * 512:(cbk + 1) * 512], mr[0:1, :], [mrb], [mvb])
            self.dma("sp", MODV[1:2, cbk * 512:(cbk + 1) * 512], mr[32:33, :], [mrb], [mvb])
        S.barrier()
        self.sb.release(mk)

    def phase_A_f(self, l, xin, xin_b):
        nc, S, A = self.nc, self.S, self.alloc
        ntl, nt, ntok = NTLF, NTF, NTOKF
        pmark = self.sb.mark()
        g1_d = self.inp("g1%d" % l, [D])
        win_d = self.inp("win%d" % l, [D, INW])
        qg_d = self.inp("qg%d" % l, [64])
        kg_d = self.inp("kg%d" % l, [64])
        wg_d = self.inp("wg%d" % l, [2, 17, 256])
        rope_d = self.inp("rope", [2, SEQ, 64])
        cm_d = self.inp("cm", [128, 4, 128])
        ci_d = self.inp("ci", [128, 2])
        gm_d = self.inp("gm", [128, 2, 128])
        bd_d = self.inp("bd", [128, 128])
        MODV = self.modv[l]
        L = "%d" % l
        QT = self.dram("QT" + L, [128, 4, ntok], BF16)
        KTP = self.dram("KTP" + L, [128, (ntl + 2) * 128], BF16)
        KTC = self.dram("KTC" + L, [128, CTX], BF16)
        VP = self.dram("VP" + L, [(ntl + 2) * 128, 128], BF16)
        VC = self.dram("VC" + L, [CTX, 128], BF16)
        FUL = self.dram("FUL" + L, [SEQ, 256], BF16)
        FUC = self.dram("FUC" + L, [CTX, 256], BF16)
        OLOC = self.dram("OLOC" + L, [ntok, 256])
        SR = self.dram("SR" + L, [ntok, 256], BF16)
        KOUTd = self.dram("KOUTd" + L, [nt, 128, 2, 256], BF16)
        VGd = self.dram("VGd" + L, [nt, 128, 256], BF16)
        QINTd = self.dram("QINTd" + L, [nt, 128, 4, 128], BF16)
        DECd = self.dram("DECd" + L, [nt, 128, 8])
        self.aout[l] = dict(QT=QT, KTP=KTP, KTC=KTC, VP=VP, VC=VC, FUL=FUL, FUC=FUC, OLOC=OLOC, SR=SR)
        db = self.dbuf
        cb = self.cb
        olb = [Buf() for _ in range(nt)]
        self.aout[l]["olb"] = olb
        scb_ = [Buf() for _ in range(nt)]

        cm = A("cm", [128, 4, 128], F32)
        ci = A("ci", [128, 2], F32)
        gm = A("gm", [128, 2, 128], BF16)
        bd = A("bd", [128, 128], F32)
        self.dma("sp", cm[:], cm_d, [], [cb])
        self.dma("sp", ci[:], ci_d, [], [cb])
        self.dma("pool", gm[:], gm_d, [], [cb])
        self.dma("sp", bd[:], bd_d, [], [cb])
        gqk = A("gqk", [128, 10, 64], F32)
        for hh in range(10):
            src = (qg_d if hh < 8 else kg_d).partition_broadcast(128)
            self.dma("sp", gqk[:, hh, :], src, [], [cb])
        wg = A("wg", [17, 2, 256], F32)
        for dr in range(2):
            self.dma("sp", wg[:, dr, :], wg_d[dr], [], [cb])
        zT = A("zT", [17, 2, 128], F32)
        zTb = Buf()
        S.op("dve", lambda e: e.memset(zT[:], 1.0), [], [zTb])
        zt = A("zeroT", [128, 128], BF16)
        ztb = Buf()
        S.op("dve", lambda e: e.memset(zt[:], 0.0), [], [ztb])
        for pos in (0, ntl + 1):
            self.dma("sp", KTP[:, pos * 128:(pos + 1) * 128], zt[:], [ztb], [db["KTP" + L]])
            self.dma("sp", VP[pos * 128:(pos + 1) * 128, :], zt[:], [ztb], [db["VP" + L]])

        W = A("w_in", [128, 8, INW], BF16)
        Wb = Buf()
        for k in range(8):
            for c0 in (0, 1040):
                self.dma("pool", W[:, k, c0:c0 + 1040], win_d[k * 128:(k + 1) * 128, c0:c0 + 1040], [], [Wb])

        MODB = A("modb", [128, 4, D], F32)
        MODBb = Buf()
        mk = self.sb.mark()
        rows = A("rowsA", [1, 3, D], F32)
        rowsb = Buf()
        for var in range(2):
            self.dma("sp", rows[0:1, 0, :], g1_d.rearrange("(o n) -> o n", o=1), [], [rowsb])
            self.dma("sp", rows[0:1, 1, :], MODV[var:var + 1, D:2 * D], [db["MODV" + L]], [rowsb])
            self.dma("sp", rows[0:1, 2, :], MODV[var:var + 1, 0:D], [db["MODV" + L]], [rowsb])
            self.stt(rows[0:1, 1, :], rows[0:1, 1, :], 1.0, rows[0:1, 0, :], ALU.add, ALU.mult, [rowsb], [rowsb])
            for (ri, slot) in ((1, 2 * var), (2, 2 * var + 1)):
                for (bk, bb, c0, c1) in self.bcast_row(rows[0:1, ri, :], rowsb, D):
                    self.cp("act", MODB[:, slot, c0:c1], bk[:, 0:c1 - c0], [bb], [MODBb])
        S.barrier()
        self.sb.release(mk)

        def mkrings(tag):
            xt_r = Ring(A, "xt" + tag, [128, D], F32, 1)
            junk_r = Ring(A, "junk" + tag, [128, D], BF16, 1)
            st_r = Ring(A, "st" + tag, [128, 8], F32, 1)
            tmp_r = Ring(A, "tmpA" + tag, [128, D], F32, 1)
            h_r = Ring(A, "h" + tag, [128, D], BF16, 1)
            hT_r = Ring(A, "hT" + tag, [128, 8, 128], BF16, 2)
            qk_r = Ring(A, "qk" + tag, [128, 10, 64], F32, 1)
            sq_r = Ring(A, "sq" + tag, [128, 10, 64], F32, 1)
            sm_r = Ring(A, "sm" + tag, [128, 3, 16], F32, 1)
            qn_r = Ring(A, "qn" + tag, [128, 10, 64], F32, 1)
            t1_r = Ring(A, "t1" + tag, [128, 10, 64], F32, 1)
            t2_r = Ring(A, "t2" + tag, [128, 10, 64], F32, 1)
            qr_r = Ring(A, "qr" + tag, [128, 10, 64], BF16, 1)
            qkT_r = Ring(A, "qkT" + tag, [128, 5, 128], BF16, 1)
            rp_r = Ring(A, "rp" + tag, [128, 2, 64], F32, 1)
            vf_r = Ring(A, "vf" + tag, [128, 384], BF16, 1)
            z_r = Ring(A, "z" + tag, [128, 32], F32, 1)
            L_r = Ring(A, "L" + tag, [128, 2, 256], F32, 1)
            E_r = Ring(A, "E" + tag, [128, 3, 512], F32, 1)
            qi_r = Ring(A, "qi" + tag, [128, 2, 2, 256], BF16, 1)
            aT_r = Ring(A, "aT" + tag, [128, 8, 128], BF16, 1)
            sg_r = Ring(A, "sg" + tag, [128, 256], F32, 1)
            sr_r = Ring(A, "srt" + tag, [128, 256], BF16, 1)
            ko_r = Ring(A, "koR" + tag, [128, 2, 256], BF16, 1)
            vg_r = Ring(A, "vgR" + tag, [128, 256], BF16, 1)
            qint_r = Ring(A, "qintR" + tag, [128, 4, 128], BF16, 1)
            ol_r = Ring(A, "olR" + tag, [128, 256], F32, 1)
            dec_r = Ring(A, "decR" + tag, [128, 8], F32, 1)
            g1s_r = Ring(A, "g1s" + tag, [128, 512], F32, 1)
            grs_r = Ring(A, "grs" + tag, [128, 256], F32, 1)
            zT_r = Ring(A, "zTp" + tag, [17, 2, 128], F32, 1)
            for (zz, zzb) in zT_r.items:
                S.op("dve", lambda e: e.memset(zz[:], 1.0), [], [zzb])
            return (xt_r, junk_r, st_r, tmp_r, h_r, hT_r, qk_r, sq_r, sm_r, qn_r, t1_r, t2_r, qr_r, qkT_r, rp_r, vf_r, z_r, L_r, E_r, qi_r, aT_r, sg_r, sr_r, ko_r, vg_r, qint_r, ol_r, dec_r, g1s_r, grs_r, zT_r)

        RR = [mkrings("a"), mkrings("b")]

        def tileA(t):
            par = t % 2
            self.brange = (0, 4) if par == 0 else (4, 8)
            (xt_r, junk_r, st_r, tmp_r, h_r, hT_r, qk_r, sq_r, sm_r, qn_r, t1_r, t2_r, qr_r, qkT_r, rp_r, vf_r, z_r, L_r, E_r, qi_r, aT_r, sg_r, sr_r, ko_r, vg_r, qint_r, ol_r, dec_r, g1s_r, grs_r, zT_r) = RR[par]
            is_ctx = t >= ntl
            var = 1 if is_ctx else 0
            tc = t - ntl
            zT, zTb = zT_r.items[0]
            xt, xtb = xt_r.get()
            self.dma("act", xt[:], xin[t * 128:(t + 1) * 128, :], xin_b(t), [xtb])
            st, stb = st_r.get()
            junk, junkb = junk_r.get()
            self.act(junk[:], xt[:], AF.Square, [xtb], [junkb, stb], accum_out=st[:, 0:1])
            self.rstd(st[:, 2:3], st[:, 0:1], 1.0 / D, [stb, self.eps_b], [stb], st[:, 1:2], stb)
            tmp, tmpb = tmp_r.get()
            self.stt(tmp[:], xt[:], st[:, 2:3], MODB[:, 2 * var, :], ALU.mult, ALU.mult, [xtb, stb, MODBb], [tmpb])
            h, hb = h_r.get()
            self.tt("pool", h[:], tmp[:], MODB[:, 2 * var + 1, :], ALU.add, [tmpb, MODBb], [hb])
            bk, bb = self.bank()
            bkb = bk[:].bitcast(BF16)
            for k in range(8):
                self.tr(bkb[:, k * 128:(k + 1) * 128], h[:, k * 128:(k + 1) * 128], self.ident_b[:], [hb, cb], [bb])
            hT, hTb = hT_r.get()
            self.cp("act", hT[:].rearrange("p k t -> p (k t)"), bkb[:, :], [bb], [hTb])
            def proj(c0, c1):
                bk, bb = self.bank()
                for k in range(8):
                    self.mm(bk[:, 0:c1 - c0], hT[:, k, :], W[:, k, c0:c1], k == 0, k == 7, [hTb, Wb], [bb])
                return bk, bb

            def proj2(c0):
                (b1, bb1), (b2, bb2) = self.bank(), self.bank()
                for k in range(8):
                    self.mm(b1[:, :], hT[:, k, :], W[:, k, c0:c0 + 512], k == 0, k == 7, [hTb, Wb], [bb1])
                    self.mm(b2[:, :], hT[:, k, :], W[:, k, c0 + 512:c0 + 1024], k == 0, k == 7, [hTb, Wb], [bb2])
                return (b1, bb1), (b2, bb2)
            qk, qkb = qk_r.get()
            sq, sqb = sq_r.get()
            qkf = qk[:].rearrange("p h d -> p (h d)")
            sqf = sq[:].rearrange("p h d -> p (h d)")
            (Pq, Pqb), (Pk, Pkb) = proj2(0)
            self.cp("act", qkf[:, 0:512], Pq[:, 0:512], [Pqb], [qkb])
            self.cp("act", qkf[:, 512:640], Pk[:, 0:128], [Pkb], [qkb])
            vf, vfb = vf_r.get()
            self.cp("dve", vf[:], Pk[:, 128:512], [Pkb], [vfb])
            (Pg1, Pg1b), (Pg2, Pg2b) = proj2(1024)
            g1s, g1sb = g1s_r.get()
            self.cp("act", g1s[:], Pg1[:, :], [Pg1b], [g1sb])
            vg, vgb = vg_r.get()
            self.cp("act", vg[:], Pg2[:, 0:256], [Pg2b], [vgb])
            grs, grsb = grs_r.get()
            self.cp("dve", grs[:], Pg2[:, 256:512], [Pg2b], [grsb])
            Pz, Pzb = proj(2048, 2080)
            if is_ctx:
                self.dma("sp", VC[tc * 128:(tc + 1) * 128, :], vf[:, 0:128], [vfb], [db["VC" + L]])
                self.dma("sp", FUC[tc * 128:(tc + 1) * 128, :], vf[:, 128:384], [vfb], [db["FUC" + L]])
            else:
                self.dma("sp", VP[(t + 1) * 128:(t + 2) * 128, :], vf[:, 0:128], [vfb], [db["VP" + L]])
                self.dma("sp", FUL[t * 128:(t + 1) * 128, :], vf[:, 128:384], [vfb], [db["FUL" + L]])
            z, zb_ = z_r.get()
            self.cp("act", z[:], Pz[:, 0:32], [Pzb], [zb_])
            self.act(sqf[:, :], qkf[:, :], AF.Square, [qkb], [sqb])
            sm, smb = sm_r.get()
            self.reduce_x(sm[:, 0, 0:10], sq[:], [sqb], [smb])
            self.rstd(sm[:, 2, 0:10], sm[:, 0, 0:10], 1.0 / 64, [smb, self.eps_b], [smb], sm[:, 1, 0:10], smb)
            qn, qnb = qn_r.get()
            self.tt("dve", qn[:], qk[:], sm[:, 2, 0:10].unsqueeze(2).broadcast_to([128, 10, 64]), ALU.mult, [qkb, smb], [qnb])
            self.tt("pool", qn[:], qn[:], gqk[:], ALU.mult, [qnb, cb], [qnb])
            qr, qrb = qr_r.get()
            if not is_ctx:
                rp, rpb = rp_r.get()
                self.dma("act", rp[:], rope_d[:, t * 128:(t + 1) * 128, :].rearrange("c t d -> t c d"), [], [rpb])
                t1, t1b = t1_r.get()
                t2, t2b = t2_r.get()
                self.tt("dve", t1[:], qn[:], rp[:, 0, :].unsqueeze(1).broadcast_to([128, 10, 64]), ALU.mult, [qnb, rpb], [t1b])
                qn5 = qn[:].rearrange("p h (a s c) -> p (h a) s c", a=2, s=2)
                t25 = t2[:].rearrange("p h (a s c) -> p (h a) s c", a=2, s=2)
                self.cp("pool", t25[:, :, 0, :], qn5[:, :, 1, :], [qnb], [t2b])
                self.cp("pool", t25[:, :, 1, :], qn5[:, :, 0, :], [qnb], [t2b])
                self.tt("pool", t2[:], t2[:], rp[:, 1, :].unsqueeze(1).broadcast_to([128, 10, 64]), ALU.mult, [t2b, rpb], [t2b])
                self.tt("dve", qr[:], t1[:], t2[:], ALU.add, [t1b, t2b], [qrb])
            else:
                self.cp("dve", qr[:], qn[:], [qnb], [qrb])
            bk, bb = self.bank()
            bkb = bk[:].bitcast(BF16)
            qrf = qr[:].rearrange("p h d -> p (h d)")
            for j in range(5):
                self.tr(bkb[:, j * 128:(j + 1) * 128], qrf[:, j * 128:(j + 1) * 128], self.ident_b[:], [qrb, cb], [bb])
            qkT, qkTb = qkT_r.get()
            self.cp("act", qkT[:].rearrange("p j t -> p (j t)"), bkb[:, 0:640], [bb], [qkTb])
            self.dma("sp", QT[:, :, t * 128:(t + 1) * 128], qkT[:, 0:4, :], [qkTb], [db["QT" + L]])
            if is_ctx:
                self.dma("sp", KTC[:, tc * 128:(tc + 1) * 128], qkT[:, 4, :], [qkTb], [db["KTC" + L]])
            else:
                self.dma("sp", KTP[:, (t + 1) * 128:(t + 2) * 128], qkT[:, 4, :], [qkTb], [db["KTP" + L]])
            S.rec.append(("mark",))
            bk, bb = self.bank()
            for dr in range(2):
                self.tr(bk[0:16, dr * 128:(dr + 1) * 128], z[:, dr * 16:(dr + 1) * 16], self.ident_f[:], [zb_, cb], [bb])
            self.cp("dve", zT[0:16, :, :].rearrange("p a t -> p (a t)"), bk[0:16, 0:256], [bb], [zTb])
            bk, bb = self.bank()
            for dr in range(2):
                self.mm(bk[:, dr * 256:(dr + 1) * 256], zT[:, dr, :], wg[:, dr, :], True, True, [zTb, cb], [bb])
            Lt, Lb = L_r.get()
            Lf = Lt[:].rearrange("p a c -> p (a c)")
            self.act(Lf, bk[:, :], AF.Exp, [bb], [Lb], scale=-1.0)
            self.act(Lf, Lf, AF.Ln, [Lb, self.eps_b], [Lb], bias=self.eps_t[:, 1:2])
            c1k, c1b = self.bank()
            c2k, c2b = self.bank()
            self.mm(c1k[:, 0:256], cm[:, 0, :], Lt[:, 0, :], True, True, [cb, Lb], [c1b])
            self.mm(c1k[:, 256:512], cm[:, 2, :], Lt[:, 1, :], True, True, [cb, Lb], [c1b])
            self.mm(c2k[:, 0:256], cm[:, 1, :], Lt[:, 0, :], True, True, [cb, Lb], [c2b])
            self.mm(c2k[:, 256:512], cm[:, 3, :], Lt[:, 1, :], True, True, [cb, Lb], [c2b])
            dk, dkb = self.bank()
            for dr in range(2):
                for pr in range(2):
                    i0 = (dr * 2 + pr) * 2
                    self.mm(dk[:, i0:i0 + 2], Lt[:, dr, pr * 128:(pr + 1) * 128], ci[:, :], True, True, [Lb, cb], [dkb])
            dec, decb = dec_r.get()
            self.act(dec[:], dk[:, 0:8], AF.Exp, [dkb], [decb])
            self.dma("sp", DECd[t], dec[:], [decb], [scb_[t]])
            E, Eb = E_r.get()
            self.act(E[:, 0, :], c1k[:, :], AF.Exp, [c1b, self.eps_b], [Eb], bias=self.eps_t[:, 2:3])
            self.act(E[:, 1, :], c1k[:, :], AF.Exp, [c1b], [Eb], scale=-1.0)
            self.act(E[:, 2, :], c2k[:, :], AF.Exp, [c2b], [Eb])
            qi, qib = qi_r.get()
            gq_b = g1s[:, 0:256].unsqueeze(1).broadcast_to([128, 2, 256])
            gk_b = g1s[:, 256:512].unsqueeze(1).broadcast_to([128, 2, 256])
            Pg1b = g1sb
            self.tt("dve", qi[:, 0, :, :], gq_b, E[:, 0, :].rearrange("p (a c) -> p a c", a=2), ALU.mult, [Pg1b, Eb], [qib])
            self.tt("dve", qi[:, 1, :, :], gk_b, E[:, 1, :].rearrange("p (a c) -> p a c", a=2), ALU.mult, [Pg1b, Eb], [qib])
            ko, kob = ko_r.get()
            self.tt("dve", ko[:], gk_b, E[:, 2, :].rearrange("p (a c) -> p a c", a=2), ALU.mult, [Pg1b, Eb], [kob])
            self.dma("sp", KOUTd[t], ko[:], [kob], [scb_[t]])
            self.dma("sp", VGd[t], vg[:], [vgb], [scb_[t]])
            sg, sgb = sg_r.get()
            self.act(sg[:], grs[:], AF.Exp, [grsb], [sgb], scale=-1.0)
            self.act(sg[:], sg[:], AF.Ln, [sgb, self.eps_b], [sgb], bias=self.eps_t[:, 1:2])
            self.act(sg[:], sg[:], AF.Exp, [sgb], [sgb], scale=-1.0)
            srt, srb = sr_r.get()
            self.tt("dve", srt[:], sg[:], grs[:], ALU.mult, [sgb, grsb], [srb])
            self.dma("sp", SR[t * 128:(t + 1) * 128, :], srt[:], [srb], [db["SR" + L]])
            bk, bb = self.bank()
            bkb = bk[:].bitcast(BF16)
            for wh in range(2):
                for dr in range(2):
                    for pr in range(2):
                        i0 = wh * 4 + dr * 2 + pr
                        self.tr(bkb[:, i0 * 128:(i0 + 1) * 128], qi[:, wh, dr, pr * 128:(pr + 1) * 128], self.ident_b[:], [qib, cb], [bb])
            kiT, kiTb = hT_r.get()
            qint, qintb = qint_r.get()
            self.cp("act", qint[:].rearrange("p a t -> p (a t)"), bkb[:, 0:512], [bb], [qintb])
            self.cp("act", kiT[:, 0:4, :].rearrange("p a t -> p (a t)"), bkb[:, 512:1024], [bb], [kiTb])
            self.dma("sp", QINTd[t], qint[:], [qintb], [scb_[t]])
            a1k, a1b = self.bank()
            a2k, a2b = self.bank()
            for half in range(2):
                ak, ab = (a1k, a1b) if half == 0 else (a2k, a2b)
                p0 = half * 64
                for dr in range(2):
                    for pr in range(2):
                        c0 = (dr * 2 + pr) * 128
                        self.mm(ak[:, c0:c0 + 128], kiT[p0:p0 + 64, dr * 2 + pr, :], qint[p0:p0 + 64, dr * 2 + pr, :],
                                True, True, [kiTb, qintb], [ab])
            aT, aTb = aT_r.get()
            aT5 = aT[:].rearrange("p (d r f) t -> p d r f t", d=2, r=2, f=2)
            for half in range(2):
                ak, ab = (a1k, a1b) if half == 0 else (a2k, a2b)
                for dr in range(2):
                    self.tt("dve", aT5[:, dr, :, half, :], ak[:, dr * 256:(dr + 1) * 256].rearrange("p (r t) -> p r t", r=2),
                            gm[:, dr, :].unsqueeze(1).broadcast_to([128, 2, 128]), ALU.mult, [ab, cb], [aTb])
            ok, ob = self.bank()
            for hd in range(4):
                for dr in range(2):
                    self.mm(ok[:, hd * 64:(hd + 1) * 64], aT[:, dr * 4 + hd, :], vg[:, hd * 64:(hd + 1) * 64], dr == 0, dr == 1,
                            [aTb, vgb], [ob])
            olt, oltb = ol_r.get()
            self.cp("act", olt[:], ok[:, 0:256], [ob], [oltb])
            self.dma("sp", OLOC[t * 128:(t + 1) * 128, :], olt[:], [oltb], [olb[t]])

        self.pipeline(tileA, list(range(nt)))
        self.brange = (0, 8)
        Sst = A("Sst", [128, 2, 2, 128], F32)
        Sbf = A("Sbf", [128, 2, 2, 128], BF16)
        S0 = A("S0", [128, 2, 2, 128], F32)
        sb_ = [[Buf() for _ in range(2)] for _ in range(2)]
        sbb = [[Buf() for _ in range(2)] for _ in range(2)]
        s0b = Buf()
        um_r = Ring(A, "um", [128, 2, 128], F32, 6)
        sko_r = Ring(A, "sko", [128, 256], BF16, 6)
        svg_r = Ring(A, "svg", [128, 256], BF16, 6)
        sqn_r = Ring(A, "sqn", [128, 2, 128], BF16, 6)
        sdc_r = Ring(A, "sdc", [128, 4], F32, 6)
        sol_r = Ring(A, "sol", [128, 256], F32, 6)

        def scan(tiles, init):
            for dr in range(2):
                for pr in range(2):
                    if init is None:
                        S.op("pool", lambda e: e.memset(Sst[:, dr, pr, :], 0.0), [], [sb_[dr][pr]])
                    else:
                        self.cp("pool", Sst[:, dr, pr, :], S0[:, dr, pr, :], [s0b], [sb_[dr][pr]])
                    self.cp("act", Sbf[:, dr, pr, :], Sst[:, dr, pr, :], [sb_[dr][pr]], [sbb[dr][pr]])
            for idx in range(len(tiles)):
                for dr in range(2):
                    t = tiles[idx] if dr == 0 else tiles[len(tiles) - 1 - idx]
                    ko, kob = sko_r.get()
                    self.dma("act", ko[:], KOUTd[t, :, dr, :], [scb_[t]], [kob])
                    vg, vgb = svg_r.get()
                    self.dma("act", vg[:], VGd[t], [scb_[t]], [vgb])
                    qn, qnb = sqn_r.get()
                    self.dma("act", qn[:], QINTd[t, :, dr * 2:(dr + 1) * 2, :], [scb_[t]], [qnb])
                    dc, dcb = sdc_r.get()
                    self.dma("act", dc[:], DECd[t, :, dr * 4:(dr + 1) * 4], [scb_[t]], [dcb])
                    ol, ol_b = sol_r.get()
                    self.dma("act", ol[:], OLOC[t * 128:(t + 1) * 128, :], [olb[t]], [ol_b])
                    chs = list(range(128 // GCH))
                    for ch in (chs if dr == 0 else chs[::-1]):
                        r0 = ch * GCH
                        ik, ib = self.bank()
                        for pr in range(2):
                            self.mm(ik[:, pr * 128:(pr + 1) * 128], qn[:, pr, :], Sbf[:, dr, pr, :], True, True, [qnb, sbb[dr][pr]], [ib])
                        self.tt("dve", ol[r0:r0 + GCH, :], ol[r0:r0 + GCH, :], ik[r0:r0 + GCH, 0:256], ALU.add, [ol_b, ib], [ol_b])
                        uk, ub = self.bank()
                        for pr in range(2):
                            self.mm(uk[:, pr * 128:(pr + 1) * 128], ko[r0:r0 + GCH, pr * 128:(pr + 1) * 128], vg[r0:r0 + GCH, pr * 128:(pr + 1) * 128],
                                    True, True, [kob, vgb], [ub])
                        um, umb = um_r.get()
                        self.tt("dve", um[:], uk[:, 0:256].rearrange("p (a e) -> p a e", a=2), bd[:].unsqueeze(1).broadcast_to([128, 2, 128]), ALU.mult, [ub, cb], [umb])
                        for pr in range(2):
                            dcol = dc[:, pr * 2 + ch:pr * 2 + ch + 1]
                            self.stt(Sst[:, dr, pr, :], Sst[:, dr, pr, :], dcol, um[:, pr, :], ALU.mult, ALU.add,
                                     [sb_[dr][pr], dcb, umb], [sb_[dr][pr]])
                            self.cp("act", Sbf[:, dr, pr, :], Sst[:, dr, pr, :], [sb_[dr][pr]], [sbb[dr][pr]])
                    self.dma("sp", OLOC[t * 128:(t + 1) * 128, :], ol[:], [ol_b], [olb[t]])

        scan(list(range(ntl, nt)), None)
        allS = [sb_[a][b] for a in range(2) for b in range(2)]
        self.cp("dve", S0[:], Sst[:], allS, [s0b])
        scan(list(range(ntl)), S0)
        S.barrier()
        self.sb.release(pmark)

    def own_gather(self, l, X1, xob0):
        nc = self.nc
        ao = self.aout[l]
        db = self.dbuf
        L = "%d" % l
        pid = nc.sync.partition_id()
        tok0 = (pid % 4) * TOK
        X1o = self.dram("X1own", [TOK, D])
        QTo = self.dram("QTown", [128, 4, TOK], BF16)
        KTPo = self.dram("KTPown", [128, (NTL + 2) * 128], BF16)
        VPo = self.dram("VPown", [(NTL + 2) * 128, 128], BF16)
        OLo = self.dram("OLown", [TOK, 256])
        SRo = self.dram("SRown", [TOK, 256], BF16)
        self.dma("sp", X1o, X1[bass.ds(tok0, TOK), :], list(xob0), [db["X1own"]])
        self.dma("sp", QTo, ao["QT"][:, :, bass.ds(tok0, TOK)], [db["QT" + L]], [db["QTown"]])
        self.dma("sp", KTPo, ao["KTP"][:, bass.ds(tok0, (NTL + 2) * 128)], [db["KTP" + L]], [db["KTPown"]])
        self.dma("sp", VPo, ao["VP"][bass.ds(tok0, (NTL + 2) * 128), :], [db["VP" + L]], [db["VPown"]])
        self.dma("sp", OLo, ao["OLOC"][bass.ds(tok0, TOK), :], list(ao["olb"]), [db["OLown"]])
        self.dma("sp", SRo, ao["SR"][bass.ds(tok0, TOK), :], [db["SR" + L]], [db["SRown"]])
        own = dict(ao)
        own.update(QT=QTo, KTP=KTPo, VP=VPo, OLOC=OLo, SR=SRo, olb=[db["OLown"]] * NTL)
        self.aout["own"] = own
        self.ownbufs = dict(QT=db["QTown"], KTP=db["KTPown"], VP=db["VPown"], SR=db["SRown"])
        return X1o, db["X1own"]

    def phase_B_f(self, l, mode, xin, xin_b):
        nc, S, A = self.nc, self.S, self.alloc
        db, cb = self.dbuf, self.cb
        own = mode == "own"
        ao = self.aout["own"] if own else self.aout[l]
        L = "%d" % l
        with_ctx = not own
        ntl = NTL if own else NTLF
        ntiles = ntl + (NTC if with_ctx else 0)
        tiles = list(range(ntiles))
        nvar = 2 if with_ctx else 1
        gtok = lambda t: slice(t * 128, (t + 1) * 128)
        kpad = lambda t: slice(t * 128, (t + 3) * 128)
        ltok = lambda t: slice(t * 128, (t + 1) * 128)
        MODV = self.modv[l]
        QT, KTP, KTC, VP, VC, FUL, FUC, OLOC, SR = [ao[k] for k in ("QT", "KTP", "KTC", "VP", "VC", "FUL", "FUC", "OLOC", "SR")]
        olb = ao["olb"]
        rb = (lambda k: self.ownbufs[k]) if own else (lambda k: db[k + L])
        amask_d = self.inp("amask%d" % (1 if own else 0), [128, 4, 128])
        TAB = self.inp("TABOWN", [2, 4 * 8, 128, 8 * 512], BF16) if own else self.inp("TAB", [2, 16 * 8, 128, 8 * 512], BF16)
        TABC = self.inp("TABC", [2, 1, 128, 2 * 256], BF16)
        CB_d = self.inp("CB", [128, 2, 128])
        wf_d = self.inp("wf" + L, [4, 64, 64])
        glag_d = self.inp("glag" + L, [64])
        sink_d = self.inp("sink" + L, [8])
        wout_d = self.inp("wout" + L, [D, D])
        g2_d = self.inp("g2" + L, [D])
        wffi_d = self.inp("wffi" + L, [D, 2 * HID])
        wffo_d = self.inp("wffo" + L, [HID, D])
        if own:
            XO = self.outp("XOUT", [TOK, D])
        else:
            XO = self.dram("X1", [NTOKF, D])
        FOURT = self.dram("FOURT" + L, [2, 128, ntiles * 128], BF16)
        H2T = self.dram("H2T" + L, [ntiles, 128, 8, 128], BF16)
        fourb = db["FOURT" + L]
        h2tb = [Buf() for _ in range(ntiles)]
        xob = [Buf() for _ in range(ntiles)]
        self.xob = xob
        pmark = self.sb.mark()

        Wi = A("w_ffi", [128, 8, 2 * HID], BF16, high=True)
        Wib = Buf()
        for k in range(8):
            for c0 in range(0, 2 * HID, 1408):
                self.dma("pool", Wi[:, k, c0:c0 + 1408], wffi_d[k * 128:(k + 1) * 128, c0:c0 + 1408], [], [Wib])

        amask = A("amask", [128, 4, 128], BF16)
        self.dma("pool", amask[:], amask_d, [], [cb])
        exs = A("exs", [128, 8], F32)
        self.dma("sp", exs[:], sink_d.partition_broadcast(128), [], [cb])
        self.act(exs[:], exs[:], AF.Exp, [cb], [cb])
        glag = A("glag", [128, 64], F32)
        self.dma("sp", glag[:], glag_d.partition_broadcast(128), [], [cb])
        MODB = A("modbB", [128, nvar, 4, D], F32)
        MODBb = Buf()
        mk = self.sb.mark()
        rows = A("rowsB", [1, 5, D], F32)
        rowsb = Buf()
        for var in range(nvar):
            self.dma("sp", rows[0:1, 0, :], g2_d.rearrange("(o n) -> o n", o=1), [], [rowsb])
            for (ri, c0) in ((1, 2 * D), (2, 4 * D), (3, 3 * D), (4, 5 * D)):
                self.dma("sp", rows[0:1, ri, :], MODV[var:var + 1, c0:c0 + D], [db["MODV" + L]], [rowsb])
            self.stt(rows[0:1, 2, :], rows[0:1, 2, :], 1.0, rows[0:1, 0, :], ALU.add, ALU.mult, [rowsb], [rowsb])
            for (ri, slot) in ((1, 0), (2, 1), (3, 2), (4, 3)):
                for (bk, bb, c0, c1) in self.bcast_row(rows[0:1, ri, :], rowsb, D):
                    self.cp("act", MODB[:, var, slot, c0:c1], bk[:, 0:c1 - c0], [bb], [MODBb])
        S.barrier()
        self.sb.release(mk)

        fmark = self.sb.mark()
        U = A("uall", [128, 64, 256], BF16)
        Ub = Buf()
        for q4 in range(4):
            self.dma("sp", U[:, q4 * 16:(q4 + 1) * 16, :], FUL[q4 * 2048:(q4 + 1) * 2048, :].rearrange("(k p) c -> p k c", p=128), [db["FUL" + L]], [Ub])
        CBs = A("CBs", [128, 2, 128], F32)
        self.dma("sp", CBs[:], CB_d, [], [cb])
        wblk = A("wblk", [128, 2, 128], F32)
        wblkb = Buf()
        S.op("dve", lambda e: e.memset(wblk[:], 0.0), [], [wblkb])
        for g in range(4):
            hf, gi = g // 2, g % 2
            self.dma("sp", wblk[gi * 64:(gi + 1) * 64, hf, gi * 64:(gi + 1) * 64], wf_d[g], [wblkb], [wblkb])
        Mblk = A("Mblk", [128, 2, 2, 128], BF16)
        Mb = Buf()
        for hf in range(2):
            for cs in range(2):
                bk, bb = self.bank()
                self.mm(bk[:, 0:128], CBs[:, cs, :], wblk[:, hf, :], True, True, [cb, wblkb], [bb])
                self.cp("act", Mblk[:, hf, cs, :], bk[:, 0:128], [bb], [Mb])
        tab_r = Ring(A, "tab", [128, 8, 512], BF16, 4)
        abt_r = Ring(A, "abt", [128, 2, 2, 512], BF16, 2)
        ft_r = Ring(A, "ft", [128, 512], BF16, 2)
        tabq = [0]

        def fnet(Usb, Usbb, ntc, tabsrc, nblk, blk, col0):
            tg = min(8, ntc)
            for b in range(nblk):
                abt, abtb = abt_r.get()
                acc = [[self.bank(), self.bank()], [self.bank(), self.bank()]]
                for g0 in range(0, ntc, tg):
                    tabs = []
                    for cs in range(2):
                        tab, tabb = tab_r.get()
                        tabq[0] += 1
                        self.dma("sp" if tabq[0] % 2 else "act", tab[:].rearrange("p k c -> p (k c)")[:, 0:tg * blk], tabsrc(cs, b, g0, tg), [], [tabb])
                        tabs.append((tab, tabb))
                    for tc in range(tg):
                        for hf in range(2):
                            for cs in range(2):
                                tab, tabb = tabs[cs]
                                self.mm(acc[hf][cs][0][:, 0:blk], Usb[:, g0 + tc, hf * 128:(hf + 1) * 128],
                                        tab[:].rearrange("p k c -> p (k c)")[:, tc * blk:(tc + 1) * blk],
                                        g0 + tc == 0, g0 + tc == ntc - 1, [Usbb, tabb], [acc[hf][cs][1]])
                for hf in range(2):
                    for cs in range(2):
                        self.cp("act" if cs == 0 else "dve", abt[:, hf, cs, 0:blk], acc[hf][cs][0][:, 0:blk], [acc[hf][cs][1]], [abtb])
                for hf in range(2):
                    bk, bb = self.bank()
                    for cs in range(2):
                        self.mm(bk[:, 0:blk], Mblk[:, hf, cs, :], abt[:, hf, cs, 0:blk], cs == 0, cs == 1, [Mb, abtb], [bb])
                    ft, ftb = ft_r.get()
                    self.cp("act", ft[:, 0:blk], bk[:, 0:blk], [bb], [ftb])
                    self.dma("sp", FOURT[hf, :, col0 + b * blk:col0 + (b + 1) * blk], ft[:, 0:blk], [ftb], [fourb])

        if own:
            fnet(U, Ub, 64, lambda cs, b, g0, tg: TAB[cs, b * 8 + g0 // 8], 4, 512, 0)
        else:
            fnet(U, Ub, 64, lambda cs, b, g0, tg: TAB[cs, b * 8 + g0 // 8], 16, 512, 0)
            Uc = A("uc", [128, 2, 256], BF16)
            Ucb = Buf()
            self.dma("sp", Uc[:], FUC.rearrange("(k p) c -> p k c", p=128), [db["FUC" + L]], [Ucb])
            fnet(Uc, Ucb, 2, lambda cs, b, g0, tg: TABC[cs, b], 1, 256, SEQ)
        S.barrier()
        self.sb.release(fmark)

        b1mark = self.sb.mark()
        Wo = A("w_out", [128, 8, D], BF16)
        Wob = Buf()
        for k in range(8):
            self.dma("pool", Wo[:, k, :], wout_d[k * 128:(k + 1) * 128, :], [], [Wob])
        KC = A("kc", [128, 2, 128], BF16)
        self.dma("sp", KC[:].rearrange("p k t -> p (k t)"), KTC, [db["KTC" + L]], [cb])
        VCa = A("vca", [128, 2, 2, 65], BF16)
        S.op("pool", lambda e: e.memset(VCa[:, :, :, 64:65], 1.0), [], [cb])
        for g in range(2):
            self.dma("sp", VCa[:, :, g, 0:64], VC[:, g * 64:(g + 1) * 64].rearrange("(k p) d -> p k d", p=128), [db["VC" + L]], [cb])
        def mkringsB(tag):
            kt_r = Ring(A, "ktR" + tag, [128, 3, 128], BF16, 1)
            ve_r = Ring(A, "veR" + tag, [128, 3, 2, 65], BF16, 1)
            xt_r = Ring(A, "xtB" + tag, [128, D], F32, 1)
            qt_r = Ring(A, "qtB" + tag, [128, 4, 128], BF16, 1)
            pT_r = Ring(A, "pT" + tag, [128, 5, 512], BF16, 1)
            den_r = Ring(A, "den" + tag, [128, 8], F32, 1)
            att_r = Ring(A, "att" + tag, [128, 8, 64], BF16, 1)
            mix_r = Ring(A, "mixT" + tag, [128, 8, 128], BF16, 1)
            srr = Ring(A, "srB" + tag, [128, 256], BF16, 1)
            o_r = Ring(A, "oB" + tag, [128, 4, 64], F32, 1)
            sq_r = Ring(A, "sqB" + tag, [128, 4, 64], F32, 1)
            sm_r = Ring(A, "smB" + tag, [128, 3, 4], F32, 1)
            y_r = Ring(A, "yB" + tag, [128, 4, 64], BF16, 1)
            tmp_r = Ring(A, "tmpB" + tag, [128, D], F32, 1)
            xn_r = Ring(A, "xnB" + tag, [128, 8], F32, 1)
            junk_r = Ring(A, "junkB" + tag, [128, 8], BF16, 1)
            st_r = Ring(A, "stB" + tag, [128, 4], F32, 1)
            h2_r = Ring(A, "h2B" + tag, [128, D], BF16, 1)
            h2T_r = Ring(A, "h2TB" + tag, [128, 8, 128], BF16, 1)
            for (ve, veb) in ve_r.items:
                S.op("pool", lambda e: e.memset(ve[:, :, :, 64:65], 1.0), [], [veb])
            return (kt_r, ve_r, xt_r, qt_r, pT_r, den_r, att_r, mix_r, srr, o_r, sq_r, sm_r, y_r, tmp_r, xn_r, junk_r, st_r, h2_r, h2T_r)

        RB = [mkringsB("a"), mkringsB("b")]

        def tileB(t):
            par = t % 2
            self.brange = (0, 4) if par == 0 else (4, 8)
            (kt_r, ve_r, xt_r, qt_r, pT_r, den_r, att_r, mix_r, srr, o_r, sq_r, sm_r, y_r, tmp_r, xn_r, junk_r, st_r, h2_r, h2T_r) = RB[par]
            is_ctx = t >= ntl
            var = 1 if is_ctx else 0
            qt, qtb = qt_r.get()
            self.dma("act", qt[:], QT[:, :, gtok(t)], [rb("QT")], [qtb])
            xt, xtb = xt_r.get()
            self.dma("act", xt[:], xin[gtok(t), :], xin_b(t), [xtb])
            o, ob = o_r.get()
            of = o[:].rearrange("p h e -> p (h e)")
            self.dma("act", of, OLOC[gtok(t), :], [olb[t]], [ob])
            sr, srb = srr.get()
            self.dma("act", sr[:], SR[gtok(t), :], [rb("SR")], [srb])
            mix, mixb = mix_r.get()
            self.dma("act", mix[:, 4:6, :], FOURT[:, :, ltok(t)].rearrange("h p t -> p h t"), [fourb], [mixb])
            keys = []
            if not is_ctx:
                kt, ktb = kt_r.get()
                self.dma("act", kt[:].rearrange("p k t -> p (k t)"), KTP[:, kpad(t)], [rb("KTP")], [ktb])
                ve, veb = ve_r.get()
                for g in range(2):
                    self.dma("act", ve[:, :, g, 0:64], VP[kpad(t), g * 64:(g + 1) * 64].rearrange("(k p) d -> p k d", p=128), [rb("VP")], [veb])
                m0 = 0 if t == 0 else 1
                m2 = 3 if t == ntl - 1 else 2
                keys += [(kt, ve, 0, ktb, veb, m0), (kt, ve, 1, ktb, veb, None), (kt, ve, 2, ktb, veb, m2)]
            keys += [(KC, VCa, 0, cb, cb, None), (KC, VCa, 1, cb, cb, None)]
            nk = len(keys)
            pv = [self.bank(pin=True), self.bank(pin=True)]
            for g in range(2):
                pT, pTb = pT_r.get()
                for i, (kten, vten, ki, kbuf, vbuf, mi) in enumerate(keys):
                    bk, bb = self.bank()
                    self.mm(bk[:, :], kten[g * 64:(g + 1) * 64, ki, :], qt[g * 64:(g + 1) * 64, :, :].rearrange("p j t -> p (j t)"),
                            True, True, [kbuf, qtb], [bb])
                    self.act(pT[:, i, :], bk[:, :], AF.Exp, [bb], [pTb], scale=0.125)
                    if mi is not None:
                        self.tt("pool" if g == 0 else "dve", pT[:, i, :].rearrange("p (j t) -> p j t", j=4), pT[:, i, :].rearrange("p (j t) -> p j t", j=4),
                                amask[:, mi, :].unsqueeze(1).broadcast_to([128, 4, 128]), ALU.mult, [pTb, cb], [pTb])
                for j in range(4):
                    for i, (kten, vten, ki, kbuf, vbuf, mi) in enumerate(keys):
                        self.mm(pv[g][0][:, j * 65:(j + 1) * 65], pT[:, i, j * 128:(j + 1) * 128], vten[:, ki, g, :], i == 0, i == nk - 1,
                                [pTb, vbuf], [pv[g][1]])
            S.rec.append(("mark",))
            den, denb = den_r.get()
            att, attb = att_r.get()
            for g in range(2):
                pvv = pv[g][0][:, 0:260].rearrange("p (j c) -> p j c", c=65)
                self.tt("dve", den[:, g * 4:(g + 1) * 4].unsqueeze(2), pvv[:, :, 64:65], exs[:, g * 4:(g + 1) * 4].unsqueeze(2), ALU.add,
                        [pv[g][1], cb], [denb])
            self.recip(den[:], den[:], [denb], [denb])
            for g in range(2):
                pvv = pv[g][0][:, 0:260].rearrange("p (j c) -> p j c", c=65)
                self.tt("dve", att[:, g * 4:(g + 1) * 4, :], pvv[:, :, 0:64], den[:, g * 4:(g + 1) * 4].unsqueeze(2).broadcast_to([128, 4, 64]),
                        ALU.mult, [pv[g][1], denb], [attb])
            self.unpin(pv[0][0])
            self.unpin(pv[1][0])
            sq, sqb = sq_r.get()
            self.act(sq[:].rearrange("p h e -> p (h e)"), of, AF.Square, [ob], [sqb])
            sm, smb = sm_r.get()
            self.reduce_x(sm[:, 0, :], sq[:], [sqb], [smb])
            self.rstd(sm[:, 2, :], sm[:, 0, :], 1.0 / 64, [smb, self.eps_b], [smb], sm[:, 1, :], smb)
            self.tt("dve", o[:], o[:], sm[:, 2, :].unsqueeze(2).broadcast_to([128, 4, 64]), ALU.mult, [ob, smb], [ob])
            self.tt("pool", o[:], o[:], glag[:].unsqueeze(1).broadcast_to([128, 4, 64]), ALU.mult, [ob, cb], [ob])
            y, yb = y_r.get()
            self.tt("dve", y[:].rearrange("p h e -> p (h e)"), of, sr[:], ALU.mult, [ob, srb], [yb])
            bk, bb = self.bank()
            bkb = bk[:].bitcast(BF16)
            attf = att[:].rearrange("p h e -> p (h e)")
            yf = y[:].rearrange("p h e -> p (h e)")
            for k in range(4):
                self.tr(bkb[:, k * 128:(k + 1) * 128], attf[:, k * 128:(k + 1) * 128], self.ident_b[:], [attb, cb], [bb])
            for k in range(2):
                self.tr(bkb[:, (4 + k) * 128:(5 + k) * 128], yf[:, k * 128:(k + 1) * 128], self.ident_b[:], [yb, cb], [bb])
            self.cp("act", mix[:, 0:4, :].rearrange("p k t -> p (k t)"), bkb[:, 0:512], [bb], [mixb])
            self.cp("act", mix[:, 6:8, :].rearrange("p k t -> p (k t)"), bkb[:, 512:768], [bb], [mixb])
            xn, xnb = xt, xtb
            tmp, tmpb = tmp_r.get()
            for hfc in range(2):
                bk, bb = self.bank()
                for k in range(8):
                    self.mm(bk[:, :], mix[:, k, :], Wo[:, k, hfc * 512:(hfc + 1) * 512], k == 0, k == 7, [mixb, Wob], [bb])
                self.tt("dve", tmp[:, hfc * 512:(hfc + 1) * 512], bk[:, :], MODB[:, var, 0, hfc * 512:(hfc + 1) * 512], ALU.mult, [bb, MODBb], [tmpb])
            self.tt("pool", xn[:], tmp[:], xt[:], ALU.add, [tmpb, xtb], [xnb])
            self.dma("sp", XO[ltok(t), :], xn[:], [xnb], [xob[t]])
            st, stb = st_r.get()
            h2, h2b = h2_r.get()
            self.act(h2[:], xn[:], AF.Square, [xnb], [h2b, stb], accum_out=st[:, 0:1])
            self.rstd(st[:, 2:3], st[:, 0:1], 1.0 / D, [stb, self.eps_b], [stb], st[:, 1:2], stb)
            tmp, tmpb = tmp_r.get()
            self.stt(tmp[:], xn[:], st[:, 2:3], MODB[:, var, 1, :], ALU.mult, ALU.mult, [xnb, stb, MODBb], [tmpb])
            self.tt("pool", h2[:], tmp[:], MODB[:, var, 2, :], ALU.add, [tmpb, MODBb], [h2b])
            bk, bb = self.bank()
            bkb = bk[:].bitcast(BF16)
            for k in range(8):
                self.tr(bkb[:, k * 128:(k + 1) * 128], h2[:, k * 128:(k + 1) * 128], self.ident_b[:], [h2b, cb], [bb])
            h2T, h2Tb = h2T_r.get()
            self.cp("act", h2T[:].rearrange("p k t -> p (k t)"), bkb[:, :], [bb], [h2Tb])
            self.dma("sp", H2T[t], h2T[:], [h2Tb], [h2tb[t]])
        self.pipeline(tileB, tiles)
        self.brange = (0, 8)
        S.barrier()
        self.sb.release(b1mark)

        Wf = A("w_ffo", [128, NCH, D], BF16, high=True)
        Wfb = Buf()
        for j in range(NCH):
            self.dma("pool", Wf[:, j, :], wffo_d[j * 128:(j + 1) * 128, :], [], [Wfb])
        hp_r = Ring(A, "hp", [128, 8, 256], BF16, 2)
        xp_r = Ring(A, "xp", [128, 2, D], F32, 2)
        sg_r = Ring(A, "sgF", [128, 256], BF16, 2)
        ac_r = Ring(A, "acF", [128, 256], BF16, 3)
        tm_r = Ring(A, "tmF", [128, D], F32, 1)
        xo_r = Ring(A, "xoF", [128, D], F32, 2)
        accs = self.banks[0:4]
        self.brange = (4, 8)
        for p in range(ntiles // 2):
            t0 = 2 * p
            var = 1 if t0 >= ntl else 0
            hp, hpb = hp_r.get()
            xp, xpb = xp_r.get()
            for i in range(2):
                self.dma("act", hp[:, :, i * 128:(i + 1) * 128], H2T[t0 + i], [h2tb[t0 + i]], [hpb])
                self.dma("act", xp[:, i, :], XO[ltok(t0 + i), :], [xob[t0 + i]], [xpb])
            def gu(j):
                gk, gbk = self.bank()
                uk, ubk = self.bank()
                for k in range(8):
                    self.mm(gk[:, 0:256], Wi[:, k, j * 128:(j + 1) * 128], hp[:, k, :], k == 0, k == 7, [Wib, hpb], [gbk])
                for k in range(8):
                    self.mm(uk[:, 0:256], Wi[:, k, HID + j * 128:HID + (j + 1) * 128], hp[:, k, :], k == 0, k == 7, [Wib, hpb], [ubk])
                sg, sgb = sg_r.get()
                self.act(sg[:], gk[:, 0:256], AF.Silu, [gbk], [sgb])
                ac, acb = ac_r.get()
                self.tt("dve", ac[:], sg[:], uk[:, 0:256], ALU.mult, [sgb, ubk], [acb])
                return ac, acb

            nxt = gu(0)
            for j in range(NCH):
                ac, acb = nxt
                if j + 1 < NCH:
                    nxt = gu(j + 1)
                for i in range(2):
                    for hfc in range(2):
                        a_, ab_ = accs[i * 2 + hfc]
                        self.mm(a_[:, :], ac[:, i * 128:(i + 1) * 128], Wf[:, j, hfc * 512:(hfc + 1) * 512], j == 0, j == NCH - 1, [acb, Wfb], [ab_])
            for i in range(2):
                tm, tmb = tm_r.get()
                for hfc in range(2):
                    a_, ab_ = accs[i * 2 + hfc]
                    self.tt("dve", tm[:, hfc * 512:(hfc + 1) * 512], a_[:, :], MODB[:, var, 3, hfc * 512:(hfc + 1) * 512], ALU.mult, [ab_, MODBb], [tmb])
                xo, xo_b = xo_r.get()
                self.tt("pool", xo[:], tm[:], xp[:, i, :], ALU.add, [tmb, xpb], [xo_b])
                self.dma("sp", XO[ltok(t0 + i), :], xo[:], [xo_b], [xob[t0 + i]])
        self.brange = (0, 8)
        S.barrier()
        self.sb.release(pmark)
        return XO, xob

    def finish_all(self):
        S = self.S
        deps = [(k, v) for k, v in S.cnt.items() if v > 0]
        S._wait("sp", deps)


def _perm_win(w_in_l):
    idx = []
    for j in range(4):
        idx += list(range(j * 64, (j + 1) * 64)) + list(range((j + 4) * 64, (j + 5) * 64))
    idx += list(range(512, INW))
    return np.ascontiguousarray(w_in_l[:, idx])


def _phaseA_inputs(inp, l, core, consts):
    b, r = core // 4, core % 4
    cT = np.stack([inp["c"][b].reshape(8, 128).T, inp["c_ctx"].reshape(8, 128).T], axis=-1).astype(np.float32)
    wg = np.stack([np.concatenate([inp["gla_w_gate_f"][l], inp["gla_b_gate_f"][l][None]], 0),
                   np.concatenate([inp["gla_w_gate_b"][l], inp["gla_b_gate_b"][l][None]], 0)], 0).astype(np.float32)
    m = {
        "cT": np.ascontiguousarray(cT),
        "wmod_A": inp["w_mod"][l], "bmod_A": inp["b_mod"][l], "g1_A": inp["g_norm1"][l],
        "win_A": _perm_win(inp["w_in"][l]), "qg_A": inp["q_norm_g"][l], "kg_A": inp["k_norm_g"][l],
        "wg_A": np.ascontiguousarray(wg), "rope": _rope_tables(r * TOK, TOK),
        "cm": consts["cm"], "ci": consts["ci"], "gm": consts["gm"], "bd": consts["bd"], "ident_f": consts["ident_f"],
    }
    return m


_TAB_CACHE = {}


def _dft_cached(s0, ns, n):
    key = (s0, ns, n)
    if key not in _TAB_CACHE:
        _TAB_CACHE[key] = _dft_tables(s0, ns, n)
    return _TAB_CACHE[key]


def _cb_const():
    c = np.arange(64)
    ang = 2.0 * np.pi * ((c[:, None] * c[None, :]) % 64) / 64.0
    cc = np.cos(ang) / 8.0
    sc = -np.sin(ang) / 8.0
    cb = np.zeros((128, 2, 128), np.float32)
    for gi in range(2):
        cb[gi * 64:(gi + 1) * 64, 0, gi * 64:(gi + 1) * 64] = cc
        cb[gi * 64:(gi + 1) * 64, 1, gi * 64:(gi + 1) * 64] = sc
    return cb


def _phaseB_inputs(inp, l, core, outs, consts):
    b, r = core // 4, core % 4
    o = outs[core]
    grp = [outs[4 * b + rr] for rr in range(4)]
    m = {}
    m["i_MODV"] = o["MODV_A"]
    m["i_QT"] = o["QT"]
    kfull = np.concatenate([g["KT"][:, :TOK] for g in grp], axis=1)
    vfull = np.concatenate([g["V"][:TOK] for g in grp], axis=0)
    kz = np.zeros((128, 128), kfull.dtype)
    vz = np.zeros((128, 128), vfull.dtype)
    lo, hi = r * TOK, (r + 1) * TOK
    m["i_KTE"] = np.ascontiguousarray(np.concatenate(
        [kfull[:, lo - 128:lo] if r > 0 else kz, kfull[:, lo:hi], kfull[:, hi:hi + 128] if r < 3 else kz, o["KT"][:, TOK:]], axis=1))
    m["i_VE"] = np.ascontiguousarray(np.concatenate(
        [vfull[lo - 128:lo] if r > 0 else vz, vfull[lo:hi], vfull[hi:hi + 128] if r < 3 else vz, o["V"][TOK:]], axis=0))
    m["i_UALL"] = np.ascontiguousarray(np.concatenate([g["FU"][:TOK] for g in grp], axis=0))
    m["i_FUC"] = np.ascontiguousarray(o["FU"][TOK:])
    m["i_OLOC"] = o["OLOC"]
    m["i_QTIL"] = o["QTIL"]
    m["i_SR"] = o["SR"]
    m["i_DSg"] = np.ascontiguousarray(np.stack([g["DS"] for g in grp], axis=1))
    m["i_SLOCg"] = np.ascontiguousarray(np.stack([g["SLOC"] for g in grp], axis=1))
    m["i_SCTX"] = o["SCTX"]
    rm = np.zeros((128, 4, 2), np.float32)
    for rr in range(4):
        rm[:, rr, 0] = 1.0 if rr < r else 0.0
        rm[:, rr, 1] = 1.0 if rr > r else 0.0
    m["i_rmask"] = rm
    z = np.zeros((128, 128), np.float32)
    m["i_amask"] = np.ascontiguousarray(np.stack([consts["mp"] if r > 0 else z, consts["mp"], consts["mn"], consts["mn"] if r < 3 else z], axis=1))
    m["i_TAB"] = _dft_cached(r * TOK, TOK, SEQ)
    m["i_TABC"] = _dft_cached(0, CTX, CTX)
    m["i_CB"] = consts["cbc"]
    m["i_wf"] = inp["w_fourier"][l]
    m["i_glag"] = inp["gla_norm_g"][l]
    m["i_sink"] = inp["attn_sink"][l]
    m["i_wout"] = inp["w_out"][l]
    m["i_g2"] = inp["g_norm2"][l]
    m["i_wffi"] = inp["w_ffn_in"][l]
    m["i_wffo"] = inp["w_ffn_out"][l]
    m["ident_f"] = consts["ident_f"]
    return m


def build_stage(stage):
    P = Prog(stage)
    P.setup_common()
    xin = P.inp("xin", [NTOK, D])
    if stage == 0:
        P.phase_A(0, xin, P.dbuf["xin"], True)
    elif stage == 1:
        P.A_mod()
        P.phase_B(0, xin, P.dbuf["xin"], True)
        P.phase_A(1, P.dout["XO"], P.dbuf["XO"], False)
    else:
        P.phase_B(1, xin, P.dbuf["xin"], False)
    P.finish_all()
    return P


_PROGS = {}


def build_fused():
    P = Prog("f")
    P.setup_common()
    xin = P.inp("xin", [NTOKF, D])
    xb = P.dbuf["xin"]
    P.A_mod_f(0)
    P.A_mod_f(1)
    P.phase_A_f(0, xin, lambda t: [xb])
    X1, xob0 = P.phase_B_f(0, "all", xin, lambda t: [xb])
    if DBG.get("stop_after") == "B0":
        P.finish_all()
        return P
    P.phase_A_f(1, X1, lambda t: [xob0[t]])
    X1o, x1ob = P.own_gather(1, X1, xob0)
    P.phase_B_f(1, "own", X1o, lambda t: [x1ob])
    P.finish_all()
    return P


def _fused_inputs(inp, core, consts, shared):
    b, r = core // 4, core % 4
    m = {}
    m["xin"] = np.ascontiguousarray(np.concatenate([inp["x"][b], inp["ctx"][b]], axis=0))
    m["cT"] = np.ascontiguousarray(np.stack([inp["c"][b].reshape(8, 128).T, inp["c_ctx"].reshape(8, 128).T], axis=-1).astype(np.float32))
    for l in range(DEPTH):
        L = "%d" % l
        m["wmod" + L] = inp["w_mod"][l]
        m["bmod" + L] = inp["b_mod"][l]
        m["g1" + L] = inp["g_norm1"][l]
        m["win" + L] = shared["win"][l]
        m["qg" + L] = inp["q_norm_g"][l]
        m["kg" + L] = inp["k_norm_g"][l]
        m["wg" + L] = shared["wg"][l]
        m["wf" + L] = inp["w_fourier"][l]
        m["glag" + L] = inp["gla_norm_g"][l]
        m["sink" + L] = inp["attn_sink"][l]
        m["wout" + L] = inp["w_out"][l]
        m["g2" + L] = inp["g_norm2"][l]
        m["wffi" + L] = inp["w_ffn_in"][l]
        m["wffo" + L] = inp["w_ffn_out"][l]
    m["rope"] = shared["rope"]
    for k in ("cm", "ci", "gm", "bd", "ident_f"):
        m[k] = consts[k]
    z = np.zeros((128, 128), np.float32)
    m["amask0"] = np.ascontiguousarray(np.stack([z, consts["mp"], consts["mn"], z], axis=1))
    m["amask1"] = np.ascontiguousarray(np.stack([consts["mp"] if r > 0 else z, consts["mp"], consts["mn"], consts["mn"] if r < 3 else z], axis=1))
    m["TAB"] = shared["TAB"]
    m["TABOWN"] = np.ascontiguousarray(shared["TAB"][:, r * 32:(r + 1) * 32])
    m["TABC"] = shared["TABC"]
    m["CB"] = consts["cbc"]
    return m


def _shared(inp):
    sh = {}
    sh["win"] = [_perm_win(inp["w_in"][l]) for l in range(DEPTH)]
    sh["wg"] = [np.ascontiguousarray(np.stack([np.concatenate([inp["gla_w_gate_f"][l], inp["gla_b_gate_f"][l][None]], 0),
                                               np.concatenate([inp["gla_w_gate_b"][l], inp["gla_b_gate_b"][l][None]], 0)], 0).astype(np.float32))
                for l in range(DEPTH)]
    sh["rope"] = _rope_tables(0, SEQ)
    tab = _dft_cached(0, SEQ, SEQ).reshape(2, 16, 8, 8, 128, 512).transpose(0, 1, 2, 4, 3, 5)
    sh["TAB"] = np.ascontiguousarray(tab).reshape(2, 16 * 8, 128, 8 * 512)
    tabc = _dft_cached(0, CTX, CTX).reshape(2, 1, 2, 128, 256).transpose(0, 1, 3, 2, 4)
    sh["TABC"] = np.ascontiguousarray(tabc).reshape(2, 1, 128, 2 * 256)
    return sh


def kernel(**inputs):
    inp = {k: np.asarray(v) for k, v in inputs.items()}
    consts = _host_consts()
    sh = _shared(inp)
    if "f" not in _PROGS:
        _PROGS["f"] = build_fused()
    P = _PROGS["f"]
    maps = [_fused_inputs(inp, core, consts, sh) for core in range(8)]
    res = run_bass_kernel_spmd(P.nc, maps, core_ids=list(range(8)))
    out = np.zeros((NB, SEQ, D), np.float32)
    for core in range(8):
        b, r = core // 4, core % 4
        out[b, r * TOK:(r + 1) * TOK] = np.asarray(res.results[core]["XOUT"])
    return out


DBG_OUT = {}
```

```python
from contextlib import ExitStack
import numpy as np
import ml_dtypes
import concourse.bass as bass
import concourse.mybir as mybir
from concourse.bass_utils import run_bass_kernel_spmd

F32 = mybir.dt.float32
BF16 = mybir.dt.bfloat16
AF = mybir.ActivationFunctionType
ALU = mybir.AluOpType
AX = mybir.AxisListType
NPBF = ml_dtypes.bfloat16

D = 1024
NB = 2
SEQ = 8192
DEPTH = 2
CTX = 256
TOK = 2048
NTL = TOK // 128
NTC = CTX // 128
NT = NTL + NTC
NTOK = NT * 128
NTLF = SEQ // 128
NTF = NTLF + NTC
NTOKF = NTF * 128
HID = 2816
NCH = HID // 128
INW = 2080
EPS = 1e-6
GRID_W = 64
GCH = 128
DBG = {}


class Buf:
    __slots__ = ("w", "r", "excl")

    def __init__(self, excl=False):
        self.w = []
        self.r = []
        self.excl = excl


class Sched:
    def __init__(self, nc, n_dma_sems=10):
        self.nc = nc
        self.eng = {"pe": nc.tensor, "dve": nc.vector, "act": nc.scalar, "pool": nc.gpsimd, "sp": nc.sync}
        self.sems = {}
        self.cnt = {}
        self.seen = {e: {} for e in self.eng}
        for e in self.eng:
            self.sems[e] = nc.alloc_semaphore("s_" + e)
            self.cnt[e] = 0
        self.dsem = {}
        self.dpos = {}
        for q in ("sp", "act", "pool"):
            self.dsem[q] = []
            for i in range(n_dma_sems):
                k = "d_%s_%d" % (q, i)
                self.sems[k] = nc.alloc_semaphore(k)
                self.cnt[k] = 0
                self.dsem[q].append(k)
            self.dpos[q] = 0
        self.all_out = []
        self.rec = None

    def _wait(self, e, deps):
        best = {}
        for (k, v) in deps:
            if v > best.get(k, 0):
                best[k] = v
        for k, v in best.items():
            if e == "pe" and k == "pe":
                continue
            if self.seen[e].get(k, 0) < v:
                self.eng[e].wait_ge(self.sems[k], v)
                self.seen[e][k] = v

    @staticmethod
    def _deps(reads, writes):
        deps = []
        for b in reads:
            deps += b.w
            if b.excl:
                deps += b.r
        for b in writes:
            deps += b.w
            deps += b.r
        return deps

    @staticmethod
    def _commit(tick, reads, writes):
        for b in reads:
            b.r.append(tick)
            if len(b.r) > 64:
                best = {}
                for (k, v) in b.r:
                    if v > best.get(k, 0):
                        best[k] = v
                b.r = list(best.items())
        for b in writes:
            b.w = [tick]
            b.r = []

    def op(self, e, fn, reads=(), writes=()):
        if self.rec is not None:
            self.rec.append(("op", e, fn, tuple(reads), tuple(writes)))
            return None
        self._wait(e, self._deps(reads, writes))
        inst = fn(self.eng[e])
        self.cnt[e] += 1
        inst.then_inc(self.sems[e], 1)
        tick = (e, self.cnt[e])
        self._commit(tick, reads, writes)
        return tick

    def play(self, lists):
        assert self.rec is None
        n = max(len(l) for l in lists)
        for i in range(n):
            for l in lists:
                if i < len(l):
                    it = l[i]
                    if it[0] == "mark":
                        continue
                    if it[0] == "op":
                        self.op(it[1], it[2], it[3], it[4])
                    else:
                        self.dma(it[1], it[2], it[3], it[4], it[5], **it[6])

    def dma(self, q, out, in_, reads=(), writes=(), **kw):
        if self.rec is not None:
            self.rec.append(("dma", q, out, in_, tuple(reads), tuple(writes), kw))
            return None
        k = self.dsem[q][self.dpos[q] % len(self.dsem[q])]
        self.dpos[q] += 1
        deps = []
        for b in reads:
            deps += b.w
        for b in writes:
            deps += [w for w in b.w if not w[0].startswith("d_")]
            deps += b.r
        if self.cnt[k] > 0:
            deps.append((k, self.cnt[k]))
        self._wait(q, deps)
        inst = self.eng[q].dma_start(out=out, in_=in_, **kw)
        self.cnt[k] += 16
        inst.then_inc(self.sems[k], 16)
        tick = (k, self.cnt[k])
        for b in reads:
            b.r.append(tick)
        for b in writes:
            best = {}
            for (kk, v) in b.w + [tick]:
                if kk.startswith("d_") and v > best.get(kk, 0):
                    best[kk] = v
            b.w = list(best.items())
            b.r = []
        return tick

    def barrier(self):
        deps = [(k, v) for k, v in self.cnt.items() if v > 0]
        for e in self.eng:
            self._wait(e, [d for d in deps if d[0] != e])

    def finish(self, bufs):
        deps = []
        for b in bufs:
            deps += b.w
        self._wait("sp", deps)


class SBAlloc:
    LO0 = 16512
    HI0 = 229344

    def __init__(self, nc):
        self.nc = nc
        self.lo = self.LO0
        self.hi = self.HI0
        self.n = 0

    @staticmethod
    def _size(shape, dt):
        n = 1
        for d in shape[1:]:
            n *= d
        b = n * (2 if dt == BF16 else 4)
        return (b + 31) // 32 * 32

    def alloc(self, name, shape, dt, high=False):
        sz = self._size(shape, dt)
        if high:
            self.hi -= sz
            off = self.hi
        else:
            off = self.lo
            self.lo += sz
        assert self.lo <= self.hi, "SBUF overflow allocating %s: lo=%d hi=%d" % (name, self.lo, self.hi)
        self.n += 1
        return self.nc.alloc_sbuf_tensor_at("sb%d_%s" % (self.n, name), list(shape), dt, offset=off)

    def mark(self):
        return (self.lo, self.hi)

    def release(self, m):
        self.lo, self.hi = m


class Ring:
    def __init__(self, alloc, name, shape, dtype, n=2):
        self.items = [(alloc("%s_%d" % (name, i), shape, dtype), Buf()) for i in range(n)]
        self.pos = 0

    def get(self):
        it = self.items[self.pos % len(self.items)]
        self.pos += 1
        return it


def _host_consts():
    c = {}
    c["ident_f"] = np.eye(128, dtype=np.float32)
    s = np.arange(128)[:, None]
    t = np.arange(128)[None, :]
    same = (s // GCH) == (t // GCH)
    cm = np.zeros((128, 4, 128), np.float32)
    cm[:, 0, :] = (same & (s <= t)) * (-1.0 / 16)
    cm[:, 1, :] = (same & (s > t)) * (-1.0 / 16)
    cm[:, 2, :] = (same & (s >= t)) * (-1.0 / 16)
    cm[:, 3, :] = (same & (s < t)) * (-1.0 / 16)
    c["cm"] = cm
    ci = np.zeros((128, 2), np.float32)
    ci[:GCH, 0] = -1.0 / 16
    ci[GCH:, 1] = -1.0 / 16
    c["ci"] = ci
    gm = np.zeros((128, 2, 128), np.float32)
    gm[:, 0, :] = same & (s <= t)
    gm[:, 1, :] = same & (s >= t)
    c["gm"] = gm
    c["bd"] = ((s // 64) == (t // 64)).astype(np.float32)
    c["mp"] = (s >= t).astype(np.float32)
    c["mn"] = (s <= t).astype(np.float32)
    c["cbc"] = _cb_const()
    return c


def _rope_tables(pos0, n):
    pos = np.arange(pos0, pos0 + n)
    row = (pos // GRID_W).astype(np.float32)
    col = (pos % GRID_W).astype(np.float32)
    inv = (10000.0 ** (-np.arange(0, 32, 2, dtype=np.float32) / 32.0)).astype(np.float32)
    ar = row[:, None] * inv[None, :]
    ac = col[:, None] * inv[None, :]
    cr, sr, cc, sc = np.cos(ar), np.sin(ar), np.cos(ac), np.sin(ac)
    cos64 = np.concatenate([cr, cr, cc, cc], axis=1).astype(np.float32)
    sin64 = np.concatenate([-sr, sr, -sc, sc], axis=1).astype(np.float32)
    return np.stack([cos64, sin64], axis=0)


def _dft_tables(s0, ns, n):
    t = np.arange(n, dtype=np.int64)[:, None]
    s = (s0 + np.arange(ns, dtype=np.int64))[None, :]
    ang = (2.0 * np.pi / n) * ((t * s) % n).astype(np.float64)
    out = []
    for f in (np.cos, np.sin):
        m = (f(ang) / np.sqrt(n)).astype(np.float32)
        blk = min(512, ns)
        m = m.reshape(n // 128, 128, ns // blk, blk).transpose(2, 0, 1, 3)
        out.append(m.astype(NPBF))
    return np.stack(out, axis=0)


class Prog:
    def __init__(self, stage):
        self.stage = stage
        self.nc = bass.Bass("TRN2", target_bir_lowering=False)
        self.S = Sched(self.nc)
        self.din = {}
        self.dout = {}
        self.dbuf = {}
        nc = self.nc
        self.sb = SBAlloc(nc)
        self.alloc = self.sb.alloc
        self.banks = [(nc.alloc_psum_tensor("ps%d" % i, [128, 512], F32), Buf(excl=True)) for i in range(8)]
        self.bpos = 0
        self.brange = (0, 8)
        self.pinned = set()
        self.mod_done = False
        self.modv = {}
        self.aout = {}

    def dram(self, name, shape, dt=F32):
        t = self.nc.dram_tensor(name, list(shape), dt, kind="Internal").ap()
        self.dbuf[name] = Buf()
        return t

    def inp(self, name, shape, dt=F32):
        if name in self.din:
            return self.din[name]
        t = self.nc.dram_tensor(name, list(shape), dt, kind="ExternalInput").ap()
        self.din[name] = t
        self.dbuf[name] = Buf()
        return t

    def outp(self, name, shape, dt=F32):
        t = self.nc.dram_tensor(name, list(shape), dt, kind="ExternalOutput").ap()
        self.dout[name] = t
        self.dbuf[name] = Buf()
        return t

    def bank(self, pin=False):
        lo, hi = self.brange
        while True:
            idx = lo + self.bpos % (hi - lo)
            self.bpos += 1
            if idx not in self.pinned:
                break
        if pin:
            self.pinned.add(idx)
        return self.banks[idx]

    def unpin(self, bk):
        for i, (t, b) in enumerate(self.banks):
            if t is bk:
                self.pinned.discard(i)

    def mm(self, out, lhsT, rhs, start, stop, reads, writes):
        return self.S.op("pe", lambda e: e.matmul(out, lhsT, rhs, start=start, stop=stop), reads, writes)

    def tr(self, out, in_, ident, reads, writes):
        return self.S.op("pe", lambda e: e.transpose(out, in_, ident), reads, writes)

    def act(self, out, in_, func, reads, writes, **kw):
        return self.S.op("act", lambda e: e.activation(out=out, in_=in_, func=func, **kw), reads, writes)

    def tt(self, eng, out, in0, in1, op, reads, writes):
        return self.S.op(eng, lambda e: e.tensor_tensor(out=out, in0=in0, in1=in1, op=op), reads, writes)

    def ts(self, eng, out, in0, s1, s2, op0, op1, reads, writes):
        if op1 is None:
            return self.S.op(eng, lambda e: e.tensor_scalar(out=out, in0=in0, scalar1=s1, scalar2=None, op0=op0), reads, writes)
        return self.S.op(eng, lambda e: e.tensor_scalar(out=out, in0=in0, scalar1=s1, scalar2=s2, op0=op0, op1=op1), reads, writes)

    def stt(self, out, in0, scalar, in1, op0, op1, reads, writes):
        return self.S.op("dve", lambda e: e.scalar_tensor_tensor(out=out, in0=in0, scalar=scalar, in1=in1, op0=op0, op1=op1), reads, writes)

    def cp(self, eng, out, in_, reads, writes):
        if eng == "act":
            return self.S.op("act", lambda e: e.copy(out=out, in_=in_), reads, writes)
        return self.S.op(eng, lambda e: e.tensor_copy(out=out, in_=in_), reads, writes)

    def dma(self, q, out, in_, reads, writes, **kw):
        return self.S.dma(q, out, in_, reads, writes, **kw)

    def pipeline(self, tile_fn, tiles):
        S = self.S
        prev = None
        for t in tiles:
            S.rec = []
            tile_fn(t)
            L = S.rec
            S.rec = None
            h = [i for i, it in enumerate(L) if it[0] == "mark"]
            h = h[0] if h else len(L) // 2
            if prev is None:
                S.play([L[:h]])
            else:
                S.play([prev, L[:h]])
            prev = L[h:]
        if prev:
            S.play([prev])

    def recip(self, out, in_, reads, writes):
        return self.S.op("dve", lambda e: e.reciprocal(out=out, in_=in_), reads, writes)

    def reduce_x(self, out, in_, reads, writes):
        return self.S.op("dve", lambda e: e.tensor_reduce(out=out, in_=in_, axis=AX.X, op=ALU.add), reads, writes)

    def rstd(self, out, ss, scale, reads, writes, tmp, tmpb):
        self.act(tmp, ss, AF.Ln, reads, [tmpb], scale=scale, bias=self.eps_t[:, 0:1])
        self.act(out, tmp, AF.Exp, [tmpb], writes, scale=-0.5)

    def setup_common(self):
        nc, S = self.nc, self.S
        A = self.alloc
        self.eps_t = A("eps_t", [128, 4], F32)
        self.eps_b = Buf()
        S.op("dve", lambda e: e.memset(self.eps_t[:, 0:1], EPS), [], [self.eps_b])
        S.op("dve", lambda e: e.memset(self.eps_t[:, 1:2], 1.0), [], [self.eps_b])
        S.op("dve", lambda e: e.memset(self.eps_t[:, 2:3], float(np.log(0.125))), [], [self.eps_b])
        S.op("dve", lambda e: e.memset(self.eps_t[:, 3:4], 0.0), [], [self.eps_b])
        self.ident_f = A("ident_f", [128, 128], F32)
        self.ident_b = A("ident_b", [128, 128], BF16)
        self.ones_r = A("ones_r", [1, 128], F32)
        self.cb = Buf()
        d = self.inp("ident_f", [128, 128])
        self.dma("sp", self.ident_f[:], d, [], [self.cb])
        self.dma("pool", self.ident_b[:], d, [], [self.cb])
        S.op("dve", lambda e: e.memset(self.ones_r[:], 1.0), [], [self.cb])

    def bcast_row(self, row_ap, rowb, n):
        outs = []
        for c0 in range(0, n, 512):
            c1 = min(n, c0 + 512)
            bk, bb = self.bank()
            self.mm(bk[:, 0:c1 - c0], self.ones_r[0:1, :], row_ap[0:1, c0:c1], True, True, [self.cb, rowb], [bb])
            outs.append((bk, bb, c0, c1))
        return outs

    def A_mod(self):
        nc, S, A = self.nc, self.S, self.alloc
        sfx = "_A"
        db = self.dbuf
        cT_d = self.inp("cT", [128, 8, 2])
        wmod_d = self.inp("wmod" + sfx, [D, 6 * D])
        bmod_d = self.inp("bmod" + sfx, [6 * D])
        MODV = self.outp("MODV" + sfx, [2, 6 * D])
        mk = self.sb.mark()
        cTs = A("cTs", [128, 8, 2], F32)
        cTb = Buf()
        self.dma("sp", cTs[:], cT_d, [], [cTb])
        scT = A("scT", [128, 8, 64], BF16)
        scb = Buf()
        S.op("dve", lambda e: e.memset(scT[:], 0.0), [], [scb])
        sil = A("sil", [128, 8, 2], F32)
        silb = Buf()
        self.act(sil[:], cTs[:], AF.Exp, [cTb], [silb], scale=-1.0)
        self.ts("dve", sil[:], sil[:], 1.0, None, ALU.add, None, [silb], [silb])
        S.op("dve", lambda e: e.reciprocal(out=sil[:], in_=sil[:]), [silb], [silb])
        self.tt("dve", sil[:], sil[:], cTs[:], ALU.mult, [silb, cTb], [silb])
        self.cp("dve", scT[:, :, 0:1], sil[:, :, 0:1], [silb], [scb])
        self.cp("dve", scT[:, :, 32:33], sil[:, :, 1:2], [silb], [scb])
        wm_ring = Ring(A, "wm", [128, 8, 512], BF16, 2)
        bm_ring = Ring(A, "bm", [64, 512], F32, 2)
        mr_ring = Ring(A, "mr", [64, 512], F32, 2)
        for cbk in range(12):
            wm, wmb = wm_ring.get()
            self.dma("pool", wm[:], wmod_d[:, cbk * 512:(cbk + 1) * 512].rearrange("(k p) c -> p k c", p=128), [], [wmb])
            bm, bmb = bm_ring.get()
            self.dma("sp", bm[:], bmod_d[cbk * 512:(cbk + 1) * 512].partition_broadcast(64), [], [bmb])
            bk, bb = self.bank()
            for k in range(8):
                self.mm(bk[0:64, :], scT[:, k, :], wm[:, k, :], k == 0, k == 7, [scb, wmb], [bb])
            mr, mrb = mr_ring.get()
            self.tt("dve", mr[:], bk[0:64, :], bm[:], ALU.add, [bb, bmb], [mrb])
            self.dma("sp", MODV[0:1, cbk * 512:(cbk + 1) * 512], mr[0:1, :], [mrb], [db["MODV" + sfx]])
            self.dma("sp", MODV[1:2, cbk * 512:(cbk + 1) * 512], mr[32:33, :], [mrb], [db["MODV" + sfx]])

        S.barrier()
        self.sb.release(mk)
        self.mod_done = True

    def phase_A(self, l, xin, xin_b, first_stage):
        nc, S, A = self.nc, self.S, self.alloc
        sfx = "_A"
        g1_d = self.inp("g1" + sfx, [D])
        win_d = self.inp("win" + sfx, [D, INW])
        qg_d = self.inp("qg" + sfx, [64])
        kg_d = self.inp("kg" + sfx, [64])
        wg_d = self.inp("wg" + sfx, [2, 17, 256])
        rope_d = self.inp("rope", [2, TOK, 64])
        cm_d = self.inp("cm", [128, 4, 128])
        ci_d = self.inp("ci", [128, 2])
        gm_d = self.inp("gm", [128, 2, 128])
        bd_d = self.inp("bd", [128, 128])
        if not self.mod_done:
            self.A_mod()
        MODV = self.dout["MODV" + sfx]
        QT = self.outp("QT", [128, 4, NTOK], BF16)
        KT = self.outp("KT", [128, NTOK], BF16)
        V = self.outp("V", [NTOK, 128], BF16)
        FU = self.outp("FU", [NTOK, 256], BF16)
        OLOC = self.outp("OLOC", [NTOK, 256])
        QTIL = self.outp("QTIL", [2, 128, 2, NTOK], BF16)
        SR = self.outp("SR", [NTOK, 256], BF16)
        DS = self.outp("DS", [128, 2, 2])
        SLOC = self.outp("SLOC", [128, 2, 2, 128])
        SCTX = self.outp("SCTX", [128, 2, 2, 128])
        db = self.dbuf
        cb = self.cb

        cm = A("cm", [128, 4, 128], F32)
        ci = A("ci", [128, 2], F32)
        gm = A("gm", [128, 2, 128], BF16)
        bd = A("bd", [128, 128], F32)
        self.dma("sp", cm[:], cm_d, [], [cb])
        self.dma("sp", ci[:], ci_d, [], [cb])
        self.dma("pool", gm[:], gm_d, [], [cb])
        self.dma("sp", bd[:], bd_d, [], [cb])
        gqk = A("gqk", [128, 10, 64], F32)
        for hh in range(10):
            src = (qg_d if hh < 8 else kg_d).partition_broadcast(128)
            self.dma("sp", gqk[:, hh, :], src, [], [cb])
        wg = A("wg", [17, 2, 256], F32)
        for dr in range(2):
            self.dma("sp", wg[:, dr, :], wg_d[dr], [], [cb])
        zT = A("zT", [17, 2, 128], F32)
        zTb = Buf()
        S.op("dve", lambda e: e.memset(zT[:], 1.0), [], [zTb])

        W = A("w_in", [128, 8, INW], BF16)
        Wb = Buf()
        for k in range(8):
            for c0 in (0, 1040):
                self.dma("pool", W[:, k, c0:c0 + 1040], win_d[k * 128:(k + 1) * 128, c0:c0 + 1040], [], [Wb])

        MODB = A("modb", [128, 4, D], F32)
        MODBb = Buf()
        mk = self.sb.mark()
        rows = A("rowsA", [1, 3, D], F32)
        rowsb = Buf()
        for var in range(2):
            self.dma("sp", rows[0:1, 0, :], g1_d.rearrange("(o n) -> o n", o=1), [], [rowsb])
            self.dma("sp", rows[0:1, 1, :], MODV[var:var + 1, D:2 * D], [db["MODV" + sfx]], [rowsb])
            self.dma("sp", rows[0:1, 2, :], MODV[var:var + 1, 0:D], [db["MODV" + sfx]], [rowsb])
            self.stt(rows[0:1, 1, :], rows[0:1, 1, :], 1.0, rows[0:1, 0, :], ALU.add, ALU.mult, [rowsb], [rowsb])
            for (ri, slot) in ((1, 2 * var), (2, 2 * var + 1)):
                for (bk, bb, c0, c1) in self.bcast_row(rows[0:1, ri, :], rowsb, D):
                    self.cp("act", MODB[:, slot, c0:c1], bk[:, 0:c1 - c0], [bb], [MODBb])

        S.barrier()
        self.sb.release(mk)
        if DBG.get("stop") == "mod":
            return
        KOUT = A("kout", [128, NT, 2, 256], BF16)
        VG = A("vg", [128, NT, 256], BF16)
        QINT = A("qint", [128, NT, 4, 128], BF16)
        OL = A("ol", [128, NT, 256], F32)
        DEC = A("dec", [128, NT, 8], F32)
        tb = [dict(kout=Buf(), vg=Buf(), qint=Buf(), ol=Buf(), dec=Buf()) for _ in range(NT)]

        xt_r = Ring(A, "xt", [128, D], F32, 2)
        junk_r = Ring(A, "junk", [128, D], BF16, 1)
        st_r = Ring(A, "st", [128, 8], F32, 2)
        tmp_r = Ring(A, "tmpA", [128, D], F32, 1)
        h_r = Ring(A, "h", [128, D], BF16, 2)
        hT_r = Ring(A, "hT", [128, 8, 128], BF16, 2)
        qk_r = Ring(A, "qk", [128, 10, 64], F32, 1)
        sq_r = Ring(A, "sq", [128, 10, 64], F32, 1)
        sm_r = Ring(A, "sm", [128, 3, 16], F32, 2)
        qn_r = Ring(A, "qn", [128, 10, 64], F32, 1)
        t1_r = Ring(A, "t1", [128, 10, 64], F32, 1)
        t2_r = Ring(A, "t2", [128, 10, 64], F32, 1)
        qr_r = Ring(A, "qr", [128, 10, 64], BF16, 2)
        qkT_r = Ring(A, "qkT", [128, 5, 128], BF16, 2)
        rp_r = Ring(A, "rp", [128, 2, 64], F32, 2)
        vf_r = Ring(A, "vf", [128, 384], BF16, 2)
        z_r = Ring(A, "z", [128, 32], F32, 2)
        L_r = Ring(A, "L", [128, 2, 256], F32, 2)
        E_r = Ring(A, "E", [128, 3, 512], F32, 1)
        qi_r = Ring(A, "qi", [128, 2, 2, 256], BF16, 2)
        aT_r = Ring(A, "aT", [128, 8, 128], BF16, 2)
        sg_r = Ring(A, "sg", [128, 256], F32, 1)
        sr_r = Ring(A, "srt", [128, 256], BF16, 2)

        for t in range(DBG.get("ntiles", NT)):
            is_ctx = t >= NTL
            var = 1 if is_ctx else 0
            xt, xtb = xt_r.get()
            self.dma("sp", xt[:], xin[t * 128:(t + 1) * 128, :], [xin_b], [xtb])
            st, stb = st_r.get()
            junk, junkb = junk_r.get()
            self.act(junk[:], xt[:], AF.Square, [xtb], [junkb, stb], accum_out=st[:, 0:1])
            self.rstd(st[:, 2:3], st[:, 0:1], 1.0 / D, [stb, self.eps_b], [stb], st[:, 1:2], stb)
            tmp, tmpb = tmp_r.get()
            self.stt(tmp[:], xt[:], st[:, 2:3], MODB[:, 2 * var, :], ALU.mult, ALU.mult, [xtb, stb, MODBb], [tmpb])
            h, hb = h_r.get()
            self.tt("pool", h[:], tmp[:], MODB[:, 2 * var + 1, :], ALU.add, [tmpb, MODBb], [hb])
            if DBG.get('tstop') == 1:
                continue
            bk, bb = self.bank()
            bkb = bk[:].bitcast(BF16)
            for k in range(8):
                self.tr(bkb[:, k * 128:(k + 1) * 128], h[:, k * 128:(k + 1) * 128], self.ident_b[:], [hb, cb], [bb])
            hT, hTb = hT_r.get()
            self.cp("act", hT[:].rearrange("p k t -> p (k t)"), bkb[:, :], [bb], [hTb])
            if DBG.get('tstop') == 2:
                continue
            pbk = []
            for (c0, c1) in ((0, 512), (512, 1024), (1024, 1536), (1536, 2048), (2048, 2080)):
                bk, bb = self.bank(pin=True)
                for k in range(8):
                    self.mm(bk[:, 0:c1 - c0], hT[:, k, :], W[:, k, c0:c1], k == 0, k == 7, [hTb, Wb], [bb])
                pbk.append((bk, bb))
            (Pq, Pqb), (Pk, Pkb), (Pg1, Pg1b), (Pg2, Pg2b), (Pz, Pzb) = pbk
            if DBG.get('tstop') == 3:
                continue
            qk, qkb = qk_r.get()
            sq, sqb = sq_r.get()
            qkf = qk[:].rearrange("p h d -> p (h d)")
            sqf = sq[:].rearrange("p h d -> p (h d)")
            self.cp("act", qkf[:, 0:512], Pq[:, 0:512], [Pqb], [qkb])
            self.cp("act", qkf[:, 512:640], Pk[:, 0:128], [Pkb], [qkb])
            self.unpin(Pq)
            vf, vfb = vf_r.get()
            self.cp("dve", vf[:], Pk[:, 128:512], [Pkb], [vfb])
            self.unpin(Pk)
            self.dma("sp", V[t * 128:(t + 1) * 128, :], vf[:, 0:128], [vfb], [db["V"]])
            self.dma("sp", FU[t * 128:(t + 1) * 128, :], vf[:, 128:384], [vfb], [db["FU"]])
            z, zb_ = z_r.get()
            self.cp("act", z[:], Pz[:, 0:32], [Pzb], [zb_])
            self.unpin(Pz)
            self.act(sqf[:, :], qkf[:, :], AF.Square, [qkb], [sqb])
            sm, smb = sm_r.get()
            S.op("dve", lambda e: e.tensor_reduce(out=sm[:, 0, 0:10], in_=sq[:], axis=AX.X, op=ALU.add), [sqb], [smb])
            self.rstd(sm[:, 2, 0:10], sm[:, 0, 0:10], 1.0 / 64, [smb, self.eps_b], [smb], sm[:, 1, 0:10], smb)
            qn, qnb = qn_r.get()
            self.tt("dve", qn[:], qk[:], sm[:, 2, 0:10].unsqueeze(2).broadcast_to([128, 10, 64]), ALU.mult, [qkb, smb], [qnb])
            self.tt("pool", qn[:], qn[:], gqk[:], ALU.mult, [qnb, cb], [qnb])
            if DBG.get('tstop') == 4:
                continue
            qr, qrb = qr_r.get()
            if not is_ctx:
                rp, rpb = rp_r.get()
                self.dma("sp", rp[:], rope_d[:, t * 128:(t + 1) * 128, :].rearrange("c t d -> t c d"), [], [rpb])
                t1, t1b = t1_r.get()
                t2, t2b = t2_r.get()
                self.tt("dve", t1[:], qn[:], rp[:, 0, :].unsqueeze(1).broadcast_to([128, 10, 64]), ALU.mult, [qnb, rpb], [t1b])
                qn5 = qn[:].rearrange("p h (a s c) -> p (h a) s c", a=2, s=2)
                t25 = t2[:].rearrange("p h (a s c) -> p (h a) s c", a=2, s=2)
                self.cp("pool", t25[:, :, 0, :], qn5[:, :, 1, :], [qnb], [t2b])
                self.cp("pool", t25[:, :, 1, :], qn5[:, :, 0, :], [qnb], [t2b])
                self.tt("pool", t2[:], t2[:], rp[:, 1, :].unsqueeze(1).broadcast_to([128, 10, 64]), ALU.mult, [t2b, rpb], [t2b])
                self.tt("dve", qr[:], t1[:], t2[:], ALU.add, [t1b, t2b], [qrb])
            else:
                self.cp("dve", qr[:], qn[:], [qnb], [qrb])
            if DBG.get('tstop') == 5:
                continue
            bk, bb = self.bank()
            bkb = bk[:].bitcast(BF16)
            qrf = qr[:].rearrange("p h d -> p (h d)")
            for j in range(5):
                self.tr(bkb[:, j * 128:(j + 1) * 128], qrf[:, j * 128:(j + 1) * 128], self.ident_b[:], [qrb, cb], [bb])
            qkT, qkTb = qkT_r.get()
            self.cp("act", qkT[:].rearrange("p j t -> p (j t)"), bkb[:, 0:640], [bb], [qkTb])
            self.dma("sp", QT[:, :, t * 128:(t + 1) * 128], qkT[:, 0:4, :], [qkTb], [db["QT"]])
            self.dma("sp", KT[:, t * 128:(t + 1) * 128], qkT[:, 4, :], [qkTb], [db["KT"]])
            if DBG.get('tstop') == 6:
                continue
            if DBG.get('tstop') == 7:
                continue
            bk, bb = self.bank()
            for dr in range(2):
                self.tr(bk[0:16, dr * 128:(dr + 1) * 128], z[:, dr * 16:(dr + 1) * 16], self.ident_f[:], [zb_, cb], [bb])
            self.cp("dve", zT[0:16, :, :].rearrange("p a t -> p (a t)"), bk[0:16, 0:256], [bb], [zTb])
            bk, bb = self.bank()
            for dr in range(2):
                self.mm(bk[:, dr * 256:(dr + 1) * 256], zT[:, dr, :], wg[:, dr, :], True, True, [zTb, cb], [bb])
            Lt, Lb = L_r.get()
            Lf = Lt[:].rearrange("p a c -> p (a c)")
            self.act(Lf, bk[:, :], AF.Exp, [bb], [Lb], scale=-1.0)
            self.act(Lf, Lf, AF.Ln, [Lb, self.eps_b], [Lb], bias=self.eps_t[:, 1:2])
            if DBG.get('tstop') == 8:
                continue
            c1k, c1b = self.bank()
            c2k, c2b = self.bank()
            self.mm(c1k[:, 0:256], cm[:, 0, :], Lt[:, 0, :], True, True, [cb, Lb], [c1b])
            self.mm(c1k[:, 256:512], cm[:, 2, :], Lt[:, 1, :], True, True, [cb, Lb], [c1b])
            self.mm(c2k[:, 0:256], cm[:, 1, :], Lt[:, 0, :], True, True, [cb, Lb], [c2b])
            self.mm(c2k[:, 256:512], cm[:, 3, :], Lt[:, 1, :], True, True, [cb, Lb], [c2b])
            dk, dkb = self.bank()
            for dr in range(2):
                for pr in range(2):
                    i0 = (dr * 2 + pr) * 2
                    self.mm(dk[:, i0:i0 + 2], Lt[:, dr, pr * 128:(pr + 1) * 128], ci[:, :], True, True, [Lb, cb], [dkb])
            self.act(DEC[:, t, :], dk[:, 0:8], AF.Exp, [dkb], [tb[t]["dec"]])
            if DBG.get('tstop') == 9:
                continue
            E, Eb = E_r.get()
            self.act(E[:, 0, :], c1k[:, :], AF.Exp, [c1b, self.eps_b], [Eb], bias=self.eps_t[:, 2:3])
            self.act(E[:, 1, :], c1k[:, :], AF.Exp, [c1b], [Eb], scale=-1.0)
            self.act(E[:, 2, :], c2k[:, :], AF.Exp, [c2b], [Eb])
            qi, qib = qi_r.get()
            gq_b = Pg1[:, 0:256].unsqueeze(1).broadcast_to([128, 2, 256])
            gk_b = Pg1[:, 256:512].unsqueeze(1).broadcast_to([128, 2, 256])
            self.tt("dve", qi[:, 0, :, :], gq_b, E[:, 0, :].rearrange("p (a c) -> p a c", a=2), ALU.mult, [Pg1b, Eb], [qib])
            self.tt("dve", qi[:, 1, :, :], gk_b, E[:, 1, :].rearrange("p (a c) -> p a c", a=2), ALU.mult, [Pg1b, Eb], [qib])
            self.tt("dve", KOUT[:, t, :, :], gk_b, E[:, 2, :].rearrange("p (a c) -> p a c", a=2), ALU.mult, [Pg1b, Eb], [tb[t]["kout"]])
            self.cp("act", VG[:, t, :], Pg2[:, 0:256], [Pg2b], [tb[t]["vg"]])
            if DBG.get('tstop') == 10:
                continue
            sg, sgb = sg_r.get()
            self.act(sg[:], Pg2[:, 256:512], AF.Exp, [Pg2b], [sgb], scale=-1.0)
            self.ts("pool", sg[:], sg[:], 1.0, None, ALU.add, None, [sgb], [sgb])
            S.op("dve", lambda e: e.reciprocal(out=sg[:], in_=sg[:]), [sgb], [sgb])
            srt, srb = sr_r.get()
            self.tt("dve", srt[:], sg[:], Pg2[:, 256:512], ALU.mult, [sgb, Pg2b], [srb])
            self.dma("sp", SR[t * 128:(t + 1) * 128, :], srt[:], [srb], [db["SR"]])
            self.unpin(Pg1)
            self.unpin(Pg2)
            if DBG.get('tstop') == 11:
                continue
            bk, bb = self.bank()
            bkb = bk[:].bitcast(BF16)
            for wh in range(2):
                for dr in range(2):
                    for pr in range(2):
                        i0 = wh * 4 + dr * 2 + pr
                        self.tr(bkb[:, i0 * 128:(i0 + 1) * 128], qi[:, wh, dr, pr * 128:(pr + 1) * 128], self.ident_b[:], [qib, cb], [bb])
            kiT, kiTb = hT_r.get()
            if DBG.get("v") != "noact":
                self.cp("act", QINT[:, t, :, :].rearrange("p a t -> p (a t)"), bkb[:, 0:512], [bb], [tb[t]["qint"]])
            if DBG.get("v") != "nodve":
                self.cp(DBG.get("kieng", "dve"), kiT[:, 0:4, :].rearrange("p a t -> p (a t)"), bkb[:, 512:1024], [bb], [kiTb])
            if DBG.get('tstop') == 12:
                continue
            a1k, a1b = self.bank()
            a2k, a2b = self.bank()
            for half in range(2):
                ak, ab = (a1k, a1b) if half == 0 else (a2k, a2b)
                p0 = half * 64
                for dr in range(2):
                    for pr in range(2):
                        c0 = (dr * 2 + pr) * 128
                        self.mm(ak[:, c0:c0 + 128], kiT[p0:p0 + 64, dr * 2 + pr, :], QINT[p0:p0 + 64, t, dr * 2 + pr, :],
                                True, True, [kiTb, tb[t]["qint"]], [ab])
            aT, aTb = aT_r.get()
            aT5 = aT[:].rearrange("p (d r f) t -> p d r f t", d=2, r=2, f=2)
            for half in range(2):
                ak, ab = (a1k, a1b) if half == 0 else (a2k, a2b)
                for dr in range(2):
                    self.tt("dve", aT5[:, dr, :, half, :], ak[:, dr * 256:(dr + 1) * 256].rearrange("p (r t) -> p r t", r=2),
                            gm[:, dr, :].unsqueeze(1).broadcast_to([128, 2, 128]), ALU.mult, [ab, cb], [aTb])
            if DBG.get('tstop') == 13:
                continue
            ok, ob = self.bank()
            for hd in range(4):
                for dr in range(2):
                    self.mm(ok[:, hd * 64:(hd + 1) * 64], aT[:, dr * 4 + hd, :], VG[:, t, hd * 64:(hd + 1) * 64], dr == 0, dr == 1,
                            [aTb, tb[t]["vg"]], [ob])
            self.cp("act", OL[:, t, :], ok[:, 0:256], [ob], [tb[t]["ol"]])

        if DBG.get("dbgA"):
            OLI = self.outp("OLI", [NTOK, 256])
            QID = self.outp("QID", [NT, 128, 4, 128], BF16)
            for t in range(NT):
                self.dma("sp", OLI[t * 128:(t + 1) * 128, :], OL[:, t, :], [tb[t]["ol"]], [db["OLI"]])
                self.dma("sp", QID[t], QINT[:, t, :, :], [tb[t]["qint"]], [db["QID"]])
        if DBG.get("stop") == "tiles":
            return
        Sst = A("Sst", [128, 2, 2, 128], F32)
        Sbf = A("Sbf", [128, 2, 2, 128], BF16)
        G = A("G", [128, 4], F32)
        sb_ = [[Buf() for _ in range(2)] for _ in range(2)]
        sbb = [[Buf() for _ in range(2)] for _ in range(2)]
        gb = [[Buf() for _ in range(2)] for _ in range(2)]
        um_r = Ring(A, "um", [128, 128], F32, 3)
        qt_r = Ring(A, "qtl", [128, 64], BF16, 4)

        def scan(tiles, final_S_dram, final_D_dram):
            for dr in range(2):
                order = tiles if dr == 0 else tiles[::-1]
                for pr in range(2):
                    S.op("pool", lambda e: e.memset(Sst[:, dr, pr, :], 0.0), [], [sb_[dr][pr]])
                    S.op("pool", lambda e: e.memset(Sbf[:, dr, pr, :], 0.0), [], [sbb[dr][pr]])
                    S.op("pool", lambda e: e.memset(G[:, dr * 2 + pr:dr * 2 + pr + 1], 1.0), [], [gb[dr][pr]])
                for t in order:
                    for ch in ((0, 1) if dr == 0 else (1, 0)):
                        r0 = ch * 64
                        for pr in range(2):
                            ip = dr * 2 + pr
                            qt, qtb = qt_r.get()
                            self.ts("pool", qt[:], QINT[:, t, ip, r0:r0 + 64], G[:, ip:ip + 1], None, ALU.mult, None,
                                    [tb[t]["qint"], gb[dr][pr]], [qtb])
                            self.dma("sp", QTIL[dr, :, pr, t * 128 + r0:t * 128 + r0 + 64], qt[:], [qtb], [db["QTIL"]])
                            ik, ib = self.bank()
                            self.mm(ik[:, 0:128], QINT[:, t, ip, :], Sbf[:, dr, pr, :], True, True, [tb[t]["qint"], sbb[dr][pr]], [ib])
                            self.tt("dve", OL[r0:r0 + 64, t, pr * 128:(pr + 1) * 128], OL[r0:r0 + 64, t, pr * 128:(pr + 1) * 128],
                                    ik[r0:r0 + 64, 0:128], ALU.add, [tb[t]["ol"], ib], [tb[t]["ol"]])
                            uk, ub = self.bank()
                            self.mm(uk[:, 0:128], KOUT[r0:r0 + 64, t, dr, pr * 128:(pr + 1) * 128], VG[r0:r0 + 64, t, pr * 128:(pr + 1) * 128],
                                    True, True, [tb[t]["kout"], tb[t]["vg"]], [ub])
                            um, umb = um_r.get()
                            self.tt("dve", um[:], uk[:, 0:128], bd[:], ALU.mult, [ub, cb], [umb])
                            dcol = DEC[:, t, ip * 2 + ch:ip * 2 + ch + 1]
                            self.stt(Sst[:, dr, pr, :], Sst[:, dr, pr, :], dcol, um[:], ALU.mult, ALU.add,
                                     [sb_[dr][pr], tb[t]["dec"], umb], [sb_[dr][pr]])
                            self.cp("act", Sbf[:, dr, pr, :], Sst[:, dr, pr, :], [sb_[dr][pr]], [sbb[dr][pr]])
                            self.ts("pool", G[:, ip:ip + 1], G[:, ip:ip + 1], dcol, None, ALU.mult, None, [gb[dr][pr], tb[t]["dec"]], [gb[dr][pr]])
            allS = [sb_[a][b] for a in range(2) for b in range(2)]
            self.dma("sp", final_S_dram, Sst[:], allS, [db["SLOC"], db["SCTX"]])
            if final_D_dram is not None:
                allG = [gb[a][b] for a in range(2) for b in range(2)]
                self.dma("sp", final_D_dram.rearrange("p a b -> p (a b)"), G[:], allG, [db["DS"]])

        scan(list(range(NTL)), SLOC, DS)
        scan(list(range(NTL, NT)), SCTX, None)
        for t in range(NT):
            self.dma("sp", OLOC[t * 128:(t + 1) * 128, :], OL[:, t, :], [tb[t]["ol"]], [db["OLOC"]])

    def phase_B(self, l, xin, xin_b, with_ctx):
        nc, S, A = self.nc, self.S, self.alloc
        db, cb = self.dbuf, self.cb
        tiles = list(range(NT if with_ctx else NTL))
        nvar = 2 if with_ctx else 1
        I = lambda name, shape, dt=F32: self.inp("i_" + name, shape, dt)
        MODV = I("MODV", [2, 6 * D])
        QT = I("QT", [128, 4, NTOK], BF16)
        KTE = I("KTE", [128, 20 * 128], BF16)
        VE = I("VE", [20 * 128, 128], BF16)
        UALL = I("UALL", [SEQ, 256], BF16)
        FUC = I("FUC", [CTX, 256], BF16)
        OLOC = I("OLOC", [NTOK, 256])
        QTIL = I("QTIL", [2, 128, 2, NTOK], BF16)
        SR = I("SR", [NTOK, 256], BF16)
        DSg = I("DSg", [128, 4, 2, 2])
        SLOCg = I("SLOCg", [128, 4, 2, 2, 128])
        SCTX = I("SCTX", [128, 2, 2, 128])
        rmask_d = I("rmask", [128, 4, 2])
        amask_d = I("amask", [128, 4, 128])
        TAB = I("TAB", [2, 4, 64, 128, 512], BF16)
        TABC = I("TABC", [2, 1, 2, 128, 256], BF16)
        CB_d = I("CB", [128, 2, 128])
        wf_d = I("wf", [4, 64, 64])
        glag_d = I("glag", [64])
        sink_d = I("sink", [8])
        wout_d = I("wout", [D, D])
        g2_d = I("g2", [D])
        wffi_d = I("wffi", [D, 2 * HID])
        wffo_d = I("wffo", [HID, D])
        XO = self.outp("XO", [NTOK, D])
        FOURT = nc.dram_tensor("FOURT", [2, 128, NTOK], BF16, kind="Internal").ap()
        H2T = nc.dram_tensor("H2T", [NT, 128, 8, 128], BF16, kind="Internal").ap()
        fourb = Buf()
        h2tb = [Buf() for _ in range(NT)]
        xob = [Buf() for _ in range(NT)]
        pmark = self.sb.mark()

        Wi = A("w_ffi", [128, 8, 2 * HID], BF16, high=True)
        Wib = Buf()
        for k in range(8):
            for c0 in range(0, 2 * HID, 1408):
                self.dma("pool", Wi[:, k, c0:c0 + 1408], wffi_d[k * 128:(k + 1) * 128, c0:c0 + 1408], [], [Wib])

        amask = A("amask", [128, 4, 128], BF16)
        self.dma("pool", amask[:], amask_d, [], [cb])
        exs = A("exs", [128, 8], F32)
        self.dma("sp", exs[:], sink_d.partition_broadcast(128), [], [cb])
        self.act(exs[:], exs[:], AF.Exp, [cb], [cb])
        glag = A("glag", [128, 64], F32)
        self.dma("sp", glag[:], glag_d.partition_broadcast(128), [], [cb])
        MODB = A("modbB", [128, nvar, 4, D], F32)
        MODBb = Buf()
        mk = self.sb.mark()
        rows = A("rowsB", [1, 5, D], F32)
        rowsb = Buf()
        for var in range(nvar):
            self.dma("sp", rows[0:1, 0, :], g2_d.rearrange("(o n) -> o n", o=1), [], [rowsb])
            for (ri, c0) in ((1, 2 * D), (2, 4 * D), (3, 3 * D), (4, 5 * D)):
                self.dma("sp", rows[0:1, ri, :], MODV[var:var + 1, c0:c0 + D], [], [rowsb])
            self.stt(rows[0:1, 2, :], rows[0:1, 2, :], 1.0, rows[0:1, 0, :], ALU.add, ALU.mult, [rowsb], [rowsb])
            for (ri, slot) in ((1, 0), (2, 1), (3, 2), (4, 3)):
                for (bk, bb, c0, c1) in self.bcast_row(rows[0:1, ri, :], rowsb, D):
                    self.cp("act", MODB[:, var, slot, c0:c1], bk[:, 0:c1 - c0], [bb], [MODBb])
        Sin = A("Sin", [128, 2, 2, 128], BF16, high=True)
        Sinb = Buf()
        dsg = A("dsg", [128, 4, 2, 2], F32)
        slg = A("slg", [128, 4, 2, 2, 128], F32)
        sct = A("sct", [128, 2, 2, 128], F32)
        rmk = A("rmk", [128, 4, 2], F32)
        sttmp = A("sttmp", [128, 2, 128], F32)
        gb_ = Buf()
        self.dma("sp", dsg[:], DSg, [], [gb_])
        self.dma("sp", slg[:], SLOCg, [], [gb_])
        self.dma("sp", sct[:], SCTX, [], [gb_])
        self.dma("sp", rmk[:], rmask_d, [], [gb_])
        for dr in range(2):
            for r in (range(4) if dr == 0 else range(3, -1, -1)):
                self.tt("dve", sttmp[:], sct[:, dr, :, :], dsg[:, r, dr, :].unsqueeze(2).broadcast_to([128, 2, 128]), ALU.mult, [gb_], [gb_])
                self.tt("dve", sttmp[:], sttmp[:], slg[:, r, dr, :, :], ALU.add, [gb_], [gb_])
                self.tt("dve", sttmp[:], sttmp[:], sct[:, dr, :, :], ALU.subtract, [gb_], [gb_])
                self.stt(sct[:, dr, :, :], sttmp[:], rmk[:, r, dr:dr + 1], sct[:, dr, :, :], ALU.mult, ALU.add, [gb_], [gb_])
        self.cp("dve", Sin[:], sct[:], [gb_], [Sinb])
        S.barrier()
        self.sb.release(mk)

        fmark = self.sb.mark()
        U = A("uall", [128, 64, 256], BF16)
        Ub = Buf()
        for q4 in range(4):
            self.dma("sp", U[:, q4 * 16:(q4 + 1) * 16, :], UALL[q4 * 2048:(q4 + 1) * 2048, :].rearrange("(k p) c -> p k c", p=128), [], [Ub])
        CBs = A("CBs", [128, 2, 128], F32)
        self.dma("sp", CBs[:], CB_d, [], [cb])
        wblk = A("wblk", [128, 2, 128], F32)
        wblkb = Buf()
        S.op("dve", lambda e: e.memset(wblk[:], 0.0), [], [wblkb])
        for g in range(4):
            hf, gi = g // 2, g % 2
            self.dma("sp", wblk[gi * 64:(gi + 1) * 64, hf, gi * 64:(gi + 1) * 64], wf_d[g], [wblkb], [wblkb])
        Mblk = A("Mblk", [128, 2, 2, 128], BF16)
        Mb = Buf()
        for hf in range(2):
            for cs in range(2):
                bk, bb = self.bank()
                self.mm(bk[:, 0:128], CBs[:, cs, :], wblk[:, hf, :], True, True, [cb, wblkb], [bb])
                self.cp("act", Mblk[:, hf, cs, :], bk[:, 0:128], [bb], [Mb])
        tab_r = Ring(A, "tab", [128, 8, 512], BF16, 2)
        abt_r = Ring(A, "abt", [128, 2, 2, 512], BF16, 1)
        ft_r = Ring(A, "ft", [128, 512], BF16, 2)

        def fnet(Usb, Usbb, ntc, tabd, nblk, blk, col0):
            tg = min(8, ntc)
            for b in range(nblk):
                abt, abtb = abt_r.get()
                for cs in range(2):
                    acc = [self.bank(), self.bank()]
                    for g0 in range(0, ntc, tg):
                        tab, tabb = tab_r.get()
                        self.dma("sp", tab[:, 0:tg, 0:blk], tabd[cs, b, g0:g0 + tg].rearrange("k p c -> p k c"), [], [tabb])
                        for tc in range(tg):
                            for hf in range(2):
                                self.mm(acc[hf][0][:, 0:blk], Usb[:, g0 + tc, hf * 128:(hf + 1) * 128], tab[:, tc, 0:blk],
                                        g0 + tc == 0, g0 + tc == ntc - 1, [Usbb, tabb], [acc[hf][1]])
                    for hf in range(2):
                        self.cp("act" if hf == 0 else "dve", abt[:, hf, cs, 0:blk], acc[hf][0][:, 0:blk], [acc[hf][1]], [abtb])
                for hf in range(2):
                    bk, bb = self.bank()
                    for cs in range(2):
                        self.mm(bk[:, 0:blk], Mblk[:, hf, cs, :], abt[:, hf, cs, 0:blk], cs == 0, cs == 1, [Mb, abtb], [bb])
                    ft, ftb = ft_r.get()
                    self.cp("act", ft[:, 0:blk], bk[:, 0:blk], [bb], [ftb])
                    self.dma("sp", FOURT[hf, :, col0 + b * blk:col0 + (b + 1) * blk], ft[:, 0:blk], [ftb], [fourb])

        fnet(U, Ub, 64, TAB, 4, 512, 0)
        if with_ctx:
            Uc = A("uc", [128, 2, 256], BF16)
            Ucb = Buf()
            self.dma("sp", Uc[:], FUC.rearrange("(k p) c -> p k c", p=128), [], [Ucb])
            fnet(Uc, Ucb, 2, TABC, 1, 256, TOK)
        S.barrier()
        self.sb.release(fmark)

        b1mark = self.sb.mark()
        Wo = A("w_out", [128, 8, D], BF16)
        Wob = Buf()
        for k in range(8):
            self.dma("pool", Wo[:, k, :], wout_d[k * 128:(k + 1) * 128, :], [], [Wob])
        KTs = A("kte", [128, 20, 128], BF16)
        self.dma("sp", KTs[:].rearrange("p k t -> p (k t)"), KTE, [], [cb])
        VEa = A("vea", [128, 20, 2, 65], BF16)
        S.op("pool", lambda e: e.memset(VEa[:, :, :, 64:65], 1.0), [], [cb])
        for g in range(2):
            self.dma("sp", VEa[:, :, g, 0:64], VE[:, g * 64:(g + 1) * 64].rearrange("(k p) d -> p k d", p=128), [], [cb])
        xt_r = Ring(A, "xtB", [128, D], F32, 2)
        qt_r = Ring(A, "qtB", [128, 4, 128], BF16, 2)
        pT_r = Ring(A, "pT", [128, 5, 512], BF16, 2)
        den_r = Ring(A, "den", [128, 8], F32, 2)
        att_r = Ring(A, "att", [128, 8, 64], BF16, 2)
        mix_r = Ring(A, "mixT", [128, 8, 128], BF16, 2)
        ol_r = Ring(A, "olB", [128, 256], F32, 2)
        srr = Ring(A, "srB", [128, 256], BF16, 2)
        qtl_r = Ring(A, "qtlB", [128, 2, 2, 128], BF16, 2)
        o_r = Ring(A, "oB", [128, 4, 64], F32, 1)
        sq_r = Ring(A, "sqB", [128, 4, 64], F32, 1)
        sm_r = Ring(A, "smB", [128, 3, 4], F32, 2)
        y_r = Ring(A, "yB", [128, 4, 64], BF16, 2)
        tmp_r = Ring(A, "tmpB", [128, D], F32, 1)
        xn_r = Ring(A, "xnB", [128, D], F32, 2)
        junk_r = Ring(A, "junkB", [128, D], BF16, 1)
        st_r = Ring(A, "stB", [128, 4], F32, 2)
        h2_r = Ring(A, "h2B", [128, D], BF16, 2)
        h2T_r = Ring(A, "h2TB", [128, 8, 128], BF16, 2)
        for t in tiles:
            is_ctx = t >= NTL
            var = 1 if is_ctx else 0
            if is_ctx:
                kts = [18, 19]
                msk = [None, None]
            else:
                kts = [t, t + 1, t + 2, 18, 19]
                msk = [0 if t == 0 else 1, None, 3 if t == NTL - 1 else 2, None, None]
            nk = len(kts)
            qt, qtb = qt_r.get()
            self.dma("sp", qt[:], QT[:, :, t * 128:(t + 1) * 128], [], [qtb])
            xt, xtb = xt_r.get()
            self.dma("sp", xt[:], xin[t * 128:(t + 1) * 128, :], [xin_b], [xtb])
            ol, olb = ol_r.get()
            self.dma("sp", ol[:], OLOC[t * 128:(t + 1) * 128, :], [], [olb])
            sr, srb = srr.get()
            self.dma("sp", sr[:], SR[t * 128:(t + 1) * 128, :], [], [srb])
            mix, mixb = mix_r.get()
            self.dma("sp", mix[:, 4:6, :], FOURT[:, :, t * 128:(t + 1) * 128].rearrange("h p t -> p h t"), [fourb], [mixb])
            pTs = []
            for g in range(2):
                pT, pTb = pT_r.get()
                for i, kt in enumerate(kts):
                    bk, bb = self.bank()
                    self.mm(bk[:, :], KTs[g * 64:(g + 1) * 64, kt, :], qt[g * 64:(g + 1) * 64, :, :].rearrange("p j t -> p (j t)"),
                            True, True, [cb, qtb], [bb])
                    self.act(pT[:, i, :], bk[:, :], AF.Exp, [bb], [pTb], scale=0.125)
                    if msk[i] is not None:
                        self.tt("pool" if g == 0 else "dve", pT[:, i, :].rearrange("p (j t) -> p j t", j=4), pT[:, i, :].rearrange("p (j t) -> p j t", j=4),
                                amask[:, msk[i], :].unsqueeze(1).broadcast_to([128, 4, 128]), ALU.mult, [pTb, cb], [pTb])
                pTs.append((pT, pTb))
            pv = [self.bank(), self.bank()]
            for g in range(2):
                pT, pTb = pTs[g]
                for j in range(4):
                    for i, kt in enumerate(kts):
                        self.mm(pv[g][0][:, j * 65:(j + 1) * 65], pT[:, i, j * 128:(j + 1) * 128], VEa[:, kt, g, :], i == 0, i == nk - 1,
                                [pTb, cb], [pv[g][1]])
            den, denb = den_r.get()
            att, attb = att_r.get()
            for g in range(2):
                pvv = pv[g][0][:, 0:260].rearrange("p (j c) -> p j c", c=65)
                self.tt("dve", den[:, g * 4:(g + 1) * 4].unsqueeze(2), pvv[:, :, 64:65], exs[:, g * 4:(g + 1) * 4].unsqueeze(2), ALU.add,
                        [pv[g][1], cb], [denb])
            S.op("dve", lambda e: e.reciprocal(out=den[:], in_=den[:]), [denb], [denb])
            for g in range(2):
                pvv = pv[g][0][:, 0:260].rearrange("p (j c) -> p j c", c=65)
                self.tt("dve", att[:, g * 4:(g + 1) * 4, :], pvv[:, :, 0:64], den[:, g * 4:(g + 1) * 4].unsqueeze(2).broadcast_to([128, 4, 64]),
                        ALU.mult, [pv[g][1], denb], [attb])
            o, ob = o_r.get()
            of = o[:].rearrange("p h e -> p (h e)")
            if not is_ctx:
                qtl, qtlb = qtl_r.get()
                for dr in range(2):
                    self.dma("sp", qtl[:, dr, :, :], QTIL[dr, :, :, t * 128:(t + 1) * 128], [], [qtlb])
                bk, bb = self.bank()
                for pr in range(2):
                    for dr in range(2):
                        self.mm(bk[:, pr * 128:(pr + 1) * 128], qtl[:, dr, pr, :], Sin[:, dr, pr, :], dr == 0, dr == 1, [qtlb, Sinb], [bb])
                self.tt("dve", of, bk[:, 0:256], ol[:], ALU.add, [bb, olb], [ob])
            else:
                self.cp("dve", of, ol[:], [olb], [ob])
            sq, sqb = sq_r.get()
            self.act(sq[:].rearrange("p h e -> p (h e)"), of, AF.Square, [ob], [sqb])
            sm, smb = sm_r.get()
            S.op("dve", lambda e: e.tensor_reduce(out=sm[:, 0, :], in_=sq[:], axis=AX.X, op=ALU.add), [sqb], [smb])
            self.rstd(sm[:, 2, :], sm[:, 0, :], 1.0 / 64, [smb, self.eps_b], [smb], sm[:, 1, :], smb)
            self.tt("dve", o[:], o[:], sm[:, 2, :].unsqueeze(2).broadcast_to([128, 4, 64]), ALU.mult, [ob, smb], [ob])
            self.tt("pool", o[:], o[:], glag[:].unsqueeze(1).broadcast_to([128, 4, 64]), ALU.mult, [ob, cb], [ob])
            y, yb = y_r.get()
            self.tt("dve", y[:].rearrange("p h e -> p (h e)"), of, sr[:], ALU.mult, [ob, srb], [yb])
            bk, bb = self.bank()
            bkb = bk[:].bitcast(BF16)
            attf = att[:].rearrange("p h e -> p (h e)")
            yf = y[:].rearrange("p h e -> p (h e)")
            for k in range(4):
                self.tr(bkb[:, k * 128:(k + 1) * 128], attf[:, k * 128:(k + 1) * 128], self.ident_b[:], [attb, cb], [bb])
            for k in range(2):
                self.tr(bkb[:, (4 + k) * 128:(5 + k) * 128], yf[:, k * 128:(k + 1) * 128], self.ident_b[:], [yb, cb], [bb])
            self.cp("act", mix[:, 0:4, :].rearrange("p k t -> p (k t)"), bkb[:, 0:512], [bb], [mixb])
            self.cp("act", mix[:, 6:8, :].rearrange("p k t -> p (k t)"), bkb[:, 512:768], [bb], [mixb])
            if DBG.get("mixout"):
                if t == tiles[0]:
                    self.MIXD = self.outp("MIXD", [NT, 128, 8, 128], BF16)
                self.dma("sp", self.MIXD[t], mix[:], [mixb], [db["MIXD"]])
            xn, xnb = xn_r.get()
            tmp, tmpb = tmp_r.get()
            for hfc in range(2):
                bk, bb = self.bank()
                for k in range(8):
                    self.mm(bk[:, :], mix[:, k, :], Wo[:, k, hfc * 512:(hfc + 1) * 512], k == 0, k == 7, [mixb, Wob], [bb])
                self.tt("dve", tmp[:, hfc * 512:(hfc + 1) * 512], bk[:, :], MODB[:, var, 0, hfc * 512:(hfc + 1) * 512], ALU.mult, [bb, MODBb], [tmpb])
            self.tt("pool", xn[:], tmp[:], xt[:], ALU.add, [tmpb, xtb], [xnb])
            self.dma("sp", XO[t * 128:(t + 1) * 128, :], xn[:], [xnb], [xob[t]])
            st, stb = st_r.get()
            junk, junkb = junk_r.get()
            self.act(junk[:], xn[:], AF.Square, [xnb], [junkb, stb], accum_out=st[:, 0:1])
            self.rstd(st[:, 2:3], st[:, 0:1], 1.0 / D, [stb, self.eps_b], [stb], st[:, 1:2], stb)
            tmp, tmpb = tmp_r.get()
            self.stt(tmp[:], xn[:], st[:, 2:3], MODB[:, var, 1, :], ALU.mult, ALU.mult, [xnb, stb, MODBb], [tmpb])
            h2, h2b = h2_r.get()
            self.tt("pool", h2[:], tmp[:], MODB[:, var, 2, :], ALU.add, [tmpb, MODBb], [h2b])
            bk, bb = self.bank()
            bkb = bk[:].bitcast(BF16)
            for k in range(8):
                self.tr(bkb[:, k * 128:(k + 1) * 128], h2[:, k * 128:(k + 1) * 128], self.ident_b[:], [h2b, cb], [bb])
            h2T, h2Tb = h2T_r.get()
            self.cp("act", h2T[:].rearrange("p k t -> p (k t)"), bkb[:, :], [bb], [h2Tb])
            self.dma("sp", H2T[t], h2T[:], [h2Tb], [h2tb[t]])
        S.barrier()
        self.sb.release(b1mark)

        Wf = A("w_ffo", [128, NCH, D], BF16, high=True)
        Wfb = Buf()
        for j in range(NCH):
            self.dma("pool", Wf[:, j, :], wffo_d[j * 128:(j + 1) * 128, :], [], [Wfb])
        hp_r = Ring(A, "hp", [128, 8, 256], BF16, 2)
        xp_r = Ring(A, "xp", [128, 2, D], F32, 2)
        sg_r = Ring(A, "sgF", [128, 256], BF16, 2)
        ac_r = Ring(A, "acF", [128, 256], BF16, 3)
        tm_r = Ring(A, "tmF", [128, D], F32, 1)
        xo_r = Ring(A, "xoF", [128, D], F32, 2)
        accs = self.banks[0:4]
        self.brange = (4, 8)
        for p in range(len(tiles) // 2):
            t0 = 2 * p
            var = 1 if t0 >= NTL else 0
            hp, hpb = hp_r.get()
            xp, xpb = xp_r.get()
            for i in range(2):
                self.dma("sp", hp[:, :, i * 128:(i + 1) * 128], H2T[t0 + i], [h2tb[t0 + i]], [hpb])
                self.dma("sp", xp[:, i, :], XO[(t0 + i) * 128:(t0 + i + 1) * 128, :], [xob[t0 + i]], [xpb])
            for j in range(NCH):
                gk, gbk = self.bank()
                uk, ubk = self.bank()
                for k in range(8):
                    self.mm(gk[:, 0:256], Wi[:, k, j * 128:(j + 1) * 128], hp[:, k, :], k == 0, k == 7, [Wib, hpb], [gbk])
                for k in range(8):
                    self.mm(uk[:, 0:256], Wi[:, k, HID + j * 128:HID + (j + 1) * 128], hp[:, k, :], k == 0, k == 7, [Wib, hpb], [ubk])
                sg, sgb = sg_r.get()
                self.act(sg[:], gk[:, 0:256], AF.Silu, [gbk], [sgb])
                ac, acb = ac_r.get()
                self.tt("dve", ac[:], sg[:], uk[:, 0:256], ALU.mult, [sgb, ubk], [acb])
                for i in range(2):
                    for hfc in range(2):
                        a_, ab_ = accs[i * 2 + hfc]
                        self.mm(a_[:, :], ac[:, i * 128:(i + 1) * 128], Wf[:, j, hfc * 512:(hfc + 1) * 512], j == 0, j == NCH - 1, [acb, Wfb], [ab_])
            for i in range(2):
                tm, tmb = tm_r.get()
                for hfc in range(2):
                    a_, ab_ = accs[i * 2 + hfc]
                    self.tt("dve", tm[:, hfc * 512:(hfc + 1) * 512], a_[:, :], MODB[:, var, 3, hfc * 512:(hfc + 1) * 512], ALU.mult, [ab_, MODBb], [tmb])
                xo, xo_b = xo_r.get()
                self.tt("pool", xo[:], tm[:], xp[:, i, :], ALU.add, [tmb, xpb], [xo_b])
                self.dma("sp", XO[(t0 + i) * 128:(t0 + i + 1) * 128, :], xo[:], [xo_b], [xob[t0 + i], db["XO"]])
        self.brange = (0, 8)
        S.barrier()
        self.sb.release(pmark)

    def A_mod_f(self, l):
        nc, S, A = self.nc, self.S, self.alloc
        db = self.dbuf
        cT_d = self.inp("cT", [128, 8, 2])
        wmod_d = self.inp("wmod%d" % l, [D, 6 * D])
        bmod_d = self.inp("bmod%d" % l, [6 * D])
        MODV = self.dram("MODV%d" % l, [2, 6 * D])
        self.modv[l] = MODV
        mk = self.sb.mark()
        cTs = A("cTs", [128, 8, 2], F32)
        cTb = Buf()
        self.dma("sp", cTs[:], cT_d, [], [cTb])
        scT = A("scT", [128, 8, 64], BF16)
        scb = Buf()
        S.op("dve", lambda e: e.memset(scT[:], 0.0), [], [scb])
        sil = A("sil", [128, 8, 2], F32)
        silb = Buf()
        self.act(sil[:], cTs[:], AF.Exp, [cTb], [silb], scale=-1.0)
        self.ts("dve", sil[:], sil[:], 1.0, None, ALU.add, None, [silb], [silb])
        S.op("dve", lambda e: e.reciprocal(out=sil[:], in_=sil[:]), [silb], [silb])
        self.tt("dve", sil[:], sil[:], cTs[:], ALU.mult, [silb, cTb], [silb])
        self.cp("dve", scT[:, :, 0:1], sil[:, :, 0:1], [silb], [scb])
        self.cp("dve", scT[:, :, 32:33], sil[:, :, 1:2], [silb], [scb])
        wm_ring = Ring(A, "wm", [128, 8, 512], BF16, 2)
        bm_ring = Ring(A, "bm", [64, 512], F32, 2)
        mr_ring = Ring(A, "mr", [64, 512], F32, 2)
        mvb = db["MODV%d" % l]
        for cbk in range(12):
            wm, wmb = wm_ring.get()
            self.dma("pool", wm[:], wmod_d[:, cbk * 512:(cbk + 1) * 512].rearrange("(k p) c -> p k c", p=128), [], [wmb])
            bm, bmb = bm_ring.get()
            self.dma("sp", bm[:], bmod_d[cbk * 512:(cbk + 1) * 512].partition_broadcast(64), [], [bmb])
            bk, bb = self.bank()
            for k in range(8):
                self.mm(bk[0:64, :], scT[:, k, :], wm[:, k, :], k == 0, k == 7, [scb, wmb], [bb])
            mr, mrb = mr_ring.get()
            self.tt("dve", mr[:], bk[0:64, :], bm[:], ALU.add, [bb, bmb], [mrb])
            self.dma("sp", MODV[0:1, cbk * 512:(cbk + 1) * 512], mr[0:1, :], [mrb], [mvb])
            self.dma("sp", MODV[1:2, cbk * 512:(cbk + 1) * 512], mr[32:33, :], [mrb], [mvb])
        S.barrier()
        self.sb.release(mk)

    def phase_A_f(self, l, xin, xin_b):
        nc, S, A = self.nc, self.S, self.alloc
        ntl, nt, ntok = NTLF, NTF, NTOKF
        pmark = self.sb.mark()
        g1_d = self.inp("g1%d" % l, [D])
        win_d = self.inp("win%d" % l, [D, INW])
        qg_d = self.inp("qg%d" % l, [64])
        kg_d = self.inp("kg%d" % l, [64])
        wg_d = self.inp("wg%d" % l, [2, 17, 256])
        rope_d = self.inp("rope", [2, SEQ, 64])
        cm_d = self.inp("cm", [128, 4, 128])
        ci_d = self.inp("ci", [128, 2])
        gm_d = self.inp("gm", [128, 2, 128])
        bd_d = self.inp("bd", [128, 128])
        MODV = self.modv[l]
        L = "%d" % l
        QT = self.dram("QT" + L, [128, 4, ntok], BF16)
        KTP = self.dram("KTP" + L, [128, (ntl + 2) * 128], BF16)
        KTC = self.dram("KTC" + L, [128, CTX], BF16)
        VP = self.dram("VP" + L, [(ntl + 2) * 128, 128], BF16)
        VC = self.dram("VC" + L, [CTX, 128], BF16)
        FUL = self.dram("FUL" + L, [SEQ, 256], BF16)
        FUC = self.dram("FUC" + L, [CTX, 256], BF16)
        OLOC = self.dram("OLOC" + L, [ntok, 256])
        SR = self.dram("SR" + L, [ntok, 256], BF16)
        KOUTd = self.dram("KOUTd" + L, [nt, 128, 2, 256], BF16)
        VGd = self.dram("VGd" + L, [nt, 128, 256], BF16)
        QINTd = self.dram("QINTd" + L, [nt, 128, 4, 128], BF16)
        DECd = self.dram("DECd" + L, [nt, 128, 8])
        self.aout[l] = dict(QT=QT, KTP=KTP, KTC=KTC, VP=VP, VC=VC, FUL=FUL, FUC=FUC, OLOC=OLOC, SR=SR)
        db = self.dbuf
        cb = self.cb
        olb = [Buf() for _ in range(nt)]
        self.aout[l]["olb"] = olb
        scb_ = [Buf() for _ in range(nt)]

        cm = A("cm", [128, 4, 128], F32)
        ci = A("ci", [128, 2], F32)
        gm = A("gm", [128, 2, 128], BF16)
        bd = A("bd", [128, 128], F32)
        self.dma("sp", cm[:], cm_d, [], [cb])
        self.dma("sp", ci[:], ci_d, [], [cb])
        self.dma("pool", gm[:], gm_d, [], [cb])
        self.dma("sp", bd[:], bd_d, [], [cb])
        gqk = A("gqk", [128, 10, 64], F32)
        for hh in range(10):
            src = (qg_d if hh < 8 else kg_d).partition_broadcast(128)
            self.dma("sp", gqk[:, hh, :], src, [], [cb])
        wg = A("wg", [17, 2, 256], F32)
        for dr in range(2):
            self.dma("sp", wg[:, dr, :], wg_d[dr], [], [cb])
        zT = A("zT", [17, 2, 128], F32)
        zTb = Buf()
        S.op("dve", lambda e: e.memset(zT[:], 1.0), [], [zTb])
        zt = A("zeroT", [128, 128], BF16)
        ztb = Buf()
        S.op("dve", lambda e: e.memset(zt[:], 0.0), [], [ztb])
        for pos in (0, ntl + 1):
            self.dma("sp", KTP[:, pos * 128:(pos + 1) * 128], zt[:], [ztb], [db["KTP" + L]])
            self.dma("sp", VP[pos * 128:(pos + 1) * 128, :], zt[:], [ztb], [db["VP" + L]])

        W = A("w_in", [128, 8, INW], BF16)
        Wb = Buf()
        for k in range(8):
            for c0 in (0, 1040):
                self.dma("pool", W[:, k, c0:c0 + 1040], win_d[k * 128:(k + 1) * 128, c0:c0 + 1040], [], [Wb])

        MODB = A("modb", [128, 4, D], F32)
        MODBb = Buf()
        mk = self.sb.mark()
        rows = A("rowsA", [1, 3, D], F32)
        rowsb = Buf()
        for var in range(2):
            self.dma("sp", rows[0:1, 0, :], g1_d.rearrange("(o n) -> o n", o=1), [], [rowsb])
            self.dma("sp", rows[0:1, 1, :], MODV[var:var + 1, D:2 * D], [db["MODV" + L]], [rowsb])
            self.dma("sp", rows[0:1, 2, :], MODV[var:var + 1, 0:D], [db["MODV" + L]], [rowsb])
            self.stt(rows[0:1, 1, :], rows[0:1, 1, :], 1.0, rows[0:1, 0, :], ALU.add, ALU.mult, [rowsb], [rowsb])
            for (ri, slot) in ((1, 2 * var), (2, 2 * var + 1)):
                for (bk, bb, c0, c1) in self.bcast_row(rows[0:1, ri, :], rowsb, D):
                    self.cp("act", MODB[:, slot, c0:c1], bk[:, 0:c1 - c0], [bb], [MODBb])
        S.barrier()
        self.sb.release(mk)

        def mkrings(tag):
            xt_r = Ring(A, "xt" + tag, [128, D], F32, 1)
            junk_r = Ring(A, "junk" + tag, [128, D], BF16, 1)
            st_r = Ring(A, "st" + tag, [128, 8], F32, 1)
            tmp_r = Ring(A, "tmpA" + tag, [128, D], F32, 1)
            h_r = Ring(A, "h" + tag, [128, D], BF16, 1)
            hT_r = Ring(A, "hT" + tag, [128, 8, 128], BF16, 2)
            qk_r = Ring(A, "qk" + tag, [128, 10, 64], F32, 1)
            sq_r = Ring(A, "sq" + tag, [128, 10, 64], F32, 1)
            sm_r = Ring(A, "sm" + tag, [128, 3, 16], F32, 1)
            qn_r = Ring(A, "qn" + tag, [128, 10, 64], F32, 1)
            t1_r = Ring(A, "t1" + tag, [128, 10, 64], F32, 1)
            t2_r = Ring(A, "t2" + tag, [128, 10, 64], F32, 1)
            qr_r = Ring(A, "qr" + tag, [128, 10, 64], BF16, 1)
            qkT_r = Ring(A, "qkT" + tag, [128, 5, 128], BF16, 1)
            rp_r = Ring(A, "rp" + tag, [128, 2, 64], F32, 1)
            vf_r = Ring(A, "vf" + tag, [128, 384], BF16, 1)
            z_r = Ring(A, "z" + tag, [128, 32], F32, 1)
            L_r = Ring(A, "L" + tag, [128, 2, 256], F32, 1)
            E_r = Ring(A, "E" + tag, [128, 3, 512], F32, 1)
            qi_r = Ring(A, "qi" + tag, [128, 2, 2, 256], BF16, 1)
            aT_r = Ring(A, "aT" + tag, [128, 8, 128], BF16, 1)
            sg_r = Ring(A, "sg" + tag, [128, 256], F32, 1)
            sr_r = Ring(A, "srt" + tag, [128, 256], BF16, 1)
            ko_r = Ring(A, "koR" + tag, [128, 2, 256], BF16, 1)
            vg_r = Ring(A, "vgR" + tag, [128, 256], BF16, 1)
            qint_r = Ring(A, "qintR" + tag, [128, 4, 128], BF16, 1)
            ol_r = Ring(A, "olR" + tag, [128, 256], F32, 1)
            dec_r = Ring(A, "decR" + tag, [128, 8], F32, 1)
            g1s_r = Ring(A, "g1s" + tag, [128, 512], F32, 1)
            grs_r = Ring(A, "grs" + tag, [128, 256], F32, 1)
            zT_r = Ring(A, "zTp" + tag, [17, 2, 128], F32, 1)
            for (zz, zzb) in zT_r.items:
                S.op("dve", lambda e: e.memset(zz[:], 1.0), [], [zzb])
            return (xt_r, junk_r, st_r, tmp_r, h_r, hT_r, qk_r, sq_r, sm_r, qn_r, t1_r, t2_r, qr_r, qkT_r, rp_r, vf_r, z_r, L_r, E_r, qi_r, aT_r, sg_r, sr_r, ko_r, vg_r, qint_r, ol_r, dec_r, g1s_r, grs_r, zT_r)

        RR = [mkrings("a"), mkrings("b")]

        def tileA(t):
            par = t % 2
            self.brange = (0, 4) if par == 0 else (4, 8)
            (xt_r, junk_r, st_r, tmp_r, h_r, hT_r, qk_r, sq_r, sm_r, qn_r, t1_r, t2_r, qr_r, qkT_r, rp_r, vf_r, z_r, L_r, E_r, qi_r, aT_r, sg_r, sr_r, ko_r, vg_r, qint_r, ol_r, dec_r, g1s_r, grs_r, zT_r) = RR[par]
            is_ctx = t >= ntl
            var = 1 if is_ctx else 0
            tc = t - ntl
            zT, zTb = zT_r.items[0]
            xt, xtb = xt_r.get()
            self.dma("act", xt[:], xin[t * 128:(t + 1) * 128, :], xin_b(t), [xtb])
            st, stb = st_r.get()
            junk, junkb = junk_r.get()
            self.act(junk[:], xt[:], AF.Square, [xtb], [junkb, stb], accum_out=st[:, 0:1])
            self.rstd(st[:, 2:3], st[:, 0:1], 1.0 / D, [stb, self.eps_b], [stb], st[:, 1:2], stb)
            tmp, tmpb = tmp_r.get()
            self.stt(tmp[:], xt[:], st[:, 2:3], MODB[:, 2 * var, :], ALU.mult, ALU.mult, [xtb, stb, MODBb], [tmpb])
            h, hb = h_r.get()
            self.tt("pool", h[:], tmp[:], MODB[:, 2 * var + 1, :], ALU.add, [tmpb, MODBb], [hb])
            bk, bb = self.bank()
            bkb = bk[:].bitcast(BF16)
            for k in range(8):
                self.tr(bkb[:, k * 128:(k + 1) * 128], h[:, k * 128:(k + 1) * 128], self.ident_b[:], [hb, cb], [bb])
            hT, hTb = hT_r.get()
            self.cp("act", hT[:].rearrange("p k t -> p (k t)"), bkb[:, :], [bb], [hTb])
            def proj(c0, c1):
                bk, bb = self.bank()
                for k in range(8):
                    self.mm(bk[:, 0:c1 - c0], hT[:, k, :], W[:, k, c0:c1], k == 0, k == 7, [hTb, Wb], [bb])
                return bk, bb

            def proj2(c0):
                (b1, bb1), (b2, bb2) = self.bank(), self.bank()
                for k in range(8):
                    self.mm(b1[:, :], hT[:, k, :], W[:, k, c0:c0 + 512], k == 0, k == 7, [hTb, Wb], [bb1])
                    self.mm(b2[:, :], hT[:, k, :], W[:, k, c0 + 512:c0 + 1024], k == 0, k == 7, [hTb, Wb], [bb2])
                return (b1, bb1), (b2, bb2)
            qk, qkb = qk_r.get()
            sq, sqb = sq_r.get()
            qkf = qk[:].rearrange("p h d -> p (h d)")
            sqf = sq[:].rearrange("p h d -> p (h d)")
            (Pq, Pqb), (Pk, Pkb) = proj2(0)
            self.cp("act", qkf[:, 0:512], Pq[:, 0:512], [Pqb], [qkb])
            self.cp("act", qkf[:, 512:640], Pk[:, 0:128], [Pkb], [qkb])
            vf, vfb = vf_r.get()
            self.cp("dve", vf[:], Pk[:, 128:512], [Pkb], [vfb])
            (Pg1, Pg1b), (Pg2, Pg2b) = proj2(1024)
            g1s, g1sb = g1s_r.get()
            self.cp("act", g1s[:], Pg1[:, :], [Pg1b], [g1sb])
            vg, vgb = vg_r.get()
            self.cp("act", vg[:], Pg2[:, 0:256], [Pg2b], [vgb])
            grs, grsb = grs_r.get()
            self.cp("dve", grs[:], Pg2[:, 256:512], [Pg2b], [grsb])
            Pz, Pzb = proj(2048, 2080)
            if is_ctx:
                self.dma("sp", VC[tc * 128:(tc + 1) * 128, :], vf[:, 0:128], [vfb], [db["VC" + L]])
                self.dma("sp", FUC[tc * 128:(tc + 1) * 128, :], vf[:, 128:384], [vfb], [db["FUC" + L]])
            else:
                self.dma("sp", VP[(t + 1) * 128:(t + 2) * 128, :], vf[:, 0:128], [vfb], [db["VP" + L]])
                self.dma("sp", FUL[t * 128:(t + 1) * 128, :], vf[:, 128:384], [vfb], [db["FUL" + L]])
            z, zb_ = z_r.get()
            self.cp("act", z[:], Pz[:, 0:32], [Pzb], [zb_])
            self.act(sqf[:, :], qkf[:, :], AF.Square, [qkb], [sqb])
            sm, smb = sm_r.get()
            self.reduce_x(sm[:, 0, 0:10], sq[:], [sqb], [smb])
            self.rstd(sm[:, 2, 0:10], sm[:, 0, 0:10], 1.0 / 64, [smb, self.eps_b], [smb], sm[:, 1, 0:10], smb)
            qn, qnb = qn_r.get()
            self.tt("dve", qn[:], qk[:], sm[:, 2, 0:10].unsqueeze(2).broadcast_to([128, 10, 64]), ALU.mult, [qkb, smb], [qnb])
            self.tt("pool", qn[:], qn[:], gqk[:], ALU.mult, [qnb, cb], [qnb])
            qr, qrb = qr_r.get()
            if not is_ctx:
                rp, rpb = rp_r.get()
                self.dma("act", rp[:], rope_d[:, t * 128:(t + 1) * 128, :].rearrange("c t d -> t c d"), [], [rpb])
                t1, t1b = t1_r.get()
                t2, t2b = t2_r.get()
                self.tt("dve", t1[:], qn[:], rp[:, 0, :].unsqueeze(1).broadcast_to([128, 10, 64]), ALU.mult, [qnb, rpb], [t1b])
                qn5 = qn[:].rearrange("p h (a s c) -> p (h a) s c", a=2, s=2)
                t25 = t2[:].rearrange("p h (a s c) -> p (h a) s c", a=2, s=2)
                self.cp("pool", t25[:, :, 0, :], qn5[:, :, 1, :], [qnb], [t2b])
                self.cp("pool", t25[:, :, 1, :], qn5[:, :, 0, :], [qnb], [t2b])
                self.tt("pool", t2[:], t2[:], rp[:, 1, :].unsqueeze(1).broadcast_to([128, 10, 64]), ALU.mult, [t2b, rpb], [t2b])
                self.tt("dve", qr[:], t1[:], t2[:], ALU.add, [t1b, t2b], [qrb])
            else:
                self.cp("dve", qr[:], qn[:], [qnb], [qrb])
            bk, bb = self.bank()
            bkb = bk[:].bitcast(BF16)
            qrf = qr[:].rearrange("p h d -> p (h d)")
            for j in range(5):
                self.tr(bkb[:, j * 128:(j + 1) * 128], qrf[:, j * 128:(j + 1) * 128], self.ident_b[:], [qrb, cb], [bb])
            qkT, qkTb = qkT_r.get()
            self.cp("act", qkT[:].rearrange("p j t -> p (j t)"), bkb[:, 0:640], [bb], [qkTb])
            self.dma("sp", QT[:, :, t * 128:(t + 1) * 128], qkT[:, 0:4, :], [qkTb], [db["QT" + L]])
            if is_ctx:
                self.dma("sp", KTC[:, tc * 128:(tc + 1) * 128], qkT[:, 4, :], [qkTb], [db["KTC" + L]])
            else:
                self.dma("sp", KTP[:, (t + 1) * 128:(t + 2) * 128], qkT[:, 4, :], [qkTb], [db["KTP" + L]])
            S.rec.append(("mark",))
            bk, bb = self.bank()
            for dr in range(2):
                self.tr(bk[0:16, dr * 128:(dr + 1) * 128], z[:, dr * 16:(dr + 1) * 16], self.ident_f[:], [zb_, cb], [bb])
            self.cp("dve", zT[0:16, :, :].rearrange("p a t -> p (a t)"), bk[0:16, 0:256], [bb], [zTb])
            bk, bb = self.bank()
            for dr in range(2):
                self.mm(bk[:, dr * 256:(dr + 1) * 256], zT[:, dr, :], wg[:, dr, :], True, True, [zTb, cb], [bb])
            Lt, Lb = L_r.get()
            Lf = Lt[:].rearrange("p a c -> p (a c)")
            self.act(Lf, bk[:, :], AF.Exp, [bb], [Lb], scale=-1.0)
            self.act(Lf, Lf, AF.Ln, [Lb, self.eps_b], [Lb], bias=self.eps_t[:, 1:2])
            c1k, c1b = self.bank()
            c2k, c2b = self.bank()
            self.mm(c1k[:, 0:256], cm[:, 0, :], Lt[:, 0, :], True, True, [cb, Lb], [c1b])
            self.mm(c1k[:, 256:512], cm[:, 2, :], Lt[:, 1, :], True, True, [cb, Lb], [c1b])
            self.mm(c2k[:, 0:256], cm[:, 1, :], Lt[:, 0, :], True, True, [cb, Lb], [c2b])
            self.mm(c2k[:, 256:512], cm[:, 3, :], Lt[:, 1, :], True, True, [cb, Lb], [c2b])
            dk, dkb = self.bank()
            for dr in range(2):
                for pr in range(2):
                    i0 = (dr * 2 + pr) * 2
                    self.mm(dk[:, i0:i0 + 2], Lt[:, dr, pr * 128:(pr + 1) * 128], ci[:, :], True, True, [Lb, cb], [dkb])
            dec, decb = dec_r.get()
            self.act(dec[:], dk[:, 0:8], AF.Exp, [dkb], [decb])
            self.dma("sp", DECd[t], dec[:], [decb], [scb_[t]])
            E, Eb = E_r.get()
            self.act(E[:, 0, :], c1k[:, :], AF.Exp, [c1b, self.eps_b], [Eb], bias=self.eps_t[:, 2:3])
            self.act(E[:, 1, :], c1k[:, :], AF.Exp, [c1b], [Eb], scale=-1.0)
            self.act(E[:, 2, :], c2k[:, :], AF.Exp, [c2b], [Eb])
            qi, qib = qi_r.get()
            gq_b = g1s[:, 0:256].unsqueeze(1).broadcast_to([128, 2, 256])
            gk_b = g1s[:, 256:512].unsqueeze(1).broadcast_to([128, 2, 256])
            Pg1b = g1sb
            self.tt("dve", qi[:, 0, :, :], gq_b, E[:, 0, :].rearrange("p (a c) -> p a c", a=2), ALU.mult, [Pg1b, Eb], [qib])
            self.tt("dve", qi[:, 1, :, :], gk_b, E[:, 1, :].rearrange("p (a c) -> p a c", a=2), ALU.mult, [Pg1b, Eb], [qib])
            ko, kob = ko_r.get()
            self.tt("dve", ko[:], gk_b, E[:, 2, :].rearrange("p (a c) -> p a c", a=2), ALU.mult, [Pg1b, Eb], [kob])
            self.dma("sp", KOUTd[t], ko[:], [kob], [scb_[t]])
            self.dma("sp", VGd[t], vg[:], [vgb], [scb_[t]])
            sg, sgb = sg_r.get()
            self.act(sg[:], grs[:], AF.Exp, [grsb], [sgb], scale=-1.0)
            self.act(sg[:], sg[:], AF.Ln, [sgb, self.eps_b], [sgb], bias=self.eps_t[:, 1:2])
            self.act(sg[:], sg[:], AF.Exp, [sgb], [sgb], scale=-1.0)
            srt, srb = sr_r.get()
            self.tt("dve", srt[:], sg[:], grs[:], ALU.mult, [sgb, grsb], [srb])
            self.dma("sp", SR[t * 128:(t + 1) * 128, :], srt[:], [srb], [db["SR" + L]])
            bk, bb = self.bank()
            bkb = bk[:].bitcast(BF16)
            for wh in range(2):
                for dr in range(2):
                    for pr in range(2):
                        i0 = wh * 4 + dr * 2 + pr
                        self.tr(bkb[:, i0 * 128:(i0 + 1) * 128], qi[:, wh, dr, pr * 128:(pr + 1) * 128], self.ident_b[:], [qib, cb], [bb])
            kiT, kiTb = hT_r.get()
            qint, qintb = qint_r.get()
            self.cp("act", qint[:].rearrange("p a t -> p (a t)"), bkb[:, 0:512], [bb], [qintb])
            self.cp("act", kiT[:, 0:4, :].rearrange("p a t -> p (a t)"), bkb[:, 512:1024], [bb], [kiTb])
            self.dma("sp", QINTd[t], qint[:], [qintb], [scb_[t]])
            a1k, a1b = self.bank()
            a2k, a2b = self.bank()
            for half in range(2):
                ak, ab = (a1k, a1b) if half == 0 else (a2k, a2b)
                p0 = half * 64
                for dr in range(2):
                    for pr in range(2):
                        c0 = (dr * 2 + pr) * 128
                        self.mm(ak[:, c0:c0 + 128], kiT[p0:p0 + 64, dr * 2 + pr, :], qint[p0:p0 + 64, dr * 2 + pr, :],
                                True, True, [kiTb, qintb], [ab])
            aT, aTb = aT_r.get()
            aT5 = aT[:].rearrange("p (d r f) t -> p d r f t", d=2, r=2, f=2)
            for half in range(2):
                ak, ab = (a1k, a1b) if half == 0 else (a2k, a2b)
                for dr in range(2):
                    self.tt("dve", aT5[:, dr, :, half, :], ak[:, dr * 256:(dr + 1) * 256].rearrange("p (r t) -> p r t", r=2),
                            gm[:, dr, :].unsqueeze(1).broadcast_to([128, 2, 128]), ALU.mult, [ab, cb], [aTb])
            ok, ob = self.bank()
            for hd in range(4):
                for dr in range(2):
                    self.mm(ok[:, hd * 64:(hd + 1) * 64], aT[:, dr * 4 + hd, :], vg[:, hd * 64:(hd + 1) * 64], dr == 0, dr == 1,
                            [aTb, vgb], [ob])
            olt, oltb = ol_r.get()
            self.cp("act", olt[:], ok[:, 0:256], [ob], [oltb])
            self.dma("sp", OLOC[t * 128:(t + 1) * 128, :], olt[:], [oltb], [olb[t]])

        self.pipeline(tileA, list(range(nt)))
        self.brange = (0, 8)
        Sst = A("Sst", [128, 2, 2, 128], F32)
        Sbf = A("Sbf", [128, 2, 2, 128], BF16)
        S0 = A("S0", [128, 2, 2, 128], F32)
        sb_ = [[Buf() for _ in range(2)] for _ in range(2)]
        sbb = [[Buf() for _ in range(2)] for _ in range(2)]
        s0b = Buf()
        um_r = Ring(A, "um", [128, 2, 128], F32, 6)
        sko_r = Ring(A, "sko", [128, 256], BF16, 6)
        svg_r = Ring(A, "svg", [128, 256], BF16, 6)
        sqn_r = Ring(A, "sqn", [128, 2, 128], BF16, 6)
        sdc_r = Ring(A, "sdc", [128, 4], F32, 6)
        sol_r = Ring(A, "sol", [128, 256], F32, 6)

        def scan(tiles, init):
            for dr in range(2):
                for pr in range(2):
                    if init is None:
                        S.op("pool", lambda e: e.memset(Sst[:, dr, pr, :], 0.0), [], [sb_[dr][pr]])
                    else:
                        self.cp("pool", Sst[:, dr, pr, :], S0[:, dr, pr, :], [s0b], [sb_[dr][pr]])
                    self.cp("act", Sbf[:, dr, pr, :], Sst[:, dr, pr, :], [sb_[dr][pr]], [sbb[dr][pr]])
            for idx in range(len(tiles)):
                for dr in range(2):
                    t = tiles[idx] if dr == 0 else tiles[len(tiles) - 1 - idx]
                    ko, kob = sko_r.get()
                    self.dma("act", ko[:], KOUTd[t, :, dr, :], [scb_[t]], [kob])
                    vg, vgb = svg_r.get()
                    self.dma("act", vg[:], VGd[t], [scb_[t]], [vgb])
                    qn, qnb = sqn_r.get()
                    self.dma("act", qn[:], QINTd[t, :, dr * 2:(dr + 1) * 2, :], [scb_[t]], [qnb])
                    dc, dcb = sdc_r.get()
                    self.dma("act", dc[:], DECd[t, :, dr * 4:(dr + 1) * 4], [scb_[t]], [dcb])
                    ol, ol_b = sol_r.get()
                    self.dma("act", ol[:], OLOC[t * 128:(t + 1) * 128, :], [olb[t]], [ol_b])
                    chs = list(range(128 // GCH))
                    for ch in (chs if dr == 0 else chs[::-1]):
                        r0 = ch * GCH
                        ik, ib = self.bank()
                        for pr in range(2):
                            self.mm(ik[:, pr * 128:(pr + 1) * 128], qn[:, pr, :], Sbf[:, dr, pr, :], True, True, [qnb, sbb[dr][pr]], [ib])
                        self.tt("dve", ol[r0:r0 + GCH, :], ol[r0:r0 + GCH, :], ik[r0:r0 + GCH, 0:256], ALU.add, [ol_b, ib], [ol_b])
                        uk, ub = self.bank()
                        for pr in range(2):
                            self.mm(uk[:, pr * 128:(pr + 1) * 128], ko[r0:r0 + GCH, pr * 128:(pr + 1) * 128], vg[r0:r0 + GCH, pr * 128:(pr + 1) * 128],
                                    True, True, [kob, vgb], [ub])
                        um, umb = um_r.get()
                        self.tt("dve", um[:], uk[:, 0:256].rearrange("p (a e) -> p a e", a=2), bd[:].unsqueeze(1).broadcast_to([128, 2, 128]), ALU.mult, [ub, cb], [umb])
                        for pr in range(2):
                            dcol = dc[:, pr * 2 + ch:pr * 2 + ch + 1]
                            self.stt(Sst[:, dr, pr, :], Sst[:, dr, pr, :], dcol, um[:, pr, :], ALU.mult, ALU.add,
                                     [sb_[dr][pr], dcb, umb], [sb_[dr][pr]])
                            self.cp("act", Sbf[:, dr, pr, :], Sst[:, dr, pr, :], [sb_[dr][pr]], [sbb[dr][pr]])
                    self.dma("sp", OLOC[t * 128:(t + 1) * 128, :], ol[:], [ol_b], [olb[t]])

        scan(list(range(ntl, nt)), None)
        allS = [sb_[a][b] for a in range(2) for b in range(2)]
        self.cp("dve", S0[:], Sst[:], allS, [s0b])
        scan(list(range(ntl)), S0)
        S.barrier()
        self.sb.release(pmark)

    def own_gather(self, l, X1, xob0):
        nc = self.nc
        ao = self.aout[l]
        db = self.dbuf
        L = "%d" % l
        pid = nc.sync.partition_id()
        tok0 = (pid % 4) * TOK
        X1o = self.dram("X1own", [TOK, D])
        QTo = self.dram("QTown", [128, 4, TOK], BF16)
        KTPo = self.dram("KTPown", [128, (NTL + 2) * 128], BF16)
        VPo = self.dram("VPown", [(NTL + 2) * 128, 128], BF16)
        OLo = self.dram("OLown", [TOK, 256])
        SRo = self.dram("SRown", [TOK, 256], BF16)
        self.dma("sp", X1o, X1[bass.ds(tok0, TOK), :], list(xob0), [db["X1own"]])
        self.dma("sp", QTo, ao["QT"][:, :, bass.ds(tok0, TOK)], [db["QT" + L]], [db["QTown"]])
        self.dma("sp", KTPo, ao["KTP"][:, bass.ds(tok0, (NTL + 2) * 128)], [db["KTP" + L]], [db["KTPown"]])
        self.dma("sp", VPo, ao["VP"][bass.ds(tok0, (NTL + 2) * 128), :], [db["VP" + L]], [db["VPown"]])
        self.dma("sp", OLo, ao["OLOC"][bass.ds(tok0, TOK), :], list(ao["olb"]), [db["OLown"]])
        self.dma("sp", SRo, ao["SR"][bass.ds(tok0, TOK), :], [db["SR" + L]], [db["SRown"]])
        own = dict(ao)
        own.update(QT=QTo, KTP=KTPo, VP=VPo, OLOC=OLo, SR=SRo, olb=[db["OLown"]] * NTL)
        self.aout["own"] = own
        self.ownbufs = dict(QT=db["QTown"], KTP=db["KTPown"], VP=db["VPown"], SR=db["SRown"])
        return X1o, db["X1own"]

    def phase_B_f(self, l, mode, xin, xin_b):
        nc, S, A = self.nc, self.S, self.alloc
        db, cb = self.dbuf, self.cb
        own = mode == "own"
        ao = self.aout["own"] if own else self.aout[l]
        L = "%d" % l
        with_ctx = not own
        ntl = NTL if own else NTLF
        ntiles = ntl + (NTC if with_ctx else 0)
        tiles = list(range(ntiles))
        nvar = 2 if with_ctx else 1
        gtok = lambda t: slice(t * 128, (t + 1) * 128)
        kpad = lambda t: slice(t * 128, (t + 3) * 128)
        ltok = lambda t: slice(t * 128, (t + 1) * 128)
        MODV = self.modv[l]
        QT, KTP, KTC, VP, VC, FUL, FUC, OLOC, SR = [ao[k] for k in ("QT", "KTP", "KTC", "VP", "VC", "FUL", "FUC", "OLOC", "SR")]
        olb = ao["olb"]
        rb = (lambda k: self.ownbufs[k]) if own else (lambda k: db[k + L])
        amask_d = self.inp("amask%d" % (1 if own else 0), [128, 4, 128])
        TAB = self.inp("TABOWN", [2, 4 * 8, 128, 8 * 512], BF16) if own else self.inp("TAB", [2, 16 * 8, 128, 8 * 512], BF16)
        TABC = self.inp("TABC", [2, 1, 128, 2 * 256], BF16)
        CB_d = self.inp("CB", [128, 2, 128])
        wf_d = self.inp("wf" + L, [4, 64, 64])
        glag_d = self.inp("glag" + L, [64])
        sink_d = self.inp("sink" + L, [8])
        wout_d = self.inp("wout" + L, [D, D])
        g2_d = self.inp("g2" + L, [D])
        wffi_d = self.inp("wffi" + L, [D, 2 * HID])
        wffo_d = self.inp("wffo" + L, [HID, D])
        if own:
            XO = self.outp("XOUT", [TOK, D])
        else:
            XO = self.dram("X1", [NTOKF, D])
        FOURT = self.dram("FOURT" + L, [2, 128, ntiles * 128], BF16)
        H2T = self.dram("H2T" + L, [ntiles, 128, 8, 128], BF16)
        fourb = db["FOURT" + L]
        h2tb = [Buf() for _ in range(ntiles)]
        xob = [Buf() for _ in range(ntiles)]
        self.xob = xob
        pmark = self.sb.mark()

        Wi = A("w_ffi", [128, 8, 2 * HID], BF16, high=True)
        Wib = Buf()
        for k in range(8):
            for c0 in range(0, 2 * HID, 1408):
                self.dma("pool", Wi[:, k, c0:c0 + 1408], wffi_d[k * 128:(k + 1) * 128, c0:c0 + 1408], [], [Wib])

        amask = A("amask", [128, 4, 128], BF16)
        self.dma("pool", amask[:], amask_d, [], [cb])
        exs = A("exs", [128, 8], F32)
        self.dma("sp", exs[:], sink_d.partition_broadcast(128), [], [cb])
        self.act(exs[:], exs[:], AF.Exp, [cb], [cb])
        glag = A("glag", [128, 64], F32)
        self.dma("sp", glag[:], glag_d.partition_broadcast(128), [], [cb])
        MODB = A("modbB", [128, nvar, 4, D], F32)
        MODBb = Buf()
        mk = self.sb.mark()
        rows = A("rowsB", [1, 5, D], F32)
        rowsb = Buf()
        for var in range(nvar):
            self.dma("sp", rows[0:1, 0, :], g2_d.rearrange("(o n) -> o n", o=1), [], [rowsb])
            for (ri, c0) in ((1, 2 * D), (2, 4 * D), (3, 3 * D), (4, 5 * D)):
                self.dma("sp", rows[0:1, ri, :], MODV[var:var + 1, c0:c0 + D], [db["MODV" + L]], [rowsb])
            self.stt(rows[0:1, 2, :], rows[0:1, 2, :], 1.0, rows[0:1, 0, :], ALU.add, ALU.mult, [rowsb], [rowsb])
            for (ri, slot) in ((1, 0), (2, 1), (3, 2), (4, 3)):
                for (bk, bb, c0, c1) in self.bcast_row(rows[0:1, ri, :], rowsb, D):
                    self.cp("act", MODB[:, var, slot, c0:c1], bk[:, 0:c1 - c0], [bb], [MODBb])
        S.barrier()
        self.sb.release(mk)

        fmark = self.sb.mark()
        U = A("uall", [128, 64, 256], BF16)
        Ub = Buf()
        for q4 in range(4):
            self.dma("sp", U[:, q4 * 16:(q4 + 1) * 16, :], FUL[q4 * 2048:(q4 + 1) * 2048, :].rearrange("(k p) c -> p k c", p=128), [db["FUL" + L]], [Ub])
        CBs = A("CBs", [128, 2, 128], F32)
        self.dma("sp", CBs[:], CB_d, [], [cb])
        wblk = A("wblk", [128, 2, 128], F32)
        wblkb = Buf()
        S.op("dve", lambda e: e.memset(wblk[:], 0.0), [], [wblkb])
        for g in range(4):
            hf, gi = g // 2, g % 2
            self.dma("sp", wblk[gi * 64:(gi + 1) * 64, hf, gi * 64:(gi + 1) * 64], wf_d[g], [wblkb], [wblkb])
        Mblk = A("Mblk", [128, 2, 2, 128], BF16)
        Mb = Buf()
        for hf in range(2):
            for cs in range(2):
                bk, bb = self.bank()
                self.mm(bk[:, 0:128], CBs[:, cs, :], wblk[:, hf, :], True, True, [cb, wblkb], [bb])
                self.cp("act", Mblk[:, hf, cs, :], bk[:, 0:128], [bb], [Mb])
        tab_r = Ring(A, "tab", [128, 8, 512], BF16, 4)
        abt_r = Ring(A, "abt", [128, 2, 2, 512], BF16, 2)
        ft_r = Ring(A, "ft", [128, 512], BF16, 2)
        tabq = [0]

        def fnet(Usb, Usbb, ntc, tabsrc, nblk, blk, col0):
            tg = min(8, ntc)
            for b in range(nblk):
                abt, abtb = abt_r.get()
                acc = [[self.bank(), self.bank()], [self.bank(), self.bank()]]
                for g0 in range(0, ntc, tg):
                    tabs = []
                    for cs in range(2):
                        tab, tabb = tab_r.get()
                        tabq[0] += 1
                        self.dma("sp" if tabq[0] % 2 else "act", tab[:].rearrange("p k c -> p (k c)")[:, 0:tg * blk], tabsrc(cs, b, g0, tg), [], [tabb])
                        tabs.append((tab, tabb))
                    for tc in range(tg):
                        for hf in range(2):
                            for cs in range(2):
                                tab, tabb = tabs[cs]
                                self.mm(acc[hf][cs][0][:, 0:blk], Usb[:, g0 + tc, hf * 128:(hf + 1) * 128],
                                        tab[:].rearrange("p k c -> p (k c)")[:, tc * blk:(tc + 1) * blk],
                                        g0 + tc == 0, g0 + tc == ntc - 1, [Usbb, tabb], [acc[hf][cs][1]])
                for hf in range(2):
                    for cs in range(2):
                        self.cp("act" if cs == 0 else "dve", abt[:, hf, cs, 0:blk], acc[hf][cs][0][:, 0:blk], [acc[hf][cs][1]], [abtb])
                for hf in range(2):
                    bk, bb = self.bank()
                    for cs in range(2):
                        self.mm(bk[:, 0:blk], Mblk[:, hf, cs, :], abt[:, hf, cs, 0:blk], cs == 0, cs == 1, [Mb, abtb], [bb])
                    ft, ftb = ft_r.get()
                    self.cp("act", ft[:, 0:blk], bk[:, 0:blk], [bb], [ftb])
                    self.dma("sp", FOURT[hf, :, col0 + b * blk:col0 + (b + 1) * blk], ft[:, 0:blk], [ftb], [fourb])

        if own:
            fnet(U, Ub, 64, lambda cs, b, g0, tg: TAB[cs, b * 8 + g0 // 8], 4, 512, 0)
        else:
            fnet(U, Ub, 64, lambda cs, b, g0, tg: TAB[cs, b * 8 + g0 // 8], 16, 512, 0)
            Uc = A("uc", [128, 2, 256], BF16)
            Ucb = Buf()
            self.dma("sp", Uc[:], FUC.rearrange("(k p) c -> p k c", p=128), [db["FUC" + L]], [Ucb])
            fnet(Uc, Ucb, 2, lambda cs, b, g0, tg: TABC[cs, b], 1, 256, SEQ)
        S.barrier()
        self.sb.release(fmark)

        b1mark = self.sb.mark()
        Wo = A("w_out", [128, 8, D], BF16)
        Wob = Buf()
        for k in range(8):
            self.dma("pool", Wo[:, k, :], wout_d[k * 128:(k + 1) * 128, :], [], [Wob])
        KC = A("kc", [128, 2, 128], BF16)
        self.dma("sp", KC[:].rearrange("p k t -> p (k t)"), KTC, [db["KTC" + L]], [cb])
        VCa = A("vca", [128, 2, 2, 65], BF16)
        S.op("pool", lambda e: e.memset(VCa[:, :, :, 64:65], 1.0), [], [cb])
        for g in range(2):
            self.dma("sp", VCa[:, :, g, 0:64], VC[:, g * 64:(g + 1) * 64].rearrange("(k p) d -> p k d", p=128), [db["VC" + L]], [cb])
        def mkringsB(tag):
            kt_r = Ring(A, "ktR" + tag, [128, 3, 128], BF16, 1)
            ve_r = Ring(A, "veR" + tag, [128, 3, 2, 65], BF16, 1)
            xt_r = Ring(A, "xtB" + tag, [128, D], F32, 1)
            qt_r = Ring(A, "qtB" + tag, [128, 4, 128], BF16, 1)
            pT_r = Ring(A, "pT" + tag, [128, 5, 512], BF16, 1)
            den_r = Ring(A, "den" + tag, [128, 8], F32, 1)
            att_r = Ring(A, "att" + tag, [128, 8, 64], BF16, 1)
            mix_r = Ring(A, "mixT" + tag, [128, 8, 128], BF16, 1)
            srr = Ring(A, "srB" + tag, [128, 256], BF16, 1)
            o_r = Ring(A, "oB" + tag, [128, 4, 64], F32, 1)
            sq_r = Ring(A, "sqB" + tag, [128, 4, 64], F32, 1)
            sm_r = Ring(A, "smB" + tag, [128, 3, 4], F32, 1)
            y_r = Ring(A, "yB" + tag, [128, 4, 64], BF16, 1)
            tmp_r = Ring(A, "tmpB" + tag, [128, D], F32, 1)
            xn_r = Ring(A, "xnB" + tag, [128, 8], F32, 1)
            junk_r = Ring(A, "junkB" + tag, [128, 8], BF16, 1)
            st_r = Ring(A, "stB" + tag, [128, 4], F32, 1)
            h2_r = Ring(A, "h2B" + tag, [128, D], BF16, 1)
            h2T_r = Ring(A, "h2TB" + tag, [128, 8, 128], BF16, 1)
            for (ve, veb) in ve_r.items:
                S.op("pool", lambda e: e.memset(ve[:, :, :, 64:65], 1.0), [], [veb])
            return (kt_r, ve_r, xt_r, qt_r, pT_r, den_r, att_r, mix_r, srr, o_r, sq_r, sm_r, y_r, tmp_r, xn_r, junk_r, st_r, h2_r, h2T_r)

        RB = [mkringsB("a"), mkringsB("b")]

        def tileB(t):
            par = t % 2
            self.brange = (0, 4) if par == 0 else (4, 8)
            (kt_r, ve_r, xt_r, qt_r, pT_r, den_r, att_r, mix_r, srr, o_r, sq_r, sm_r, y_r, tmp_r, xn_r, junk_r, st_r, h2_r, h2T_r) = RB[par]
            is_ctx = t >= ntl
            var = 1 if is_ctx else 0
            qt, qtb = qt_r.get()
            self.dma("act", qt[:], QT[:, :, gtok(t)], [rb("QT")], [qtb])
            xt, xtb = xt_r.get()
            self.dma("act", xt[:], xin[gtok(t), :], xin_b(t), [xtb])
            o, ob = o_r.get()
            of = o[:].rearrange("p h e -> p (h e)")
            self.dma("act", of, OLOC[gtok(t), :], [olb[t]], [ob])
            sr, srb = srr.get()
            self.dma("act", sr[:], SR[gtok(t), :], [rb("SR")], [srb])
            mix, mixb = mix_r.get()
            self.dma("act", mix[:, 4:6, :], FOURT[:, :, ltok(t)].rearrange("h p t -> p h t"), [fourb], [mixb])
            keys = []
            if not is_ctx:
                kt, ktb = kt_r.get()
                self.dma("act", kt[:].rearrange("p k t -> p (k t)"), KTP[:, kpad(t)], [rb("KTP")], [ktb])
                ve, veb = ve_r.get()
                for g in range(2):
                    self.dma("act", ve[:, :, g, 0:64], VP[kpad(t), g * 64:(g + 1) * 64].rearrange("(k p) d -> p k d", p=128), [rb("VP")], [veb])
                m0 = 0 if t == 0 else 1
                m2 = 3 if t == ntl - 1 else 2
                keys += [(kt, ve, 0, ktb, veb, m0), (kt, ve, 1, ktb, veb, None), (kt, ve, 2, ktb, veb, m2)]
            keys += [(KC, VCa, 0, cb, cb, None), (KC, VCa, 1, cb, cb, None)]
            nk = len(keys)
            pv = [self.bank(pin=True), self.bank(pin=True)]
            for g in range(2):
                pT, pTb = pT_r.get()
                for i, (kten, vten, ki, kbuf, vbuf, mi) in enumerate(keys):
                    bk, bb = self.bank()
                    self.mm(bk[:, :], kten[g * 64:(g + 1) * 64, ki, :], qt[g * 64:(g + 1) * 64, :, :].rearrange("p j t -> p (j t)"),
                            True, True, [kbuf, qtb], [bb])
                    self.act(pT[:, i, :], bk[:, :], AF.Exp, [bb], [pTb], scale=0.125)
                    if mi is not None:
                        self.tt("pool" if g == 0 else "dve", pT[:, i, :].rearrange("p (j t) -> p j t", j=4), pT[:, i, :].rearrange("p (j t) -> p j t", j=4),
                                amask[:, mi, :].unsqueeze(1).broadcast_to([128, 4, 128]), ALU.mult, [pTb, cb], [pTb])
                for j in range(4):
                    for i, (kten, vten, ki, kbuf, vbuf, mi) in enumerate(keys):
                        self.mm(pv[g][0][:, j * 65:(j + 1) * 65], pT[:, i, j * 128:(j + 1) * 128], vten[:, ki, g, :], i == 0, i == nk - 1,
                                [pTb, vbuf], [pv[g][1]])
            S.rec.append(("mark",))
            den, denb = den_r.get()
            att, attb = att_r.get()
            for g in range(2):
                pvv = pv[g][0][:, 0:260].rearrange("p (j c) -> p j c", c=65)
                self.tt("dve", den[:, g * 4:(g + 1) * 4].unsqueeze(2), pvv[:, :, 64:65], exs[:, g * 4:(g + 1) * 4].unsqueeze(2), ALU.add,
                        [pv[g][1], cb], [denb])
            self.recip(den[:], den[:], [denb], [denb])
            for g in range(2):
                pvv = pv[g][0][:, 0:260].rearrange("p (j c) -> p j c", c=65)
                self.tt("dve", att[:, g * 4:(g + 1) * 4, :], pvv[:, :, 0:64], den[:, g * 4:(g + 1) * 4].unsqueeze(2).broadcast_to([128, 4, 64]),
                        ALU.mult, [pv[g][1], denb], [attb])
            self.unpin(pv[0][0])
            self.unpin(pv[1][0])
            sq, sqb = sq_r.get()
            self.tt("pool", sq[:].rearrange("p h e -> p (h e)"), of, of, ALU.mult, [ob], [sqb])
            sm, smb = sm_r.get()
            self.reduce_x(sm[:, 0, :], sq[:], [sqb], [smb])
            self.rstd(sm[:, 2, :], sm[:, 0, :], 1.0 / 64, [smb, self.eps_b], [smb], sm[:, 1, :], smb)
            self.tt("dve", o[:], o[:], sm[:, 2, :].unsqueeze(2).broadcast_to([128, 4, 64]), ALU.mult, [ob, smb], [ob])
            self.tt("pool", o[:], o[:], glag[:].unsqueeze(1).broadcast_to([128, 4, 64]), ALU.mult, [ob, cb], [ob])
            y, yb = y_r.get()
            self.tt("dve", y[:].rearrange("p h e -> p (h e)"), of, sr[:], ALU.mult, [ob, srb], [yb])
            bk, bb = self.bank()
            bkb = bk[:].bitcast(BF16)
            attf = att[:].rearrange("p h e -> p (h e)")
            yf = y[:].rearrange("p h e -> p (h e)")
            for k in range(4):
                self.tr(bkb[:, k * 128:(k + 1) * 128], attf[:, k * 128:(k + 1) * 128], self.ident_b[:], [attb, cb], [bb])
            for k in range(2):
                self.tr(bkb[:, (4 + k) * 128:(5 + k) * 128], yf[:, k * 128:(k + 1) * 128], self.ident_b[:], [yb, cb], [bb])
            self.cp("dve", mix[:, 0:4, :].rearrange("p k t -> p (k t)"), bkb[:, 0:512], [bb], [mixb])
            self.cp("dve", mix[:, 6:8, :].rearrange("p k t -> p (k t)"), bkb[:, 512:768], [bb], [mixb])
            xn, xnb = xt, xtb
            tmp, tmpb = tmp_r.get()
            for hfc in range(2):
                bk, bb = self.bank()
                for k in range(8):
                    self.mm(bk[:, :], mix[:, k, :], Wo[:, k, hfc * 512:(hfc + 1) * 512], k == 0, k == 7, [mixb, Wob], [bb])
                self.tt("dve", tmp[:, hfc * 512:(hfc + 1) * 512], bk[:, :], MODB[:, var, 0, hfc * 512:(hfc + 1) * 512], ALU.mult, [bb, MODBb], [tmpb])
            self.tt("pool", xn[:], tmp[:], xt[:], ALU.add, [tmpb, xtb], [xnb])
            self.dma("sp", XO[ltok(t), :], xn[:], [xnb], [xob[t]])
            st, stb = st_r.get()
            h2, h2b = h2_r.get()
            self.act(h2[:], xn[:], AF.Square, [xnb], [h2b, stb], accum_out=st[:, 0:1])
            self.rstd(st[:, 2:3], st[:, 0:1], 1.0 / D, [stb, self.eps_b], [stb], st[:, 1:2], stb)
            tmp, tmpb = tmp_r.get()
            self.stt(tmp[:], xn[:], st[:, 2:3], MODB[:, var, 1, :], ALU.mult, ALU.mult, [xnb, stb, MODBb], [tmpb])
            self.tt("pool", h2[:], tmp[:], MODB[:, var, 2, :], ALU.add, [tmpb, MODBb], [h2b])
            bk, bb = self.bank()
            bkb = bk[:].bitcast(BF16)
            for k in range(8):
                self.tr(bkb[:, k * 128:(k + 1) * 128], h2[:, k * 128:(k + 1) * 128], self.ident_b[:], [h2b, cb], [bb])
            h2T, h2Tb = h2T_r.get()
            self.cp("act", h2T[:].rearrange("p k t -> p (k t)"), bkb[:, :], [bb], [h2Tb])
            self.dma("sp", H2T[t], h2T[:], [h2Tb], [h2tb[t]])
        self.pipeline(tileB, tiles)
        self.brange = (0, 8)
        S.barrier()
        self.sb.release(b1mark)

        Wf = A("w_ffo", [128, NCH, D], BF16, high=True)
        Wfb = Buf()
        for j in range(NCH):
            self.dma("pool", Wf[:, j, :], wffo_d[j * 128:(j + 1) * 128, :], [], [Wfb])
        hp_r = Ring(A, "hp", [128, 8, 256], BF16, 2)
        xp_r = Ring(A, "xp", [128, 2, D], F32, 2)
        sg_r = Ring(A, "sgF", [128, 256], BF16, 2)
        ac_r = Ring(A, "acF", [128, 256], BF16, 3)
        tm_r = Ring(A, "tmF", [128, D], F32, 1)
        xo_r = Ring(A, "xoF", [128, D], F32, 2)
        accs = self.banks[0:4]
        self.brange = (4, 8)
        for p in range(ntiles // 2):
            t0 = 2 * p
            var = 1 if t0 >= ntl else 0
            hp, hpb = hp_r.get()
            xp, xpb = xp_r.get()
            for i in range(2):
                self.dma("act", hp[:, :, i * 128:(i + 1) * 128], H2T[t0 + i], [h2tb[t0 + i]], [hpb])
                self.dma("act", xp[:, i, :], XO[ltok(t0 + i), :], [xob[t0 + i]], [xpb])
            def gu(j):
                gk, gbk = self.bank()
                uk, ubk = self.bank()
                for k in range(8):
                    self.mm(gk[:, 0:256], Wi[:, k, j * 128:(j + 1) * 128], hp[:, k, :], k == 0, k == 7, [Wib, hpb], [gbk])
                for k in range(8):
                    self.mm(uk[:, 0:256], Wi[:, k, HID + j * 128:HID + (j + 1) * 128], hp[:, k, :], k == 0, k == 7, [Wib, hpb], [ubk])
                sg, sgb = sg_r.get()
                self.act(sg[:], gk[:, 0:256], AF.Silu, [gbk], [sgb])
                ac, acb = ac_r.get()
                self.tt("dve", ac[:], sg[:], uk[:, 0:256], ALU.mult, [sgb, ubk], [acb])
                return ac, acb

            nxt = gu(0)
            for j in range(NCH):
                ac, acb = nxt
                if j + 1 < NCH:
                    nxt = gu(j + 1)
                for i in range(2):
                    for hfc in range(2):
                        a_, ab_ = accs[i * 2 + hfc]
                        self.mm(a_[:, :], ac[:, i * 128:(i + 1) * 128], Wf[:, j, hfc * 512:(hfc + 1) * 512], j == 0, j == NCH - 1, [acb, Wfb], [ab_])
            for i in range(2):
                tm, tmb = tm_r.get()
                for hfc in range(2):
                    a_, ab_ = accs[i * 2 + hfc]
                    self.tt("dve", tm[:, hfc * 512:(hfc + 1) * 512], a_[:, :], MODB[:, var, 3, hfc * 512:(hfc + 1) * 512], ALU.mult, [ab_, MODBb], [tmb])
                xo, xo_b = xo_r.get()
                self.tt("pool", xo[:], tm[:], xp[:, i, :], ALU.add, [tmb, xpb], [xo_b])
                self.dma("sp", XO[ltok(t0 + i), :], xo[:], [xo_b], [xob[t0 + i]])
        self.brange = (0, 8)
        S.barrier()
        self.sb.release(pmark)
        return XO, xob

    def finish_all(self):
        S = self.S
        deps = [(k, v) for k, v in S.cnt.items() if v > 0]
        S._wait("sp", deps)


def _perm_win(w_in_l):
    idx = []
    for j in range(4):
        idx += list(range(j * 64, (j + 1) * 64)) + list(range((j + 4) * 64, (j + 5) * 64))
    idx += list(range(512, INW))
    return np.ascontiguousarray(w_in_l[:, idx])


def _phaseA_inputs(inp, l, core, consts):
    b, r = core // 4, core % 4
    cT = np.stack([inp["c"][b].reshape(8, 128).T, inp["c_ctx"].reshape(8, 128).T], axis=-1).astype(np.float32)
    wg = np.stack([np.concatenate([inp["gla_w_gate_f"][l], inp["gla_b_gate_f"][l][None]], 0),
                   np.concatenate([inp["gla_w_gate_b"][l], inp["gla_b_gate_b"][l][None]], 0)], 0).astype(np.float32)
    m = {
        "cT": np.ascontiguousarray(cT),
        "wmod_A": inp["w_mod"][l], "bmod_A": inp["b_mod"][l], "g1_A": inp["g_norm1"][l],
        "win_A": _perm_win(inp["w_in"][l]), "qg_A": inp["q_norm_g"][l], "kg_A": inp["k_norm_g"][l],
        "wg_A": np.ascontiguousarray(wg), "rope": _rope_tables(r * TOK, TOK),
        "cm": consts["cm"], "ci": consts["ci"], "gm": consts["gm"], "bd": consts["bd"], "ident_f": consts["ident_f"],
    }
    return m


_TAB_CACHE = {}


def _dft_cached(s0, ns, n):
    key = (s0, ns, n)
    if key not in _TAB_CACHE:
        _TAB_CACHE[key] = _dft_tables(s0, ns, n)
    return _TAB_CACHE[key]


def _cb_const():
    c = np.arange(64)
    ang = 2.0 * np.pi * ((c[:, None] * c[None, :]) % 64) / 64.0
    cc = np.cos(ang) / 8.0
    sc = -np.sin(ang) / 8.0
    cb = np.zeros((128, 2, 128), np.float32)
    for gi in range(2):
        cb[gi * 64:(gi + 1) * 64, 0, gi * 64:(gi + 1) * 64] = cc
        cb[gi * 64:(gi + 1) * 64, 1, gi * 64:(gi + 1) * 64] = sc
    return cb


def _phaseB_inputs(inp, l, core, outs, consts):
    b, r = core // 4, core % 4
    o = outs[core]
    grp = [outs[4 * b + rr] for rr in range(4)]
    m = {}
    m["i_MODV"] = o["MODV_A"]
    m["i_QT"] = o["QT"]
    kfull = np.concatenate([g["KT"][:, :TOK] for g in grp], axis=1)
    vfull = np.concatenate([g["V"][:TOK] for g in grp], axis=0)
    kz = np.zeros((128, 128), kfull.dtype)
    vz = np.zeros((128, 128), vfull.dtype)
    lo, hi = r * TOK, (r + 1) * TOK
    m["i_KTE"] = np.ascontiguousarray(np.concatenate(
        [kfull[:, lo - 128:lo] if r > 0 else kz, kfull[:, lo:hi], kfull[:, hi:hi + 128] if r < 3 else kz, o["KT"][:, TOK:]], axis=1))
    m["i_VE"] = np.ascontiguousarray(np.concatenate(
        [vfull[lo - 128:lo] if r > 0 else vz, vfull[lo:hi], vfull[hi:hi + 128] if r < 3 else vz, o["V"][TOK:]], axis=0))
    m["i_UALL"] = np.ascontiguousarray(np.concatenate([g["FU"][:TOK] for g in grp], axis=0))
    m["i_FUC"] = np.ascontiguousarray(o["FU"][TOK:])
    m["i_OLOC"] = o["OLOC"]
    m["i_QTIL"] = o["QTIL"]
    m["i_SR"] = o["SR"]
    m["i_DSg"] = np.ascontiguousarray(np.stack([g["DS"] for g in grp], axis=1))
    m["i_SLOCg"] = np.ascontiguousarray(np.stack([g["SLOC"] for g in grp], axis=1))
    m["i_SCTX"] = o["SCTX"]
    rm = np.zeros((128, 4, 2), np.float32)
    for rr in range(4):
        rm[:, rr, 0] = 1.0 if rr < r else 0.0
        rm[:, rr, 1] = 1.0 if rr > r else 0.0
    m["i_rmask"] = rm
    z = np.zeros((128, 128), np.float32)
    m["i_amask"] = np.ascontiguousarray(np.stack([consts["mp"] if r > 0 else z, consts["mp"], consts["mn"], consts["mn"] if r < 3 else z], axis=1))
    m["i_TAB"] = _dft_cached(r * TOK, TOK, SEQ)
    m["i_TABC"] = _dft_cached(0, CTX, CTX)
    m["i_CB"] = consts["cbc"]
    m["i_wf"] = inp["w_fourier"][l]
    m["i_glag"] = inp["gla_norm_g"][l]
    m["i_sink"] = inp["attn_sink"][l]
    m["i_wout"] = inp["w_out"][l]
    m["i_g2"] = inp["g_norm2"][l]
    m["i_wffi"] = inp["w_ffn_in"][l]
    m["i_wffo"] = inp["w_ffn_out"][l]
    m["ident_f"] = consts["ident_f"]
    return m


def build_stage(stage):
    P = Prog(stage)
    P.setup_common()
    xin = P.inp("xin", [NTOK, D])
    if stage == 0:
        P.phase_A(0, xin, P.dbuf["xin"], True)
    elif stage == 1:
        P.A_mod()
        P.phase_B(0, xin, P.dbuf["xin"], True)
        P.phase_A(1, P.dout["XO"], P.dbuf["XO"], False)
    else:
        P.phase_B(1, xin, P.dbuf["xin"], False)
    P.finish_all()
    return P


_PROGS = {}


def build_fused():
    P = Prog("f")
    P.setup_common()
    xin = P.inp("xin", [NTOKF, D])
    xb = P.dbuf["xin"]
    P.A_mod_f(0)
    P.A_mod_f(1)
    P.phase_A_f(0, xin, lambda t: [xb])
    X1, xob0 = P.phase_B_f(0, "all", xin, lambda t: [xb])
    if DBG.get("stop_after") == "B0":
        P.finish_all()
        return P
    P.phase_A_f(1, X1, lambda t: [xob0[t]])
    X1o, x1ob = P.own_gather(1, X1, xob0)
    P.phase_B_f(1, "own", X1o, lambda t: [x1ob])
    P.finish_all()
    return P


def _fused_inputs(inp, core, consts, shared):
    b, r = core // 4, core % 4
    m = {}
    m["xin"] = np.ascontiguousarray(np.concatenate([inp["x"][b], inp["ctx"][b]], axis=0))
    m["cT"] = np.ascontiguousarray(np.stack([inp["c"][b].reshape(8, 128).T, inp["c_ctx"].reshape(8, 128).T], axis=-1).astype(np.float32))
    for l in range(DEPTH):
        L = "%d" % l
        m["wmod" + L] = inp["w_mod"][l]
        m["bmod" + L] = inp["b_mod"][l]
        m["g1" + L] = inp["g_norm1"][l]
        m["win" + L] = shared["win"][l]
        m["qg" + L] = inp["q_norm_g"][l]
        m["kg" + L] = inp["k_norm_g"][l]
        m["wg" + L] = shared["wg"][l]
        m["wf" + L] = inp["w_fourier"][l]
        m["glag" + L] = inp["gla_norm_g"][l]
        m["sink" + L] = inp["attn_sink"][l]
        m["wout" + L] = inp["w_out"][l]
        m["g2" + L] = inp["g_norm2"][l]
        m["wffi" + L] = inp["w_ffn_in"][l]
        m["wffo" + L] = inp["w_ffn_out"][l]
    m["rope"] = shared["rope"]
    for k in ("cm", "ci", "gm", "bd", "ident_f"):
        m[k] = consts[k]
    z = np.zeros((128, 128), np.float32)
    m["amask0"] = np.ascontiguousarray(np.stack([z, consts["mp"], consts["mn"], z], axis=1))
    m["amask1"] = np.ascontiguousarray(np.stack([consts["mp"] if r > 0 else z, consts["mp"], consts["mn"], consts["mn"] if r < 3 else z], axis=1))
    m["TAB"] = shared["TAB"]
    m["TABOWN"] = np.ascontiguousarray(shared["TAB"][:, r * 32:(r + 1) * 32])
    m["TABC"] = shared["TABC"]
    m["CB"] = consts["cbc"]
    return m


def _shared(inp):
    sh = {}
    sh["win"] = [_perm_win(inp["w_in"][l]) for l in range(DEPTH)]
    sh["wg"] = [np.ascontiguousarray(np.stack([np.concatenate([inp["gla_w_gate_f"][l], inp["gla_b_gate_f"][l][None]], 0),
                                               np.concatenate([inp["gla_w_gate_b"][l], inp["gla_b_gate_b"][l][None]], 0)], 0).astype(np.float32))
                for l in range(DEPTH)]
    sh["rope"] = _rope_tables(0, SEQ)
    tab = _dft_cached(0, SEQ, SEQ).reshape(2, 16, 8, 8, 128, 512).transpose(0, 1, 2, 4, 3, 5)
    sh["TAB"] = np.ascontiguousarray(tab).reshape(2, 16 * 8, 128, 8 * 512)
    tabc = _dft_cached(0, CTX, CTX).reshape(2, 1, 2, 128, 256).transpose(0, 1, 3, 2, 4)
    sh["TABC"] = np.ascontiguousarray(tabc).reshape(2, 1, 128, 2 * 256)
    return sh


def kernel(**inputs):
    inp = {k: np.asarray(v) for k, v in inputs.items()}
    consts = _host_consts()
    sh = _shared(inp)
    if "f" not in _PROGS:
        _PROGS["f"] = build_fused()
    P = _PROGS["f"]
    maps = [_fused_inputs(inp, core, consts, sh) for core in range(8)]
    res = run_bass_kernel_spmd(P.nc, maps, core_ids=list(range(8)))
    out = np.zeros((NB, SEQ, D), np.float32)
    for core in range(8):
        b, r = core // 4, core % 4
        out[b, r * TOK:(r + 1) * TOK] = np.asarray(res.results[core]["XOUT"])
    return out


DBG_OUT = {}
```

```python
from contextlib import ExitStack
import numpy as np
import ml_dtypes
import concourse.bass as bass
import concourse.mybir as mybir
from concourse.bass_utils import run_bass_kernel_spmd

F32 = mybir.dt.float32
BF16 = mybir.dt.bfloat16
AF = mybir.ActivationFunctionType
ALU = mybir.AluOpType
AX = mybir.AxisListType
NPBF = ml_dtypes.bfloat16

D = 1024
NB = 2
SEQ = 8192
DEPTH = 2
CTX = 256
TOK = 2048
NTL = TOK // 128
NTC = CTX // 128
NT = NTL + NTC
NTOK = NT * 128
NTLF = SEQ // 128
NTF = NTLF + NTC
NTOKF = NTF * 128
HID = 2816
NCH = HID // 128
INW = 2080
EPS = 1e-6
GRID_W = 64
GCH = 128
DBG = {}


class Buf:
    __slots__ = ("w", "r", "excl")

    def __init__(self, excl=False):
        self.w = []
        self.r = []
        self.excl = excl


class Sched:
    def __init__(self, nc, n_dma_sems=10):
        self.nc = nc
        self.eng = {"pe": nc.tensor, "dve": nc.vector, "act": nc.scalar, "pool": nc.gpsimd, "sp": nc.sync}
        self.sems = {}
        self.cnt = {}
        self.seen = {e: {} for e in self.eng}
        for e in self.eng:
            self.sems[e] = nc.alloc_semaphore("s_" + e)
            self.cnt[e] = 0
        self.dsem = {}
        self.dpos = {}
        for q in ("sp", "act", "pool"):
            self.dsem[q] = []
            for i in range(n_dma_sems):
                k = "d_%s_%d" % (q, i)
                self.sems[k] = nc.alloc_semaphore(k)
                self.cnt[k] = 0
                self.dsem[q].append(k)
            self.dpos[q] = 0
        self.all_out = []
        self.rec = None

    def _wait(self, e, deps):
        best = {}
        for (k, v) in deps:
            if v > best.get(k, 0):
                best[k] = v
        for k, v in best.items():
            if e == "pe" and k == "pe":
                continue
            if self.seen[e].get(k, 0) < v:
                self.eng[e].wait_ge(self.sems[k], v)
                self.seen[e][k] = v

    @staticmethod
    def _deps(reads, writes):
        deps = []
        for b in reads:
            deps += b.w
            if b.excl:
                deps += b.r
        for b in writes:
            deps += b.w
            deps += b.r
        return deps

    @staticmethod
    def _commit(tick, reads, writes):
        for b in reads:
            b.r.append(tick)
            if len(b.r) > 64:
                best = {}
                for (k, v) in b.r:
                    if v > best.get(k, 0):
                        best[k] = v
                b.r = list(best.items())
        for b in writes:
            b.w = [tick]
            b.r = []

    def op(self, e, fn, reads=(), writes=()):
        if self.rec is not None:
            self.rec.append(("op", e, fn, tuple(reads), tuple(writes)))
            return None
        self._wait(e, self._deps(reads, writes))
        inst = fn(self.eng[e])
        self.cnt[e] += 1
        inst.then_inc(self.sems[e], 1)
        tick = (e, self.cnt[e])
        self._commit(tick, reads, writes)
        return tick

    def play(self, lists):
        assert self.rec is None
        n = max(len(l) for l in lists)
        for i in range(n):
            for l in lists:
                if i < len(l):
                    it = l[i]
                    if it[0] == "mark":
                        continue
                    if it[0] == "op":
                        self.op(it[1], it[2], it[3], it[4])
                    else:
                        self.dma(it[1], it[2], it[3], it[4], it[5], **it[6])

    def dma(self, q, out, in_, reads=(), writes=(), **kw):
        if self.rec is not None:
            self.rec.append(("dma", q, out, in_, tuple(reads), tuple(writes), kw))
            return None
        k = self.dsem[q][self.dpos[q] % len(self.dsem[q])]
        self.dpos[q] += 1
        deps = []
        for b in reads:
            deps += b.w
        for b in writes:
            deps += [w for w in b.w if not w[0].startswith("d_")]
            deps += b.r
        if self.cnt[k] > 0:
            deps.append((k, self.cnt[k]))
        self._wait(q, deps)
        inst = self.eng[q].dma_start(out=out, in_=in_, **kw)
        self.cnt[k] += 16
        inst.then_inc(self.sems[k], 16)
        tick = (k, self.cnt[k])
        for b in reads:
            b.r.append(tick)
        for b in writes:
            best = {}
            for (kk, v) in b.w + [tick]:
                if kk.startswith("d_") and v > best.get(kk, 0):
                    best[kk] = v
            b.w = list(best.items())
            b.r = []
        return tick

    def barrier(self):
        deps = [(k, v) for k, v in self.cnt.items() if v > 0]
        for e in self.eng:
            self._wait(e, [d for d in deps if d[0] != e])

    def finish(self, bufs):
        deps = []
        for b in bufs:
            deps += b.w
        self._wait("sp", deps)


class SBAlloc:
    LO0 = 16512
    HI0 = 229344

    def __init__(self, nc):
        self.nc = nc
        self.lo = self.LO0
        self.hi = self.HI0
        self.n = 0

    @staticmethod
    def _size(shape, dt):
        n = 1
        for d in shape[1:]:
            n *= d
        b = n * (2 if dt == BF16 else 4)
        return (b + 31) // 32 * 32

    def alloc(self, name, shape, dt, high=False):
        sz = self._size(shape, dt)
        if high:
            self.hi -= sz
            off = self.hi
        else:
            off = self.lo
            self.lo += sz
        assert self.lo <= self.hi, "SBUF overflow allocating %s: lo=%d hi=%d" % (name, self.lo, self.hi)
        self.n += 1
        return self.nc.alloc_sbuf_tensor_at("sb%d_%s" % (self.n, name), list(shape), dt, offset=off)

    def mark(self):
        return (self.lo, self.hi)

    def release(self, m):
        self.lo, self.hi = m


class Ring:
    def __init__(self, alloc, name, shape, dtype, n=2):
        self.items = [(alloc("%s_%d" % (name, i), shape, dtype), Buf()) for i in range(n)]
        self.pos = 0

    def get(self):
        it = self.items[self.pos % len(self.items)]
        self.pos += 1
        return it


def _host_consts():
    c = {}
    c["ident_f"] = np.eye(128, dtype=np.float32)
    s = np.arange(128)[:, None]
    t = np.arange(128)[None, :]
    same = (s // GCH) == (t // GCH)
    cm = np.zeros((128, 4, 128), np.float32)
    cm[:, 0, :] = (same & (s <= t)) * (-1.0 / 16)
    cm[:, 1, :] = (same & (s > t)) * (-1.0 / 16)
    cm[:, 2, :] = (same & (s >= t)) * (-1.0 / 16)
    cm[:, 3, :] = (same & (s < t)) * (-1.0 / 16)
    c["cm"] = cm
    ci = np.zeros((128, 2), np.float32)
    ci[:GCH, 0] = -1.0 / 16
    ci[GCH:, 1] = -1.0 / 16
    c["ci"] = ci
    gm = np.zeros((128, 2, 128), np.float32)
    gm[:, 0, :] = same & (s <= t)
    gm[:, 1, :] = same & (s >= t)
    c["gm"] = gm
    c["bd"] = ((s // 64) == (t // 64)).astype(np.float32)
    c["mp"] = (s >= t).astype(np.float32)
    c["mn"] = (s <= t).astype(np.float32)
    c["cbc"] = _cb_const()
    return c


def _rope_tables(pos0, n):
    pos = np.arange(pos0, pos0 + n)
    row = (pos // GRID_W).astype(np.float32)
    col = (pos % GRID_W).astype(np.float32)
    inv = (10000.0 ** (-np.arange(0, 32, 2, dtype=np.float32) / 32.0)).astype(np.float32)
    ar = row[:, None] * inv[None, :]
    ac = col[:, None] * inv[None, :]
    cr, sr, cc, sc = np.cos(ar), np.sin(ar), np.cos(ac), np.sin(ac)
    cos64 = np.concatenate([cr, cr, cc, cc], axis=1).astype(np.float32)
    sin64 = np.concatenate([-sr, sr, -sc, sc], axis=1).astype(np.float32)
    return np.stack([cos64, sin64], axis=0)


def _dft_tables(s0, ns, n):
    t = np.arange(n, dtype=np.int64)[:, None]
    s = (s0 + np.arange(ns, dtype=np.int64))[None, :]
    ang = (2.0 * np.pi / n) * ((t * s) % n).astype(np.float64)
    out = []
    for f in (np.cos, np.sin):
        m = (f(ang) / np.sqrt(n)).astype(np.float32)
        blk = min(512, ns)
        m = m.reshape(n // 128, 128, ns // blk, blk).transpose(2, 0, 1, 3)
        out.append(m.astype(NPBF))
    return np.stack(out, axis=0)


class Prog:
    def __init__(self, stage):
        self.stage = stage
        self.nc = bass.Bass("TRN2", target_bir_lowering=False)
        self.S = Sched(self.nc)
        self.din = {}
        self.dout = {}
        self.dbuf = {}
        nc = self.nc
        self.sb = SBAlloc(nc)
        self.alloc = self.sb.alloc
        self.banks = [(nc.alloc_psum_tensor("ps%d" % i, [128, 512], F32), Buf(excl=True)) for i in range(8)]
        self.bpos = 0
        self.brange = (0, 8)
        self.pinned = set()
        self.mod_done = False
        self.modv = {}
        self.aout = {}

    def dram(self, name, shape, dt=F32):
        t = self.nc.dram_tensor(name, list(shape), dt, kind="Internal").ap()
        self.dbuf[name] = Buf()
        return t

    def inp(self, name, shape, dt=F32):
        if name in self.din:
            return self.din[name]
        t = self.nc.dram_tensor(name, list(shape), dt, kind="ExternalInput").ap()
        self.din[name] = t
        self.dbuf[name] = Buf()
        return t

    def outp(self, name, shape, dt=F32):
        t = self.nc.dram_tensor(name, list(shape), dt, kind="ExternalOutput").ap()
        self.dout[name] = t
        self.dbuf[name] = Buf()
        return t

    def bank(self, pin=False):
        lo, hi = self.brange
        while True:
            idx = lo + self.bpos % (hi - lo)
            self.bpos += 1
            if idx not in self.pinned:
                break
        if pin:
            self.pinned.add(idx)
        return self.banks[idx]

    def unpin(self, bk):
        for i, (t, b) in enumerate(self.banks):
            if t is bk:
                self.pinned.discard(i)

    def mm(self, out, lhsT, rhs, start, stop, reads, writes):
        return self.S.op("pe", lambda e: e.matmul(out, lhsT, rhs, start=start, stop=stop), reads, writes)

    def tr(self, out, in_, ident, reads, writes):
        return self.S.op("pe", lambda e: e.transpose(out, in_, ident), reads, writes)

    def act(self, out, in_, func, reads, writes, **kw):
        return self.S.op("act", lambda e: e.activation(out=out, in_=in_, func=func, **kw), reads, writes)

    def tt(self, eng, out, in0, in1, op, reads, writes):
        return self.S.op(eng, lambda e: e.tensor_tensor(out=out, in0=in0, in1=in1, op=op), reads, writes)

    def ts(self, eng, out, in0, s1, s2, op0, op1, reads, writes):
        if op1 is None:
            return self.S.op(eng, lambda e: e.tensor_scalar(out=out, in0=in0, scalar1=s1, scalar2=None, op0=op0), reads, writes)
        return self.S.op(eng, lambda e: e.tensor_scalar(out=out, in0=in0, scalar1=s1, scalar2=s2, op0=op0, op1=op1), reads, writes)

    def stt(self, out, in0, scalar, in1, op0, op1, reads, writes):
        return self.S.op("dve", lambda e: e.scalar_tensor_tensor(out=out, in0=in0, scalar=scalar, in1=in1, op0=op0, op1=op1), reads, writes)

    def cp(self, eng, out, in_, reads, writes):
        if eng == "act":
            return self.S.op("act", lambda e: e.copy(out=out, in_=in_), reads, writes)
        return self.S.op(eng, lambda e: e.tensor_copy(out=out, in_=in_), reads, writes)

    def dma(self, q, out, in_, reads, writes, **kw):
        return self.S.dma(q, out, in_, reads, writes, **kw)

    def pipeline(self, tile_fn, tiles):
        S = self.S
        prev = None
        for t in tiles:
            S.rec = []
            tile_fn(t)
            L = S.rec
            S.rec = None
            h = [i for i, it in enumerate(L) if it[0] == "mark"]
            h = h[0] if h else len(L) // 2
            if prev is None:
                S.play([L[:h]])
            else:
                S.play([prev, L[:h]])
            prev = L[h:]
        if prev:
            S.play([prev])

    def recip(self, out, in_, reads, writes):
        return self.S.op("dve", lambda e: e.reciprocal(out=out, in_=in_), reads, writes)

    def reduce_x(self, out, in_, reads, writes):
        return self.S.op("dve", lambda e: e.tensor_reduce(out=out, in_=in_, axis=AX.X, op=ALU.add), reads, writes)

    def rstd(self, out, ss, scale, reads, writes, tmp, tmpb):
        self.act(tmp, ss, AF.Ln, reads, [tmpb], scale=scale, bias=self.eps_t[:, 0:1])
        self.act(out, tmp, AF.Exp, [tmpb], writes, scale=-0.5)

    def setup_common(self):
        nc, S = self.nc, self.S
        A = self.alloc
        self.eps_t = A("eps_t", [128, 4], F32)
        self.eps_b = Buf()
        S.op("dve", lambda e: e.memset(self.eps_t[:, 0:1], EPS), [], [self.eps_b])
        S.op("dve", lambda e: e.memset(self.eps_t[:, 1:2], 1.0), [], [self.eps_b])
        S.op("dve", lambda e: e.memset(self.eps_t[:, 2:3], float(np.log(0.125))), [], [self.eps_b])
        S.op("dve", lambda e: e.memset(self.eps_t[:, 3:4], 0.0), [], [self.eps_b])
        self.ident_f = A("ident_f", [128, 128], F32)
        self.ident_b = A("ident_b", [128, 128], BF16)
        self.ones_r = A("ones_r", [1, 128], F32)
        self.cb = Buf()
        d = self.inp("ident_f", [128, 128])
        self.dma("sp", self.ident_f[:], d, [], [self.cb])
        self.dma("pool", self.ident_b[:], d, [], [self.cb])
        S.op("dve", lambda e: e.memset(self.ones_r[:], 1.0), [], [self.cb])

    def bcast_row(self, row_ap, rowb, n):
        outs = []
        for c0 in range(0, n, 512):
            c1 = min(n, c0 + 512)
            bk, bb = self.bank()
            self.mm(bk[:, 0:c1 - c0], self.ones_r[0:1, :], row_ap[0:1, c0:c1], True, True, [self.cb, rowb], [bb])
            outs.append((bk, bb, c0, c1))
        return outs

    def A_mod(self):
        nc, S, A = self.nc, self.S, self.alloc
        sfx = "_A"
        db = self.dbuf
        cT_d = self.inp("cT", [128, 8, 2])
        wmod_d = self.inp("wmod" + sfx, [D, 6 * D])
        bmod_d = self.inp("bmod" + sfx, [6 * D])
        MODV = self.outp("MODV" + sfx, [2, 6 * D])
        mk = self.sb.mark()
        cTs = A("cTs", [128, 8, 2], F32)
        cTb = Buf()
        self.dma("sp", cTs[:], cT_d, [], [cTb])
        scT = A("scT", [128, 8, 64], BF16)
        scb = Buf()
        S.op("dve", lambda e: e.memset(scT[:], 0.0), [], [scb])
        sil = A("sil", [128, 8, 2], F32)
        silb = Buf()
        self.act(sil[:], cTs[:], AF.Exp, [cTb], [silb], scale=-1.0)
        self.ts("dve", sil[:], sil[:], 1.0, None, ALU.add, None, [silb], [silb])
        S.op("dve", lambda e: e.reciprocal(out=sil[:], in_=sil[:]), [silb], [silb])
        self.tt("dve", sil[:], sil[:], cTs[:], ALU.mult, [silb, cTb], [silb])
        self.cp("dve", scT[:, :, 0:1], sil[:, :, 0:1], [silb], [scb])
        self.cp("dve", scT[:, :, 32:33], sil[:, :, 1:2], [silb], [scb])
        wm_ring = Ring(A, "wm", [128, 8, 512], BF16, 2)
        bm_ring = Ring(A, "bm", [64, 512], F32, 2)
        mr_ring = Ring(A, "mr", [64, 512], F32, 2)
        for cbk in range(12):
            wm, wmb = wm_ring.get()
            self.dma("pool", wm[:], wmod_d[:, cbk * 512:(cbk + 1) * 512].rearrange("(k p) c -> p k c", p=128), [], [wmb])
            bm, bmb = bm_ring.get()
            self.dma("sp", bm[:], bmod_d[cbk * 512:(cbk + 1) * 512].partition_broadcast(64), [], [bmb])
            bk, bb = self.bank()
            for k in range(8):
                self.mm(bk[0:64, :], scT[:, k, :], wm[:, k, :], k == 0, k == 7, [scb, wmb], [bb])
            mr, mrb = mr_ring.get()
            self.tt("dve", mr[:], bk[0:64, :], bm[:], ALU.add, [bb, bmb], [mrb])
            self.dma("sp", MODV[0:1, cbk * 512:(cbk + 1) * 512], mr[0:1, :], [mrb], [db["MODV" + sfx]])
            self.dma("sp", MODV[1:2, cbk * 512:(cbk + 1) * 512], mr[32:33, :], [mrb], [db["MODV" + sfx]])

        S.barrier()
        self.sb.release(mk)
        self.mod_done = True

    def phase_A(self, l, xin, xin_b, first_stage):
        nc, S, A = self.nc, self.S, self.alloc
        sfx = "_A"
        g1_d = self.inp("g1" + sfx, [D])
        win_d = self.inp("win" + sfx, [D, INW])
        qg_d = self.inp("qg" + sfx, [64])
        kg_d = self.inp("kg" + sfx, [64])
        wg_d = self.inp("wg" + sfx, [2, 17, 256])
        rope_d = self.inp("rope", [2, TOK, 64])
        cm_d = self.inp("cm", [128, 4, 128])
        ci_d = self.inp("ci", [128, 2])
        gm_d = self.inp("gm", [128, 2, 128])
        bd_d = self.inp("bd", [128, 128])
        if not self.mod_done:
            self.A_mod()
        MODV = self.dout["MODV" + sfx]
        QT = self.outp("QT", [128, 4, NTOK], BF16)
        KT = self.outp("KT", [128, NTOK], BF16)
        V = self.outp("V", [NTOK, 128], BF16)
        FU = self.outp("FU", [NTOK, 256], BF16)
        OLOC = self.outp("OLOC", [NTOK, 256])
        QTIL = self.outp("QTIL", [2, 128, 2, NTOK], BF16)
        SR = self.outp("SR", [NTOK, 256], BF16)
        DS = self.outp("DS", [128, 2, 2])
        SLOC = self.outp("SLOC", [128, 2, 2, 128])
        SCTX = self.outp("SCTX", [128, 2, 2, 128])
        db = self.dbuf
        cb = self.cb

        cm = A("cm", [128, 4, 128], F32)
        ci = A("ci", [128, 2], F32)
        gm = A("gm", [128, 2, 128], BF16)
        bd = A("bd", [128, 128], F32)
        self.dma("sp", cm[:], cm_d, [], [cb])
        self.dma("sp", ci[:], ci_d, [], [cb])
        self.dma("pool", gm[:], gm_d, [], [cb])
        self.dma("sp", bd[:], bd_d, [], [cb])
        gqk = A("gqk", [128, 10, 64], F32)
        for hh in range(10):
            src = (qg_d if hh < 8 else kg_d).partition_broadcast(128)
            self.dma("sp", gqk[:, hh, :], src, [], [cb])
        wg = A("wg", [17, 2, 256], F32)
        for dr in range(2):
            self.dma("sp", wg[:, dr, :], wg_d[dr], [], [cb])
        zT = A("zT", [17, 2, 128], F32)
        zTb = Buf()
        S.op("dve", lambda e: e.memset(zT[:], 1.0), [], [zTb])

        W = A("w_in", [128, 8, INW], BF16)
        Wb = Buf()
        for k in range(8):
            for c0 in (0, 1040):
                self.dma("pool", W[:, k, c0:c0 + 1040], win_d[k * 128:(k + 1) * 128, c0:c0 + 1040], [], [Wb])

        MODB = A("modb", [128, 4, D], F32)
        MODBb = Buf()
        mk = self.sb.mark()
        rows = A("rowsA", [1, 3, D], F32)
        rowsb = Buf()
        for var in range(2):
            self.dma("sp", rows[0:1, 0, :], g1_d.rearrange("(o n) -> o n", o=1), [], [rowsb])
            self.dma("sp", rows[0:1, 1, :], MODV[var:var + 1, D:2 * D], [db["MODV" + sfx]], [rowsb])
            self.dma("sp", rows[0:1, 2, :], MODV[var:var + 1, 0:D], [db["MODV" + sfx]], [rowsb])
            self.stt(rows[0:1, 1, :], rows[0:1, 1, :], 1.0, rows[0:1, 0, :], ALU.add, ALU.mult, [rowsb], [rowsb])
            for (ri, slot) in ((1, 2 * var), (2, 2 * var + 1)):
                for (bk, bb, c0, c1) in self.bcast_row(rows[0:1, ri, :], rowsb, D):
                    self.cp("act", MODB[:, slot, c0:c1], bk[:, 0:c1 - c0], [bb], [MODBb])

        S.barrier()
        self.sb.release(mk)
        if DBG.get("stop") == "mod":
            return
        KOUT = A("kout", [128, NT, 2, 256], BF16)
        VG = A("vg", [128, NT, 256], BF16)
        QINT = A("qint", [128, NT, 4, 128], BF16)
        OL = A("ol", [128, NT, 256], F32)
        DEC = A("dec", [128, NT, 8], F32)
        tb = [dict(kout=Buf(), vg=Buf(), qint=Buf(), ol=Buf(), dec=Buf()) for _ in range(NT)]

        xt_r = Ring(A, "xt", [128, D], F32, 2)
        junk_r = Ring(A, "junk", [128, D], BF16, 1)
        st_r = Ring(A, "st", [128, 8], F32, 2)
        tmp_r = Ring(A, "tmpA", [128, D], F32, 1)
        h_r = Ring(A, "h", [128, D], BF16, 2)
        hT_r = Ring(A, "hT", [128, 8, 128], BF16, 2)
        qk_r = Ring(A, "qk", [128, 10, 64], F32, 1)
        sq_r = Ring(A, "sq", [128, 10, 64], F32, 1)
        sm_r = Ring(A, "sm", [128, 3, 16], F32, 2)
        qn_r = Ring(A, "qn", [128, 10, 64], F32, 1)
        t1_r = Ring(A, "t1", [128, 10, 64], F32, 1)
        t2_r = Ring(A, "t2", [128, 10, 64], F32, 1)
        qr_r = Ring(A, "qr", [128, 10, 64], BF16, 2)
        qkT_r = Ring(A, "qkT", [128, 5, 128], BF16, 2)
        rp_r = Ring(A, "rp", [128, 2, 64], F32, 2)
        vf_r = Ring(A, "vf", [128, 384], BF16, 2)
        z_r = Ring(A, "z", [128, 32], F32, 2)
        L_r = Ring(A, "L", [128, 2, 256], F32, 2)
        E_r = Ring(A, "E", [128, 3, 512], F32, 1)
        qi_r = Ring(A, "qi", [128, 2, 2, 256], BF16, 2)
        aT_r = Ring(A, "aT", [128, 8, 128], BF16, 2)
        sg_r = Ring(A, "sg", [128, 256], F32, 1)
        sr_r = Ring(A, "srt", [128, 256], BF16, 2)

        for t in range(DBG.get("ntiles", NT)):
            is_ctx = t >= NTL
            var = 1 if is_ctx else 0
            xt, xtb = xt_r.get()
            self.dma("sp", xt[:], xin[t * 128:(t + 1) * 128, :], [xin_b], [xtb])
            st, stb = st_r.get()
            junk, junkb = junk_r.get()
            self.act(junk[:], xt[:], AF.Square, [xtb], [junkb, stb], accum_out=st[:, 0:1])
            self.rstd(st[:, 2:3], st[:, 0:1], 1.0 / D, [stb, self.eps_b], [stb], st[:, 1:2], stb)
            tmp, tmpb = tmp_r.get()
            self.stt(tmp[:], xt[:], st[:, 2:3], MODB[:, 2 * var, :], ALU.mult, ALU.mult, [xtb, stb, MODBb], [tmpb])
            h, hb = h_r.get()
            self.tt("pool", h[:], tmp[:], MODB[:, 2 * var + 1, :], ALU.add, [tmpb, MODBb], [hb])
            if DBG.get('tstop') == 1:
                continue
            bk, bb = self.bank()
            bkb = bk[:].bitcast(BF16)
            for k in range(8):
                self.tr(bkb[:, k * 128:(k + 1) * 128], h[:, k * 128:(k + 1) * 128], self.ident_b[:], [hb, cb], [bb])
            hT, hTb = hT_r.get()
            self.cp("act", hT[:].rearrange("p k t -> p (k t)"), bkb[:, :], [bb], [hTb])
            if DBG.get('tstop') == 2:
                continue
            pbk = []
            for (c0, c1) in ((0, 512), (512, 1024), (1024, 1536), (1536, 2048), (2048, 2080)):
                bk, bb = self.bank(pin=True)
                for k in range(8):
                    self.mm(bk[:, 0:c1 - c0], hT[:, k, :], W[:, k, c0:c1], k == 0, k == 7, [hTb, Wb], [bb])
                pbk.append((bk, bb))
            (Pq, Pqb), (Pk, Pkb), (Pg1, Pg1b), (Pg2, Pg2b), (Pz, Pzb) = pbk
            if DBG.get('tstop') == 3:
                continue
            qk, qkb = qk_r.get()
            sq, sqb = sq_r.get()
            qkf = qk[:].rearrange("p h d -> p (h d)")
            sqf = sq[:].rearrange("p h d -> p (h d)")
            self.cp("act", qkf[:, 0:512], Pq[:, 0:512], [Pqb], [qkb])
            self.cp("act", qkf[:, 512:640], Pk[:, 0:128], [Pkb], [qkb])
            self.unpin(Pq)
            vf, vfb = vf_r.get()
            self.cp("dve", vf[:], Pk[:, 128:512], [Pkb], [vfb])
            self.unpin(Pk)
            self.dma("sp", V[t * 128:(t + 1) * 128, :], vf[:, 0:128], [vfb], [db["V"]])
            self.dma("sp", FU[t * 128:(t + 1) * 128, :], vf[:, 128:384], [vfb], [db["FU"]])
            z, zb_ = z_r.get()
            self.cp("act", z[:], Pz[:, 0:32], [Pzb], [zb_])
            self.unpin(Pz)
            self.act(sqf[:, :], qkf[:, :], AF.Square, [qkb], [sqb])
            sm, smb = sm_r.get()
            S.op("dve", lambda e: e.tensor_reduce(out=sm[:, 0, 0:10], in_=sq[:], axis=AX.X, op=ALU.add), [sqb], [smb])
            self.rstd(sm[:, 2, 0:10], sm[:, 0, 0:10], 1.0 / 64, [smb, self.eps_b], [smb], sm[:, 1, 0:10], smb)
            qn, qnb = qn_r.get()
            self.tt("dve", qn[:], qk[:], sm[:, 2, 0:10].unsqueeze(2).broadcast_to([128, 10, 64]), ALU.mult, [qkb, smb], [qnb])
            self.tt("pool", qn[:], qn[:], gqk[:], ALU.mult, [qnb, cb], [qnb])
            if DBG.get('tstop') == 4:
                continue
            qr, qrb = qr_r.get()
            if not is_ctx:
                rp, rpb = rp_r.get()
                self.dma("sp", rp[:], rope_d[:, t * 128:(t + 1) * 128, :].rearrange("c t d -> t c d"), [], [rpb])
                t1, t1b = t1_r.get()
                t2, t2b = t2_r.get()
                self.tt("dve", t1[:], qn[:], rp[:, 0, :].unsqueeze(1).broadcast_to([128, 10, 64]), ALU.mult, [qnb, rpb], [t1b])
                qn5 = qn[:].rearrange("p h (a s c) -> p (h a) s c", a=2, s=2)
                t25 = t2[:].rearrange("p h (a s c) -> p (h a) s c", a=2, s=2)
                self.cp("pool", t25[:, :, 0, :], qn5[:, :, 1, :], [qnb], [t2b])
                self.cp("pool", t25[:, :, 1, :], qn5[:, :, 0, :], [qnb], [t2b])
                self.tt("pool", t2[:], t2[:], rp[:, 1, :].unsqueeze(1).broadcast_to([128, 10, 64]), ALU.mult, [t2b, rpb], [t2b])
                self.tt("dve", qr[:], t1[:], t2[:], ALU.add, [t1b, t2b], [qrb])
            else:
                self.cp("dve", qr[:], qn[:], [qnb], [qrb])
            if DBG.get('tstop') == 5:
                continue
            bk, bb = self.bank()
            bkb = bk[:].bitcast(BF16)
            qrf = qr[:].rearrange("p h d -> p (h d)")
            for j in range(5):
                self.tr(bkb[:, j * 128:(j + 1) * 128], qrf[:, j * 128:(j + 1) * 128], self.ident_b[:], [qrb, cb], [bb])
            qkT, qkTb = qkT_r.get()
            self.cp("act", qkT[:].rearrange("p j t -> p (j t)"), bkb[:, 0:640], [bb], [qkTb])
            self.dma("sp", QT[:, :, t * 128:(t + 1) * 128], qkT[:, 0:4, :], [qkTb], [db["QT"]])
            self.dma("sp", KT[:, t * 128:(t + 1) * 128], qkT[:, 4, :], [qkTb], [db["KT"]])
            if DBG.get('tstop') == 6:
                continue
            if DBG.get('tstop') == 7:
                continue
            bk, bb = self.bank()
            for dr in range(2):
                self.tr(bk[0:16, dr * 128:(dr + 1) * 128], z[:, dr * 16:(dr + 1) * 16], self.ident_f[:], [zb_, cb], [bb])
            self.cp("dve", zT[0:16, :, :].rearrange("p a t -> p (a t)"), bk[0:16, 0:256], [bb], [zTb])
            bk, bb = self.bank()
            for dr in range(2):
                self.mm(bk[:, dr * 256:(dr + 1) * 256], zT[:, dr, :], wg[:, dr, :], True, True, [zTb, cb], [bb])
            Lt, Lb = L_r.get()
            Lf = Lt[:].rearrange("p a c -> p (a c)")
            self.act(Lf, bk[:, :], AF.Exp, [bb], [Lb], scale=-1.0)
            self.act(Lf, Lf, AF.Ln, [Lb, self.eps_b], [Lb], bias=self.eps_t[:, 1:2])
            if DBG.get('tstop') == 8:
                continue
            c1k, c1b = self.bank()
            c2k, c2b = self.bank()
            self.mm(c1k[:, 0:256], cm[:, 0, :], Lt[:, 0, :], True, True, [cb, Lb], [c1b])
            self.mm(c1k[:, 256:512], cm[:, 2, :], Lt[:, 1, :], True, True, [cb, Lb], [c1b])
            self.mm(c2k[:, 0:256], cm[:, 1, :], Lt[:, 0, :], True, True, [cb, Lb], [c2b])
            self.mm(c2k[:, 256:512], cm[:, 3, :], Lt[:, 1, :], True, True, [cb, Lb], [c2b])
            dk, dkb = self.bank()
            for dr in range(2):
                for pr in range(2):
                    i0 = (dr * 2 + pr) * 2
                    self.mm(dk[:, i0:i0 + 2], Lt[:, dr, pr * 128:(pr + 1) * 128], ci[:, :], True, True, [Lb, cb], [dkb])
            self.act(DEC[:, t, :], dk[:, 0:8], AF.Exp, [dkb], [tb[t]["dec"]])
            if DBG.get('tstop') == 9:
                continue
            E, Eb = E_r.get()
            self.act(E[:, 0, :], c1k[:, :], AF.Exp, [c1b, self.eps_b], [Eb], bias=self.eps_t[:, 2:3])
            self.act(E[:, 1, :], c1k[:, :], AF.Exp, [c1b], [Eb], scale=-1.0)
            self.act(E[:, 2, :], c2k[:, :], AF.Exp, [c2b], [Eb])
            qi, qib = qi_r.get()
            gq_b = Pg1[:, 0:256].unsqueeze(1).broadcast_to([128, 2, 256])
            gk_b = Pg1[:, 256:512].unsqueeze(1).broadcast_to([128, 2, 256])
            self.tt("dve", qi[:, 0, :, :], gq_b, E[:, 0, :].rearrange("p (a c) -> p a c", a=2), ALU.mult, [Pg1b, Eb], [qib])
            self.tt("dve", qi[:, 1, :, :], gk_b, E[:, 1, :].rearrange("p (a c) -> p a c", a=2), ALU.mult, [Pg1b, Eb], [qib])
            self.tt("dve", KOUT[:, t, :, :], gk_b, E[:, 2, :].rearrange("p (a c) -> p a c", a=2), ALU.mult, [Pg1b, Eb], [tb[t]["kout"]])
            self.cp("act", VG[:, t, :], Pg2[:, 0:256], [Pg2b], [tb[t]["vg"]])
            if DBG.get('tstop') == 10:
                continue
            sg, sgb = sg_r.get()
            self.act(sg[:], Pg2[:, 256:512], AF.Exp, [Pg2b], [sgb], scale=-1.0)
            self.ts("pool", sg[:], sg[:], 1.0, None, ALU.add, None, [sgb], [sgb])
            S.op("dve", lambda e: e.reciprocal(out=sg[:], in_=sg[:]), [sgb], [sgb])
            srt, srb = sr_r.get()
            self.tt("dve", srt[:], sg[:], Pg2[:, 256:512], ALU.mult, [sgb, Pg2b], [srb])
            self.dma("sp", SR[t * 128:(t + 1) * 128, :], srt[:], [srb], [db["SR"]])
            self.unpin(Pg1)
            self.unpin(Pg2)
            if DBG.get('tstop') == 11:
                continue
            bk, bb = self.bank()
            bkb = bk[:].bitcast(BF16)
            for wh in range(2):
                for dr in range(2):
                    for pr in range(2):
                        i0 = wh * 4 + dr * 2 + pr
                        self.tr(bkb[:, i0 * 128:(i0 + 1) * 128], qi[:, wh, dr, pr * 128:(pr + 1) * 128], self.ident_b[:], [qib, cb], [bb])
            kiT, kiTb = hT_r.get()
            if DBG.get("v") != "noact":
                self.cp("act", QINT[:, t, :, :].rearrange("p a t -> p (a t)"), bkb[:, 0:512], [bb], [tb[t]["qint"]])
            if DBG.get("v") != "nodve":
                self.cp(DBG.get("kieng", "dve"), kiT[:, 0:4, :].rearrange("p a t -> p (a t)"), bkb[:, 512:1024], [bb], [kiTb])
            if DBG.get('tstop') == 12:
                continue
            a1k, a1b = self.bank()
            a2k, a2b = self.bank()
            for half in range(2):
                ak, ab = (a1k, a1b) if half == 0 else (a2k, a2b)
                p0 = half * 64
                for dr in range(2):
                    for pr in range(2):
                        c0 = (dr * 2 + pr) * 128
                        self.mm(ak[:, c0:c0 + 128], kiT[p0:p0 + 64, dr * 2 + pr, :], QINT[p0:p0 + 64, t, dr * 2 + pr, :],
                                True, True, [kiTb, tb[t]["qint"]], [ab])
            aT, aTb = aT_r.get()
            aT5 = aT[:].rearrange("p (d r f) t -> p d r f t", d=2, r=2, f=2)
            for half in range(2):
                ak, ab = (a1k, a1b) if half == 0 else (a2k, a2b)
                for dr in range(2):
                    self.tt("dve", aT5[:, dr, :, half, :], ak[:, dr * 256:(dr + 1) * 256].rearrange("p (r t) -> p r t", r=2),
                            gm[:, dr, :].unsqueeze(1).broadcast_to([128, 2, 128]), ALU.mult, [ab, cb], [aTb])
            if DBG.get('tstop') == 13:
                continue
            ok, ob = self.bank()
            for hd in range(4):
                for dr in range(2):
                    self.mm(ok[:, hd * 64:(hd + 1) * 64], aT[:, dr * 4 + hd, :], VG[:, t, hd * 64:(hd + 1) * 64], dr == 0, dr == 1,
                            [aTb, tb[t]["vg"]], [ob])
            self.cp("act", OL[:, t, :], ok[:, 0:256], [ob], [tb[t]["ol"]])

        if DBG.get("dbgA"):
            OLI = self.outp("OLI", [NTOK, 256])
            QID = self.outp("QID", [NT, 128, 4, 128], BF16)
            for t in range(NT):
                self.dma("sp", OLI[t * 128:(t + 1) * 128, :], OL[:, t, :], [tb[t]["ol"]], [db["OLI"]])
                self.dma("sp", QID[t], QINT[:, t, :, :], [tb[t]["qint"]], [db["QID"]])
        if DBG.get("stop") == "tiles":
            return
        Sst = A("Sst", [128, 2, 2, 128], F32)
        Sbf = A("Sbf", [128, 2, 2, 128], BF16)
        G = A("G", [128, 4], F32)
        sb_ = [[Buf() for _ in range(2)] for _ in range(2)]
        sbb = [[Buf() for _ in range(2)] for _ in range(2)]
        gb = [[Buf() for _ in range(2)] for _ in range(2)]
        um_r = Ring(A, "um", [128, 128], F32, 3)
        qt_r = Ring(A, "qtl", [128, 64], BF16, 4)

        def scan(tiles, final_S_dram, final_D_dram):
            for dr in range(2):
                order = tiles if dr == 0 else tiles[::-1]
                for pr in range(2):
                    S.op("pool", lambda e: e.memset(Sst[:, dr, pr, :], 0.0), [], [sb_[dr][pr]])
                    S.op("pool", lambda e: e.memset(Sbf[:, dr, pr, :], 0.0), [], [sbb[dr][pr]])
                    S.op("pool", lambda e: e.memset(G[:, dr * 2 + pr:dr * 2 + pr + 1], 1.0), [], [gb[dr][pr]])
                for t in order:
                    for ch in ((0, 1) if dr == 0 else (1, 0)):
                        r0 = ch * 64
                        for pr in range(2):
                            ip = dr * 2 + pr
                            qt, qtb = qt_r.get()
                            self.ts("pool", qt[:], QINT[:, t, ip, r0:r0 + 64], G[:, ip:ip + 1], None, ALU.mult, None,
                                    [tb[t]["qint"], gb[dr][pr]], [qtb])
                            self.dma("sp", QTIL[dr, :, pr, t * 128 + r0:t * 128 + r0 + 64], qt[:], [qtb], [db["QTIL"]])
                            ik, ib = self.bank()
                            self.mm(ik[:, 0:128], QINT[:, t, ip, :], Sbf[:, dr, pr, :], True, True, [tb[t]["qint"], sbb[dr][pr]], [ib])
                            self.tt("dve", OL[r0:r0 + 64, t, pr * 128:(pr + 1) * 128], OL[r0:r0 + 64, t, pr * 128:(pr + 1) * 128],
                                    ik[r0:r0 + 64, 0:128], ALU.add, [tb[t]["ol"], ib], [tb[t]["ol"]])
                            uk, ub = self.bank()
                            self.mm(uk[:, 0:128], KOUT[r0:r0 + 64, t, dr, pr * 128:(pr + 1) * 128], VG[r0:r0 + 64, t, pr * 128:(pr + 1) * 128],
                                    True, True, [tb[t]["kout"], tb[t]["vg"]], [ub])
                            um, umb = um_r.get()
                            self.tt("dve", um[:], uk[:, 0:128], bd[:], ALU.mult, [ub, cb], [umb])
                            dcol = DEC[:, t, ip * 2 + ch:ip * 2 + ch + 1]
                            self.stt(Sst[:, dr, pr, :], Sst[:, dr, pr, :], dcol, um[:], ALU.mult, ALU.add,
                                     [sb_[dr][pr], tb[t]["dec"], umb], [sb_[dr][pr]])
                            self.cp("act", Sbf[:, dr, pr, :], Sst[:, dr, pr, :], [sb_[dr][pr]], [sbb[dr][pr]])
                            self.ts("pool", G[:, ip:ip + 1], G[:, ip:ip + 1], dcol, None, ALU.mult, None, [gb[dr][pr], tb[t]["dec"]], [gb[dr][pr]])
            allS = [sb_[a][b] for a in range(2) for b in range(2)]
            self.dma("sp", final_S_dram, Sst[:], allS, [db["SLOC"], db["SCTX"]])
            if final_D_dram is not None:
                allG = [gb[a][b] for a in range(2) for b in range(2)]
                self.dma("sp", final_D_dram.rearrange("p a b -> p (a b)"), G[:], allG, [db["DS"]])

        scan(list(range(NTL)), SLOC, DS)
        scan(list(range(NTL, NT)), SCTX, None)
        for t in range(NT):
            self.dma("sp", OLOC[t * 128:(t + 1) * 128, :], OL[:, t, :], [tb[t]["ol"]], [db["OLOC"]])

    def phase_B(self, l, xin, xin_b, with_ctx):
        nc, S, A = self.nc, self.S, self.alloc
        db, cb = self.dbuf, self.cb
        tiles = list(range(NT if with_ctx else NTL))
        nvar = 2 if with_ctx else 1
        I = lambda name, shape, dt=F32: self.inp("i_" + name, shape, dt)
        MODV = I("MODV", [2, 6 * D])
        QT = I("QT", [128, 4, NTOK], BF16)
        KTE = I("KTE", [128, 20 * 128], BF16)
        VE = I("VE", [20 * 128, 128], BF16)
        UALL = I("UALL", [SEQ, 256], BF16)
        FUC = I("FUC", [CTX, 256], BF16)
        OLOC = I("OLOC", [NTOK, 256])
        QTIL = I("QTIL", [2, 128, 2, NTOK], BF16)
        SR = I("SR", [NTOK, 256], BF16)
        DSg = I("DSg", [128, 4, 2, 2])
        SLOCg = I("SLOCg", [128, 4, 2, 2, 128])
        SCTX = I("SCTX", [128, 2, 2, 128])
        rmask_d = I("rmask", [128, 4, 2])
        amask_d = I("amask", [128, 4, 128])
        TAB = I("TAB", [2, 4, 64, 128, 512], BF16)
        TABC = I("TABC", [2, 1, 2, 128, 256], BF16)
        CB_d = I("CB", [128, 2, 128])
        wf_d = I("wf", [4, 64, 64])
        glag_d = I("glag", [64])
        sink_d = I("sink", [8])
        wout_d = I("wout", [D, D])
        g2_d = I("g2", [D])
        wffi_d = I("wffi", [D, 2 * HID])
        wffo_d = I("wffo", [HID, D])
        XO = self.outp("XO", [NTOK, D])
        FOURT = nc.dram_tensor("FOURT", [2, 128, NTOK], BF16, kind="Internal").ap()
        H2T = nc.dram_tensor("H2T", [NT, 128, 8, 128], BF16, kind="Internal").ap()
        fourb = Buf()
        h2tb = [Buf() for _ in range(NT)]
        xob = [Buf() for _ in range(NT)]
        pmark = self.sb.mark()

        Wi = A("w_ffi", [128, 8, 2 * HID], BF16, high=True)
        Wib = Buf()
        for k in range(8):
            for c0 in range(0, 2 * HID, 1408):
                self.dma("pool", Wi[:, k, c0:c0 + 1408], wffi_d[k * 128:(k + 1) * 128, c0:c0 + 1408], [], [Wib])

        amask = A("amask", [128, 4, 128], BF16)
        self.dma("pool", amask[:], amask_d, [], [cb])
        exs = A("exs", [128, 8], F32)
        self.dma("sp", exs[:], sink_d.partition_broadcast(128), [], [cb])
        self.act(exs[:], exs[:], AF.Exp, [cb], [cb])
        glag = A("glag", [128, 64], F32)
        self.dma("sp", glag[:], glag_d.partition_broadcast(128), [], [cb])
        MODB = A("modbB", [128, nvar, 4, D], F32)
        MODBb = Buf()
        mk = self.sb.mark()
        rows = A("rowsB", [1, 5, D], F32)
        rowsb = Buf()
        for var in range(nvar):
            self.dma("sp", rows[0:1, 0, :], g2_d.rearrange("(o n) -> o n", o=1), [], [rowsb])
            for (ri, c0) in ((1, 2 * D), (2, 4 * D), (3, 3 * D), (4, 5 * D)):
                self.dma("sp", rows[0:1, ri, :], MODV[var:var + 1, c0:c0 + D], [], [rowsb])
            self.stt(rows[0:1, 2, :], rows[0:1, 2, :], 1.0, rows[0:1, 0, :], ALU.add, ALU.mult, [rowsb], [rowsb])
            for (ri, slot) in ((1, 0), (2, 1), (3, 2), (4, 3)):
                for (bk, bb, c0, c1) in self.bcast_row(rows[0:1, ri, :], rowsb, D):
                    self.cp("act", MODB[:, var, slot, c0:c1], bk[:, 0:c1 - c0], [bb], [MODBb])
        Sin = A("Sin", [128, 2, 2, 128], BF16, high=True)
        Sinb = Buf()
        dsg = A("dsg", [128, 4, 2, 2], F32)
        slg = A("slg", [128, 4, 2, 2, 128], F32)
        sct = A("sct", [128, 2, 2, 128], F32)
        rmk = A("rmk", [128, 4, 2], F32)
        sttmp = A("sttmp", [128, 2, 128], F32)
        gb_ = Buf()
        self.dma("sp", dsg[:], DSg, [], [gb_])
        self.dma("sp", slg[:], SLOCg, [], [gb_])
        self.dma("sp", sct[:], SCTX, [], [gb_])
        self.dma("sp", rmk[:], rmask_d, [], [gb_])
        for dr in range(2):
            for r in (range(4) if dr == 0 else range(3, -1, -1)):
                self.tt("dve", sttmp[:], sct[:, dr, :, :], dsg[:, r, dr, :].unsqueeze(2).broadcast_to([128, 2, 128]), ALU.mult, [gb_], [gb_])
                self.tt("dve", sttmp[:], sttmp[:], slg[:, r, dr, :, :], ALU.add, [gb_], [gb_])
                self.tt("dve", sttmp[:], sttmp[:], sct[:, dr, :, :], ALU.subtract, [gb_], [gb_])
                self.stt(sct[:, dr, :, :], sttmp[:], rmk[:, r, dr:dr + 1], sct[:, dr, :, :], ALU.mult, ALU.add, [gb_], [gb_])
        self.cp("dve", Sin[:], sct[:], [gb_], [Sinb])
        S.barrier()
        self.sb.release(mk)

        fmark = self.sb.mark()
        U = A("uall", [128, 64, 256], BF16)
        Ub = Buf()
        for q4 in range(4):
            self.dma("sp", U[:, q4 * 16:(q4 + 1) * 16, :], UALL[q4 * 2048:(q4 + 1) * 2048, :].rearrange("(k p) c -> p k c", p=128), [], [Ub])
        CBs = A("CBs", [128, 2, 128], F32)
        self.dma("sp", CBs[:], CB_d, [], [cb])
        wblk = A("wblk", [128, 2, 128], F32)
        wblkb = Buf()
        S.op("dve", lambda e: e.memset(wblk[:], 0.0), [], [wblkb])
        for g in range(4):
            hf, gi = g // 2, g % 2
            self.dma("sp", wblk[gi * 64:(gi + 1) * 64, hf, gi * 64:(gi + 1) * 64], wf_d[g], [wblkb], [wblkb])
        Mblk = A("Mblk", [128, 2, 2, 128], BF16)
        Mb = Buf()
        for hf in range(2):
            for cs in range(2):
                bk, bb = self.bank()
                self.mm(bk[:, 0:128], CBs[:, cs, :], wblk[:, hf, :], True, True, [cb, wblkb], [bb])
                self.cp("act", Mblk[:, hf, cs, :], bk[:, 0:128], [bb], [Mb])
        tab_r = Ring(A, "tab", [128, 8, 512], BF16, 2)
        abt_r = Ring(A, "abt", [128, 2, 2, 512], BF16, 1)
        ft_r = Ring(A, "ft", [128, 512], BF16, 2)

        def fnet(Usb, Usbb, ntc, tabd, nblk, blk, col0):
            tg = min(8, ntc)
            for b in range(nblk):
                abt, abtb = abt_r.get()
                for cs in range(2):
                    acc = [self.bank(), self.bank()]
                    for g0 in range(0, ntc, tg):
                        tab, tabb = tab_r.get()
                        self.dma("sp", tab[:, 0:tg, 0:blk], tabd[cs, b, g0:g0 + tg].rearrange("k p c -> p k c"), [], [tabb])
                        for tc in range(tg):
                            for hf in range(2):
                                self.mm(acc[hf][0][:, 0:blk], Usb[:, g0 + tc, hf * 128:(hf + 1) * 128], tab[:, tc, 0:blk],
                                        g0 + tc == 0, g0 + tc == ntc - 1, [Usbb, tabb], [acc[hf][1]])
                    for hf in range(2):
                        self.cp("act" if hf == 0 else "dve", abt[:, hf, cs, 0:blk], acc[hf][0][:, 0:blk], [acc[hf][1]], [abtb])
                for hf in range(2):
                    bk, bb = self.bank()
                    for cs in range(2):
                        self.mm(bk[:, 0:blk], Mblk[:, hf, cs, :], abt[:, hf, cs, 0:blk], cs == 0, cs == 1, [Mb, abtb], [bb])
                    ft, ftb = ft_r.get()
                    self.cp("act", ft[:, 0:blk], bk[:, 0:blk], [bb], [ftb])
                    self.dma("sp", FOURT[hf, :, col0 + b * blk:col0 + (b + 1) * blk], ft[:, 0:blk], [ftb], [fourb])

        fnet(U, Ub, 64, TAB, 4, 512, 0)
        if with_ctx:
            Uc = A("uc", [128, 2, 256], BF16)
            Ucb = Buf()
            self.dma("sp", Uc[:], FUC.rearrange("(k p) c -> p k c", p=128), [], [Ucb])
            fnet(Uc, Ucb, 2, TABC, 1, 256, TOK)
        S.barrier()
        self.sb.release(fmark)

        b1mark = self.sb.mark()
        Wo = A("w_out", [128, 8, D], BF16)
        Wob = Buf()
        for k in range(8):
            self.dma("pool", Wo[:, k, :], wout_d[k * 128:(k + 1) * 128, :], [], [Wob])
        KTs = A("kte", [128, 20, 128], BF16)
        self.dma("sp", KTs[:].rearrange("p k t -> p (k t)"), KTE, [], [cb])
        VEa = A("vea", [128, 20, 2, 65], BF16)
        S.op("pool", lambda e: e.memset(VEa[:, :, :, 64:65], 1.0), [], [cb])
        for g in range(2):
            self.dma("sp", VEa[:, :, g, 0:64], VE[:, g * 64:(g + 1) * 64].rearrange("(k p) d -> p k d", p=128), [], [cb])
        xt_r = Ring(A, "xtB", [128, D], F32, 2)
        qt_r = Ring(A, "qtB", [128, 4, 128], BF16, 2)
        pT_r = Ring(A, "pT", [128, 5, 512], BF16, 2)
        den_r = Ring(A, "den", [128, 8], F32, 2)
        att_r = Ring(A, "att", [128, 8, 64], BF16, 2)
        mix_r = Ring(A, "mixT", [128, 8, 128], BF16, 2)
        ol_r = Ring(A, "olB", [128, 256], F32, 2)
        srr = Ring(A, "srB", [128, 256], BF16, 2)
        qtl_r = Ring(A, "qtlB", [128, 2, 2, 128], BF16, 2)
        o_r = Ring(A, "oB", [128, 4, 64], F32, 1)
        sq_r = Ring(A, "sqB", [128, 4, 64], F32, 1)
        sm_r = Ring(A, "smB", [128, 3, 4], F32, 2)
        y_r = Ring(A, "yB", [128, 4, 64], BF16, 2)
        tmp_r = Ring(A, "tmpB", [128, D], F32, 1)
        xn_r = Ring(A, "xnB", [128, D], F32, 2)
        junk_r = Ring(A, "junkB", [128, D], BF16, 1)
        st_r = Ring(A, "stB", [128, 4], F32, 2)
        h2_r = Ring(A, "h2B", [128, D], BF16, 2)
        h2T_r = Ring(A, "h2TB", [128, 8, 128], BF16, 2)
        for t in tiles:
            is_ctx = t >= NTL
            var = 1 if is_ctx else 0
            if is_ctx:
                kts = [18, 19]
                msk = [None, None]
            else:
                kts = [t, t + 1, t + 2, 18, 19]
                msk = [0 if t == 0 else 1, None, 3 if t == NTL - 1 else 2, None, None]
            nk = len(kts)
            qt, qtb = qt_r.get()
            self.dma("sp", qt[:], QT[:, :, t * 128:(t + 1) * 128], [], [qtb])
            xt, xtb = xt_r.get()
            self.dma("sp", xt[:], xin[t * 128:(t + 1) * 128, :], [xin_b], [xtb])
            ol, olb = ol_r.get()
            self.dma("sp", ol[:], OLOC[t * 128:(t + 1) * 128, :], [], [olb])
            sr, srb = srr.get()
            self.dma("sp", sr[:], SR[t * 128:(t + 1) * 128, :], [], [srb])
            mix, mixb = mix_r.get()
            self.dma("sp", mix[:, 4:6, :], FOURT[:, :, t * 128:(t + 1) * 128].rearrange("h p t -> p h t"), [fourb], [mixb])
            pTs = []
            for g in range(2):
                pT, pTb = pT_r.get()
                for i, kt in enumerate(kts):
                    bk, bb = self.bank()
                    self.mm(bk[:, :], KTs[g * 64:(g + 1) * 64, kt, :], qt[g * 64:(g + 1) * 64, :, :].rearrange("p j t -> p (j t)"),
                            True, True, [cb, qtb], [bb])
                    self.act(pT[:, i, :], bk[:, :], AF.Exp, [bb], [pTb], scale=0.125)
                    if msk[i] is not None:
                        self.tt("pool" if g == 0 else "dve", pT[:, i, :].rearrange("p (j t) -> p j t", j=4), pT[:, i, :].rearrange("p (j t) -> p j t", j=4),
                                amask[:, msk[i], :].unsqueeze(1).broadcast_to([128, 4, 128]), ALU.mult, [pTb, cb], [pTb])
                pTs.append((pT, pTb))
            pv = [self.bank(), self.bank()]
            for g in range(2):
                pT, pTb = pTs[g]
                for j in range(4):
                    for i, kt in enumerate(kts):
                        self.mm(pv[g][0][:, j * 65:(j + 1) * 65], pT[:, i, j * 128:(j + 1) * 128], VEa[:, kt, g, :], i == 0, i == nk - 1,
                                [pTb, cb], [pv[g][1]])
            den, denb = den_r.get()
            att, attb = att_r.get()
            for g in range(2):
                pvv = pv[g][0][:, 0:260].rearrange("p (j c) -> p j c", c=65)
                self.tt("dve", den[:, g * 4:(g + 1) * 4].unsqueeze(2), pvv[:, :, 64:65], exs[:, g * 4:(g + 1) * 4].unsqueeze(2), ALU.add,
                        [pv[g][1], cb], [denb])
            S.op("dve", lambda e: e.reciprocal(out=den[:], in_=den[:]), [denb], [denb])
            for g in range(2):
                pvv = pv[g][0][:, 0:260].rearrange("p (j c) -> p j c", c=65)
                self.tt("dve", att[:, g * 4:(g + 1) * 4, :], pvv[:, :, 0:64], den[:, g * 4:(g + 1) * 4].unsqueeze(2).broadcast_to([128, 4, 64]),
                        ALU.mult, [pv[g][1], denb], [attb])
            o, ob = o_r.get()
            of = o[:].rearrange("p h e -> p (h e)")
            if not is_ctx:
                qtl, qtlb = qtl_r.get()
                for dr in range(2):
                    self.dma("sp", qtl[:, dr, :, :], QTIL[dr, :, :, t * 128:(t + 1) * 128], [], [qtlb])
                bk, bb = self.bank()
                for pr in range(2):
                    for dr in range(2):
                        self.mm(bk[:, pr * 128:(pr + 1) * 128], qtl[:, dr, pr, :], Sin[:, dr, pr, :], dr == 0, dr == 1, [qtlb, Sinb], [bb])
                self.tt("dve", of, bk[:, 0:256], ol[:], ALU.add, [bb, olb], [ob])
            else:
                self.cp("dve", of, ol[:], [olb], [ob])
            sq, sqb = sq_r.get()
            self.act(sq[:].rearrange("p h e -> p (h e)"), of, AF.Square, [ob], [sqb])
            sm, smb = sm_r.get()
            S.op("dve", lambda e: e.tensor_reduce(out=sm[:, 0, :], in_=sq[:], axis=AX.X, op=ALU.add), [sqb], [smb])
            self.rstd(sm[:, 2, :], sm[:, 0, :], 1.0 / 64, [smb, self.eps_b], [smb], sm[:, 1, :], smb)
            self.tt("dve", o[:], o[:], sm[:, 2, :].unsqueeze(2).broadcast_to([128, 4, 64]), ALU.mult, [ob, smb], [ob])
            self.tt("pool", o[:], o[:], glag[:].unsqueeze(1).broadcast_to([128, 4, 64]), ALU.mult, [ob, cb], [ob])
            y, yb = y_r.get()
            self.tt("dve", y[:].rearrange("p h e -> p (h e)"), of, sr[:], ALU.mult, [ob, srb], [yb])
            bk, bb = self.bank()
            bkb = bk[:].bitcast(BF16)
            attf = att[:].rearrange("p h e -> p (h e)")
            yf = y[:].rearrange("p h e -> p (h e)")
            for k in range(4):
                self.tr(bkb[:, k * 128:(k + 1) * 128], attf[:, k * 128:(k + 1) * 128], self.ident_b[:], [attb, cb], [bb])
            for k in range(2):
                self.tr(bkb[:, (4 + k) * 128:(5 + k) * 128], yf[:, k * 128:(k + 1) * 128], self.ident_b[:], [yb, cb], [bb])
            self.cp("act", mix[:, 0:4, :].rearrange("p k t -> p (k t)"), bkb[:, 0:512], [bb], [mixb])
            self.cp("act", mix[:, 6:8, :].rearrange("p k t -> p (k t)"), bkb[:, 512:768], [bb], [mixb])
            if DBG.get("mixout"):
                if t == tiles[0]:
                    self.MIXD = self.outp("MIXD", [NT, 128, 8, 128], BF16)
                self.dma("sp", self.MIXD[t], mix[:], [mixb], [db["MIXD"]])
            xn, xnb = xn_r.get()
            tmp, tmpb = tmp_r.get()
            for hfc in range(2):
                bk, bb = self.bank()
                for k in range(8):
                    self.mm(bk[:, :], mix[:, k, :], Wo[:, k, hfc * 512:(hfc + 1) * 512], k == 0, k == 7, [mixb, Wob], [bb])
                self.tt("dve", tmp[:, hfc * 512:(hfc + 1) * 512], bk[:, :], MODB[:, var, 0, hfc * 512:(hfc + 1) * 512], ALU.mult, [bb, MODBb], [tmpb])
            self.tt("pool", xn[:], tmp[:], xt[:], ALU.add, [tmpb, xtb], [xnb])
            self.dma("sp", XO[t * 128:(t + 1) * 128, :], xn[:], [xnb], [xob[t]])
            st, stb = st_r.get()
            junk, junkb = junk_r.get()
            self.act(junk[:], xn[:], AF.Square, [xnb], [junkb, stb], accum_out=st[:, 0:1])
            self.rstd(st[:, 2:3], st[:, 0:1], 1.0 / D, [stb, self.eps_b], [stb], st[:, 1:2], stb)
            tmp, tmpb = tmp_r.get()
            self.stt(tmp[:], xn[:], st[:, 2:3], MODB[:, var, 1, :], ALU.mult, ALU.mult, [xnb, stb, MODBb], [tmpb])
            h2, h2b = h2_r.get()
            self.tt("pool", h2[:], tmp[:], MODB[:, var, 2, :], ALU.add, [tmpb, MODBb], [h2b])
            bk, bb = self.bank()
            bkb = bk[:].bitcast(BF16)
            for k in range(8):
                self.tr(bkb[:, k * 128:(k + 1) * 128], h2[:, k * 128:(k + 1) * 128], self.ident_b[:], [h2b, cb], [bb])
            h2T, h2Tb = h2T_r.get()
            self.cp("act", h2T[:].rearrange("p k t -> p (k t)"), bkb[:, :], [bb], [h2Tb])
            self.dma("sp", H2T[t], h2T[:], [h2Tb], [h2tb[t]])
        S.barrier()
        self.sb.release(b1mark)

        Wf = A("w_ffo", [128, NCH, D], BF16, high=True)
        Wfb = Buf()
        for j in range(NCH):
            self.dma("pool", Wf[:, j, :], wffo_d[j * 128:(j + 1) * 128, :], [], [Wfb])
        hp_r = Ring(A, "hp", [128, 8, 256], BF16, 2)
        xp_r = Ring(A, "xp", [128, 2, D], F32, 2)
        sg_r = Ring(A, "sgF", [128, 256], BF16, 2)
        ac_r = Ring(A, "acF", [128, 256], BF16, 3)
        tm_r = Ring(A, "tmF", [128, D], F32, 1)
        xo_r = Ring(A, "xoF", [128, D], F32, 2)
        accs = self.banks[0:4]
        self.brange = (4, 8)
        for p in range(len(tiles) // 2):
            t0 = 2 * p
            var = 1 if t0 >= NTL else 0
            hp, hpb = hp_r.get()
            xp, xpb = xp_r.get()
            for i in range(2):
                self.dma("sp", hp[:, :, i * 128:(i + 1) * 128], H2T[t0 + i], [h2tb[t0 + i]], [hpb])
                self.dma("sp", xp[:, i, :], XO[(t0 + i) * 128:(t0 + i + 1) * 128, :], [xob[t0 + i]], [xpb])
            for j in range(NCH):
                gk, gbk = self.bank()
                uk, ubk = self.bank()
                for k in range(8):
                    self.mm(gk[:, 0:256], Wi[:, k, j * 128:(j + 1) * 128], hp[:, k, :], k == 0, k == 7, [Wib, hpb], [gbk])
                for k in range(8):
                    self.mm(uk[:, 0:256], Wi[:, k, HID + j * 128:HID + (j + 1) * 128], hp[:, k, :], k == 0, k == 7, [Wib, hpb], [ubk])
                sg, sgb = sg_r.get()
                self.act(sg[:], gk[:, 0:256], AF.Silu, [gbk], [sgb])
                ac, acb = ac_r.get()
                self.tt("dve", ac[:], sg[:], uk[:, 0:256], ALU.mult, [sgb, ubk], [acb])
                for i in range(2):
                    for hfc in range(2):
                        a_, ab_ = accs[i * 2 + hfc]
                        self.mm(a_[:, :], ac[:, i * 128:(i + 1) * 128], Wf[:, j, hfc * 512:(hfc + 1) * 512], j == 0, j == NCH - 1, [acb, Wfb], [ab_])
            for i in range(2):
                tm, tmb = tm_r.get()
                for hfc in range(2):
                    a_, ab_ = accs[i * 2 + hfc]
                    self.tt("dve", tm[:, hfc * 512:(hfc + 1) * 512], a_[:, :], MODB[:, var, 3, hfc * 512:(hfc + 1) * 512], ALU.mult, [ab_, MODBb], [tmb])
                xo, xo_b = xo_r.get()
                self.tt("pool", xo[:], tm[:], xp[:, i, :], ALU.add, [tmb, xpb], [xo_b])
                self.dma("sp", XO[(t0 + i) * 128:(t0 + i + 1) * 128, :], xo[:], [xo_b], [xob[t0 + i], db["XO"]])
        self.brange = (0, 8)
        S.barrier()
        self.sb.release(pmark)

    def A_mod_f(self, l):
        nc, S, A = self.nc, self.S, self.alloc
        db = self.dbuf
        cT_d = self.inp("cT", [128, 8, 2])
        wmod_d = self.inp("wmod%d" % l, [D, 6 * D])
        bmod_d = self.inp("bmod%d" % l, [6 * D])
        MODV = self.dram("MODV%d" % l, [2, 6 * D])
        self.modv[l] = MODV
        mk = self.sb.mark()
        cTs = A("cTs", [128, 8, 2], F32)
        cTb = Buf()
        self.dma("sp", cTs[:], cT_d, [], [cTb])
        scT = A("scT", [128, 8, 64], BF16)
        scb = Buf()
        S.op("dve", lambda e: e.memset(scT[:], 0.0), [], [scb])
        sil = A("sil", [128, 8, 2], F32)
        silb = Buf()
        self.act(sil[:], cTs[:], AF.Exp, [cTb], [silb], scale=-1.0)
        self.ts("dve", sil[:], sil[:], 1.0, None, ALU.add, None, [silb], [silb])
        S.op("dve", lambda e: e.reciprocal(out=sil[:], in_=sil[:]), [silb], [silb])
        self.tt("dve", sil[:], sil[:], cTs[:], ALU.mult, [silb, cTb], [silb])
        self.cp("dve", scT[:, :, 0:1], sil[:, :, 0:1], [silb], [scb])
        self.cp("dve", scT[:, :, 32:33], sil[:, :, 1:2], [silb], [scb])
        wm_ring = Ring(A, "wm", [128, 8, 512], BF16, 2)
        bm_ring = Ring(A, "bm", [64, 512], F32, 2)
        mr_ring = Ring(A, "mr", [64, 512], F32, 2)
        mvb = db["MODV%d" % l]
        for cbk in range(12):
            wm, wmb = wm_ring.get()
            self.dma("pool", wm[:], wmod_d[:, cbk * 512:(cbk + 1) * 512].rearrange("(k p) c -> p k c", p=128), [], [wmb])
            bm, bmb = bm_ring.get()
            self.dma("sp", bm[:], bmod_d[cbk * 512:(cbk + 1) * 512].partition_broadcast(64), [], [bmb])
            bk, bb = self.bank()
            for k in range(8):
                self.mm(bk[0:64, :], scT[:, k, :], wm[:, k, :], k == 0, k == 7, [scb, wmb], [bb])
            mr, mrb = mr_ring.get()
            self.tt("dve", mr[:], bk[0:64, :], bm[:], ALU.add, [bb, bmb], [mrb])
            self.dma("sp", MODV[0:1, cbk * 512:(cbk + 1) * 512], mr[0:1, :], [mrb], [mvb])
            self.dma("sp", MODV[1:2, cbk * 512:(cbk + 1) * 512], mr[32:33, :], [mrb], [mvb])
        S.barrier()
        self.sb.release(mk)

    def phase_A_f(self, l, xin, xin_b):
        nc, S, A = self.nc, self.S, self.alloc
        ntl, nt, ntok = NTLF, NTF, NTOKF
        pmark = self.sb.mark()
        g1_d = self.inp("g1%d" % l, [D])
        win_d = self.inp("win%d" % l, [D, INW])
        qg_d = self.inp("qg%d" % l, [64])
        kg_d = self.inp("kg%d" % l, [64])
        wg_d = self.inp("wg%d" % l, [2, 17, 256])
        rope_d = self.inp("rope", [2, SEQ, 64])
        cm_d = self.inp("cm", [128, 4, 128])
        ci_d = self.inp("ci", [128, 2])
        gm_d = self.inp("gm", [128, 2, 128])
        bd_d = self.inp("bd", [128, 128])
        MODV = self.modv[l]
        L = "%d" % l
        QT = self.dram("QT" + L, [128, 4, ntok], BF16)
        KTP = self.dram("KTP" + L, [128, (ntl + 2) * 128], BF16)
        KTC = self.dram("KTC" + L, [128, CTX], BF16)
        VP = self.dram("VP" + L, [(ntl + 2) * 128, 128], BF16)
        VC = self.dram("VC" + L, [CTX, 128], BF16)
        FUL = self.dram("FUL" + L, [SEQ, 256], BF16)
        FUC = self.dram("FUC" + L, [CTX, 256], BF16)
        OLOC = self.dram("OLOC" + L, [ntok, 256])
        SR = self.dram("SR" + L, [ntok, 256], BF16)
        KOUTd = self.dram("KOUTd" + L, [nt, 128, 2, 256], BF16)
        VGd = self.dram("VGd" + L, [nt, 128, 256], BF16)
        QINTd = self.dram("QINTd" + L, [nt, 128, 4, 128], BF16)
        DECd = self.dram("DECd" + L, [nt, 128, 8])
        self.aout[l] = dict(QT=QT, KTP=KTP, KTC=KTC, VP=VP, VC=VC, FUL=FUL, FUC=FUC, OLOC=OLOC, SR=SR)
        db = self.dbuf
        cb = self.cb
        olb = [Buf() for _ in range(nt)]
        self.aout[l]["olb"] = olb
        scb_ = [Buf() for _ in range(nt)]

        cm = A("cm", [128, 4, 128], F32)
        ci = A("ci", [128, 2], F32)
        gm = A("gm", [128, 2, 128], BF16)
        bd = A("bd", [128, 128], F32)
        self.dma("sp", cm[:], cm_d, [], [cb])
        self.dma("sp", ci[:], ci_d, [], [cb])
        self.dma("pool", gm[:], gm_d, [], [cb])
        self.dma("sp", bd[:], bd_d, [], [cb])
        gqk = A("gqk", [128, 10, 64], F32)
        for hh in range(10):
            src = (qg_d if hh < 8 else kg_d).partition_broadcast(128)
            self.dma("sp", gqk[:, hh, :], src, [], [cb])
        wg = A("wg", [17, 2, 256], F32)
        for dr in range(2):
            self.dma("sp", wg[:, dr, :], wg_d[dr], [], [cb])
        zT = A("zT", [17, 2, 128], F32)
        zTb = Buf()
        S.op("dve", lambda e: e.memset(zT[:], 1.0), [], [zTb])
        zt = A("zeroT", [128, 128], BF16)
        ztb = Buf()
        S.op("dve", lambda e: e.memset(zt[:], 0.0), [], [ztb])
        for pos in (0, ntl + 1):
            self.dma("sp", KTP[:, pos * 128:(pos + 1) * 128], zt[:], [ztb], [db["KTP" + L]])
            self.dma("sp", VP[pos * 128:(pos + 1) * 128, :], zt[:], [ztb], [db["VP" + L]])

        W = A("w_in", [128, 8, INW], BF16)
        Wb = Buf()
        for k in range(8):
            for c0 in (0, 1040):
                self.dma("pool", W[:, k, c0:c0 + 1040], win_d[k * 128:(k + 1) * 128, c0:c0 + 1040], [], [Wb])

        MODB = A("modb", [128, 4, D], F32)
        MODBb = Buf()
        mk = self.sb.mark()
        rows = A("rowsA", [1, 3, D], F32)
        rowsb = Buf()
        for var in range(2):
            self.dma("sp", rows[0:1, 0, :], g1_d.rearrange("(o n) -> o n", o=1), [], [rowsb])
            self.dma("sp", rows[0:1, 1, :], MODV[var:var + 1, D:2 * D], [db["MODV" + L]], [rowsb])
            self.dma("sp", rows[0:1, 2, :], MODV[var:var + 1, 0:D], [db["MODV" + L]], [rowsb])
            self.stt(rows[0:1, 1, :], rows[0:1, 1, :], 1.0, rows[0:1, 0, :], ALU.add, ALU.mult, [rowsb], [rowsb])
            for (ri, slot) in ((1, 2 * var), (2, 2 * var + 1)):
                for (bk, bb, c0, c1) in self.bcast_row(rows[0:1, ri, :], rowsb, D):
                    self.cp("act", MODB[:, slot, c0:c1], bk[:, 0:c1 - c0], [bb], [MODBb])
        S.barrier()
        self.sb.release(mk)

        def mkrings(tag):
            xt_r = Ring(A, "xt" + tag, [128, D], F32, 1)
            junk_r = Ring(A, "junk" + tag, [128, D], BF16, 1)
            st_r = Ring(A, "st" + tag, [128, 8], F32, 1)
            tmp_r = Ring(A, "tmpA" + tag, [128, D], F32, 1)
            h_r = Ring(A, "h" + tag, [128, D], BF16, 1)
            hT_r = Ring(A, "hT" + tag, [128, 8, 128], BF16, 2)
            qk_r = Ring(A, "qk" + tag, [128, 10, 64], F32, 1)
            sq_r = Ring(A, "sq" + tag, [128, 10, 64], F32, 1)
            sm_r = Ring(A, "sm" + tag, [128, 3, 16], F32, 1)
            qn_r = Ring(A, "qn" + tag, [128, 10, 64], F32, 1)
            t1_r = Ring(A, "t1" + tag, [128, 10, 64], F32, 1)
            t2_r = Ring(A, "t2" + tag, [128, 10, 64], F32, 1)
            qr_r = Ring(A, "qr" + tag, [128, 10, 64], BF16, 1)
            qkT_r = Ring(A, "qkT" + tag, [128, 5, 128], BF16, 1)
            rp_r = Ring(A, "rp" + tag, [128, 2, 64], F32, 1)
            vf_r = Ring(A, "vf" + tag, [128, 384], BF16, 1)
            z_r = Ring(A, "z" + tag, [128, 32], F32, 1)
            L_r = Ring(A, "L" + tag, [128, 2, 256], F32, 1)
            E_r = Ring(A, "E" + tag, [128, 3, 512], F32, 1)
            qi_r = Ring(A, "qi" + tag, [128, 2, 2, 256], BF16, 1)
            aT_r = Ring(A, "aT" + tag, [128, 8, 128], BF16, 1)
            sg_r = Ring(A, "sg" + tag, [128, 256], F32, 1)
            sr_r = Ring(A, "srt" + tag, [128, 256], BF16, 1)
            ko_r = Ring(A, "koR" + tag, [128, 2, 256], BF16, 1)
            vg_r = Ring(A, "vgR" + tag, [128, 256], BF16, 1)
            qint_r = Ring(A, "qintR" + tag, [128, 4, 128], BF16, 1)
            ol_r = Ring(A, "olR" + tag, [128, 256], F32, 1)
            dec_r = Ring(A, "decR" + tag, [128, 8], F32, 1)
            g1s_r = Ring(A, "g1s" + tag, [128, 512], F32, 1)
            grs_r = Ring(A, "grs" + tag, [128, 256], F32, 1)
            zT_r = Ring(A, "zTp" + tag, [17, 2, 128], F32, 1)
            for (zz, zzb) in zT_r.items:
                S.op("dve", lambda e: e.memset(zz[:], 1.0), [], [zzb])
            return (xt_r, junk_r, st_r, tmp_r, h_r, hT_r, qk_r, sq_r, sm_r, qn_r, t1_r, t2_r, qr_r, qkT_r, rp_r, vf_r, z_r, L_r, E_r, qi_r, aT_r, sg_r, sr_r, ko_r, vg_r, qint_r, ol_r, dec_r, g1s_r, grs_r, zT_r)

        RR = [mkrings("a"), mkrings("b")]

        def tileA(t):
            par = t % 2
            self.brange = (0, 4) if par == 0 else (4, 8)
            (xt_r, junk_r, st_r, tmp_r, h_r, hT_r, qk_r, sq_r, sm_r, qn_r, t1_r, t2_r, qr_r, qkT_r, rp_r, vf_r, z_r, L_r, E_r, qi_r, aT_r, sg_r, sr_r, ko_r, vg_r, qint_r, ol_r, dec_r, g1s_r, grs_r, zT_r) = RR[par]
            is_ctx = t >= ntl
            var = 1 if is_ctx else 0
            tc = t - ntl
            zT, zTb = zT_r.items[0]
            xt, xtb = xt_r.get()
            self.dma("act", xt[:], xin[t * 128:(t + 1) * 128, :], xin_b(t), [xtb])
            st, stb = st_r.get()
            junk, junkb = junk_r.get()
            self.act(junk[:], xt[:], AF.Square, [xtb], [junkb, stb], accum_out=st[:, 0:1])
            self.rstd(st[:, 2:3], st[:, 0:1], 1.0 / D, [stb, self.eps_b], [stb], st[:, 1:2], stb)
            tmp, tmpb = tmp_r.get()
            self.stt(tmp[:], xt[:], st[:, 2:3], MODB[:, 2 * var, :], ALU.mult, ALU.mult, [xtb, stb, MODBb], [tmpb])
            h, hb = h_r.get()
            self.tt("pool", h[:], tmp[:], MODB[:, 2 * var + 1, :], ALU.add, [tmpb, MODBb], [hb])
            bk, bb = self.bank()
            bkb = bk[:].bitcast(BF16)
            for k in range(8):
                self.tr(bkb[:, k * 128:(k + 1) * 128], h[:, k * 128:(k + 1) * 128], self.ident_b[:], [hb, cb], [bb])
            hT, hTb = hT_r.get()
            self.cp("act", hT[:].rearrange("p k t -> p (k t)"), bkb[:, :], [bb], [hTb])
            def proj(c0, c1):
                bk, bb = self.bank()
                for k in range(8):
                    self.mm(bk[:, 0:c1 - c0], hT[:, k, :], W[:, k, c0:c1], k == 0, k == 7, [hTb, Wb], [bb])
                return bk, bb

            def proj2(c0):
                (b1, bb1), (b2, bb2) = self.bank(), self.bank()
                for k in range(8):
                    self.mm(b1[:, :], hT[:, k, :], W[:, k, c0:c0 + 512], k == 0, k == 7, [hTb, Wb], [bb1])
                    self.mm(b2[:, :], hT[:, k, :], W[:, k, c0 + 512:c0 + 1024], k == 0, k == 7, [hTb, Wb], [bb2])
                return (b1, bb1), (b2, bb2)
            qk, qkb = qk_r.get()
            sq, sqb = sq_r.get()
            qkf = qk[:].rearrange("p h d -> p (h d)")
            sqf = sq[:].rearrange("p h d -> p (h d)")
            (Pq, Pqb), (Pk, Pkb) = proj2(0)
            self.cp("act", qkf[:, 0:512], Pq[:, 0:512], [Pqb], [qkb])
            self.cp("act", qkf[:, 512:640], Pk[:, 0:128], [Pkb], [qkb])
            vf, vfb = vf_r.get()
            self.cp("dve", vf[:], Pk[:, 128:512], [Pkb], [vfb])
            (Pg1, Pg1b), (Pg2, Pg2b) = proj2(1024)
            g1s, g1sb = g1s_r.get()
            self.cp("act", g1s[:], Pg1[:, :], [Pg1b], [g1sb])
            vg, vgb = vg_r.get()
            self.cp("act", vg[:], Pg2[:, 0:256], [Pg2b], [vgb])
            grs, grsb = grs_r.get()
            self.cp("dve", grs[:], Pg2[:, 256:512], [Pg2b], [grsb])
            Pz, Pzb = proj(2048, 2080)
            if is_ctx:
                self.dma("sp", VC[tc * 128:(tc + 1) * 128, :], vf[:, 0:128], [vfb], [db["VC" + L]])
                self.dma("sp", FUC[tc * 128:(tc + 1) * 128, :], vf[:, 128:384], [vfb], [db["FUC" + L]])
            else:
                self.dma("sp", VP[(t + 1) * 128:(t + 2) * 128, :], vf[:, 0:128], [vfb], [db["VP" + L]])
                self.dma("sp", FUL[t * 128:(t + 1) * 128, :], vf[:, 128:384], [vfb], [db["FUL" + L]])
            z, zb_ = z_r.get()
            self.cp("act", z[:], Pz[:, 0:32], [Pzb], [zb_])
            self.act(sqf[:, :], qkf[:, :], AF.Square, [qkb], [sqb])
            sm, smb = sm_r.get()
            self.reduce_x(sm[:, 0, 0:10], sq[:], [sqb], [smb])
            self.rstd(sm[:, 2, 0:10], sm[:, 0, 0:10], 1.0 / 64, [smb, self.eps_b], [smb], sm[:, 1, 0:10], smb)
            qn, qnb = qn_r.get()
            self.tt("dve", qn[:], qk[:], sm[:, 2, 0:10].unsqueeze(2).broadcast_to([128, 10, 64]), ALU.mult, [qkb, smb], [qnb])
            self.tt("pool", qn[:], qn[:], gqk[:], ALU.mult, [qnb, cb], [qnb])
            qr, qrb = qr_r.get()
            if not is_ctx:
                rp, rpb = rp_r.get()
                self.dma("act", rp[:], rope_d[:, t * 128:(t + 1) * 128, :].rearrange("c t d -> t c d"), [], [rpb])
                t1, t1b = t1_r.get()
                t2, t2b = t2_r.get()
                self.tt("dve", t1[:], qn[:], rp[:, 0, :].unsqueeze(1).broadcast_to([128, 10, 64]), ALU.mult, [qnb, rpb], [t1b])
                qn5 = qn[:].rearrange("p h (a s c) -> p (h a) s c", a=2, s=2)
                t25 = t2[:].rearrange("p h (a s c) -> p (h a) s c", a=2, s=2)
                self.cp("pool", t25[:, :, 0, :], qn5[:, :, 1, :], [qnb], [t2b])
                self.cp("pool", t25[:, :, 1, :], qn5[:, :, 0, :], [qnb], [t2b])
                self.tt("pool", t2[:], t2[:], rp[:, 1, :].unsqueeze(1).broadcast_to([128, 10, 64]), ALU.mult, [t2b, rpb], [t2b])
                self.tt("dve", qr[:], t1[:], t2[:], ALU.add, [t1b, t2b], [qrb])
            else:
                self.cp("dve", qr[:], qn[:], [qnb], [qrb])
            bk, bb = self.bank()
            bkb = bk[:].bitcast(BF16)
            qrf = qr[:].rearrange("p h d -> p (h d)")
            for j in range(5):
                self.tr(bkb[:, j * 128:(j + 1) * 128], qrf[:, j * 128:(j + 1) * 128], self.ident_b[:], [qrb, cb], [bb])
            qkT, qkTb = qkT_r.get()
            self.cp("act", qkT[:].rearrange("p j t -> p (j t)"), bkb[:, 0:640], [bb], [qkTb])
            self.dma("sp", QT[:, :, t * 128:(t + 1) * 128], qkT[:, 0:4, :], [qkTb], [db["QT" + L]])
            if is_ctx:
                self.dma("sp", KTC[:, tc * 128:(tc + 1) * 128], qkT[:, 4, :], [qkTb], [db["KTC" + L]])
            else:
                self.dma("sp", KTP[:, (t + 1) * 128:(t + 2) * 128], qkT[:, 4, :], [qkTb], [db["KTP" + L]])
            S.rec.append(("mark",))
            bk, bb = self.bank()
            for dr in range(2):
                self.tr(bk[0:16, dr * 128:(dr + 1) * 128], z[:, dr * 16:(dr + 1) * 16], self.ident_f[:], [zb_, cb], [bb])
            self.cp("dve", zT[0:16, :, :].rearrange("p a t -> p (a t)"), bk[0:16, 0:256], [bb], [zTb])
            bk, bb = self.bank()
            for dr in range(2):
                self.mm(bk[:, dr * 256:(dr + 1) * 256], zT[:, dr, :], wg[:, dr, :], True, True, [zTb, cb], [bb])
            Lt, Lb = L_r.get()
            Lf = Lt[:].rearrange("p a c -> p (a c)")
            self.act(Lf, bk[:, :], AF.Exp, [bb], [Lb], scale=-1.0)
            self.act(Lf, Lf, AF.Ln, [Lb, self.eps_b], [Lb], bias=self.eps_t[:, 1:2])
            c1k, c1b = self.bank()
            c2k, c2b = self.bank()
            self.mm(c1k[:, 0:256], cm[:, 0, :], Lt[:, 0, :], True, True, [cb, Lb], [c1b])
            self.mm(c1k[:, 256:512], cm[:, 2, :], Lt[:, 1, :], True, True, [cb, Lb], [c1b])
            self.mm(c2k[:, 0:256], cm[:, 1, :], Lt[:, 0, :], True, True, [cb, Lb], [c2b])
            self.mm(c2k[:, 256:512], cm[:, 3, :], Lt[:, 1, :], True, True, [cb, Lb], [c2b])
            dk, dkb = self.bank()
            for dr in range(2):
                for pr in range(2):
                    i0 = (dr * 2 + pr) * 2
                    self.mm(dk[:, i0:i0 + 2], Lt[:, dr, pr * 128:(pr + 1) * 128], ci[:, :], True, True, [Lb, cb], [dkb])
            dec, decb = dec_r.get()
            self.act(dec[:], dk[:, 0:8], AF.Exp, [dkb], [decb])
            self.dma("sp", DECd[t], dec[:], [decb], [scb_[t]])
            E, Eb = E_r.get()
            self.act(E[:, 0, :], c1k[:, :], AF.Exp, [c1b, self.eps_b], [Eb], bias=self.eps_t[:, 2:3])
            self.act(E[:, 1, :], c1k[:, :], AF.Exp, [c1b], [Eb], scale=-1.0)
            self.act(E[:, 2, :], c2k[:, :], AF.Exp, [c2b], [Eb])
            qi, qib = qi_r.get()
            gq_b = g1s[:, 0:256].unsqueeze(1).broadcast_to([128, 2, 256])
            gk_b = g1s[:, 256:512].unsqueeze(1).broadcast_to([128, 2, 256])
            Pg1b = g1sb
            self.tt("dve", qi[:, 0, :, :], gq_b, E[:, 0, :].rearrange("p (a c) -> p a c", a=2), ALU.mult, [Pg1b, Eb], [qib])
            self.tt("dve", qi[:, 1, :, :], gk_b, E[:, 1, :].rearrange("p (a c) -> p a c", a=2), ALU.mult, [Pg1b, Eb], [qib])
            ko, kob = ko_r.get()
            self.tt("dve", ko[:], gk_b, E[:, 2, :].rearrange("p (a c) -> p a c", a=2), ALU.mult, [Pg1b, Eb], [kob])
            self.dma("sp", KOUTd[t], ko[:], [kob], [scb_[t]])
            self.dma("sp", VGd[t], vg[:], [vgb], [scb_[t]])
            sg, sgb = sg_r.get()
            self.act(sg[:], grs[:], AF.Exp, [grsb], [sgb], scale=-1.0)
            self.act(sg[:], sg[:], AF.Ln, [sgb, self.eps_b], [sgb], bias=self.eps_t[:, 1:2])
            self.act(sg[:], sg[:], AF.Exp, [sgb], [sgb], scale=-1.0)
            srt, srb = sr_r.get()
            self.tt("dve", srt[:], sg[:], grs[:], ALU.mult, [sgb, grsb], [srb])
            self.dma("sp", SR[t * 128:(t + 1) * 128, :], srt[:], [srb], [db["SR" + L]])
            bk, bb = self.bank()
            bkb = bk[:].bitcast(BF16)
            for wh in range(2):
                for dr in range(2):
                    for pr in range(2):
                        i0 = wh * 4 + dr * 2 + pr
                        self.tr(bkb[:, i0 * 128:(i0 + 1) * 128], qi[:, wh, dr, pr * 128:(pr + 1) * 128], self.ident_b[:], [qib, cb], [bb])
            kiT, kiTb = hT_r.get()
            qint, qintb = qint_r.get()
            self.cp("act", qint[:].rearrange("p a t -> p (a t)"), bkb[:, 0:512], [bb], [qintb])
            self.cp("act", kiT[:, 0:4, :].rearrange("p a t -> p (a t)"), bkb[:, 512:1024], [bb], [kiTb])
            self.dma("sp", QINTd[t], qint[:], [qintb], [scb_[t]])
            a1k, a1b = self.bank()
            a2k, a2b = self.bank()
            for half in range(2):
                ak, ab = (a1k, a1b) if half == 0 else (a2k, a2b)
                p0 = half * 64
                for dr in range(2):
                    for pr in range(2):
                        c0 = (dr * 2 + pr) * 128
                        self.mm(ak[:, c0:c0 + 128], kiT[p0:p0 + 64, dr * 2 + pr, :], qint[p0:p0 + 64, dr * 2 + pr, :],
                                True, True, [kiTb, qintb], [ab])
            aT, aTb = aT_r.get()
            aT5 = aT[:].rearrange("p (d r f) t -> p d r f t", d=2, r=2, f=2)
            for half in range(2):
                ak, ab = (a1k, a1b) if half == 0 else (a2k, a2b)
                for dr in range(2):
                    self.tt("dve", aT5[:, dr, :, half, :], ak[:, dr * 256:(dr + 1) * 256].rearrange("p (r t) -> p r t", r=2),
                            gm[:, dr, :].unsqueeze(1).broadcast_to([128, 2, 128]), ALU.mult, [ab, cb], [aTb])
            ok, ob = self.bank()
            for hd in range(4):
                for dr in range(2):
                    self.mm(ok[:, hd * 64:(hd + 1) * 64], aT[:, dr * 4 + hd, :], vg[:, hd * 64:(hd + 1) * 64], dr == 0, dr == 1,
                            [aTb, vgb], [ob])
            olt, oltb = ol_r.get()
            self.cp("act", olt[:], ok[:, 0:256], [ob], [oltb])
            self.dma("sp", OLOC[t * 128:(t + 1) * 128, :], olt[:], [oltb], [olb[t]])

        self.pipeline(tileA, list(range(nt)))
        self.brange = (0, 8)
        Sst = A("Sst", [128, 2, 2, 128], F32)
        Sbf = A("Sbf", [128, 2, 2, 128], BF16)
        S0 = A("S0", [128, 2, 2, 128], F32)
        sb_ = [[Buf() for _ in range(2)] for _ in range(2)]
        sbb = [[Buf() for _ in range(2)] for _ in range(2)]
        s0b = Buf()
        um_r = Ring(A, "um", [128, 2, 128], F32, 6)
        sko_r = Ring(A, "sko", [128, 256], BF16, 6)
        svg_r = Ring(A, "svg", [128, 256], BF16, 6)
        sqn_r = Ring(A, "sqn", [128, 2, 128], BF16, 6)
        sdc_r = Ring(A, "sdc", [128, 4], F32, 6)
        sol_r = Ring(A, "sol", [128, 256], F32, 6)

        def scan(tiles, init):
            for dr in range(2):
                for pr in range(2):
                    if init is None:
                        S.op("pool", lambda e: e.memset(Sst[:, dr, pr, :], 0.0), [], [sb_[dr][pr]])
                    else:
                        self.cp("pool", Sst[:, dr, pr, :], S0[:, dr, pr, :], [s0b], [sb_[dr][pr]])
                    self.cp("act", Sbf[:, dr, pr, :], Sst[:, dr, pr, :], [sb_[dr][pr]], [sbb[dr][pr]])
            for idx in range(len(tiles)):
                for dr in range(2):
                    t = tiles[idx] if dr == 0 else tiles[len(tiles) - 1 - idx]
                    ko, kob = sko_r.get()
                    self.dma("act", ko[:], KOUTd[t, :, dr, :], [scb_[t]], [kob])
                    vg, vgb = svg_r.get()
                    self.dma("act", vg[:], VGd[t], [scb_[t]], [vgb])
                    qn, qnb = sqn_r.get()
                    self.dma("act", qn[:], QINTd[t, :, dr * 2:(dr + 1) * 2, :], [scb_[t]], [qnb])
                    dc, dcb = sdc_r.get()
                    self.dma("act", dc[:], DECd[t, :, dr * 4:(dr + 1) * 4], [scb_[t]], [dcb])
                    ol, ol_b = sol_r.get()
                    self.dma("act", ol[:], OLOC[t * 128:(t + 1) * 128, :], [olb[t]], [ol_b])
                    chs = list(range(128 // GCH))
                    for ch in (chs if dr == 0 else chs[::-1]):
                        r0 = ch * GCH
                        ik, ib = self.bank()
                        for pr in range(2):
                            self.mm(ik[:, pr * 128:(pr + 1) * 128], qn[:, pr, :], Sbf[:, dr, pr, :], True, True, [qnb, sbb[dr][pr]], [ib])
                        self.tt("dve", ol[r0:r0 + GCH, :], ol[r0:r0 + GCH, :], ik[r0:r0 + GCH, 0:256], ALU.add, [ol_b, ib], [ol_b])
                        uk, ub = self.bank()
                        for pr in range(2):
                            self.mm(uk[:, pr * 128:(pr + 1) * 128], ko[r0:r0 + GCH, pr * 128:(pr + 1) * 128], vg[r0:r0 + GCH, pr * 128:(pr + 1) * 128],
                                    True, True, [kob, vgb], [ub])
                        um, umb = um_r.get()
                        self.tt("dve", um[:], uk[:, 0:256].rearrange("p (a e) -> p a e", a=2), bd[:].unsqueeze(1).broadcast_to([128, 2, 128]), ALU.mult, [ub, cb], [umb])
                        for pr in range(2):
                            dcol = dc[:, pr * 2 + ch:pr * 2 + ch + 1]
                            self.stt(Sst[:, dr, pr, :], Sst[:, dr, pr, :], dcol, um[:, pr, :], ALU.mult, ALU.add,
                                     [sb_[dr][pr], dcb, umb], [sb_[dr][pr]])
                            self.cp("pool", Sbf[:, dr, pr, :], Sst[:, dr, pr, :], [sb_[dr][pr]], [sbb[dr][pr]])
                    self.dma("sp", OLOC[t * 128:(t + 1) * 128, :], ol[:], [ol_b], [olb[t]])

        scan(list(range(ntl, nt)), None)
        allS = [sb_[a][b] for a in range(2) for b in range(2)]
        self.cp("dve", S0[:], Sst[:], allS, [s0b])
        scan(list(range(ntl)), S0)
        S.barrier()
        self.sb.release(pmark)

    def own_gather(self, l, X1, xob0):
        nc = self.nc
        ao = self.aout[l]
        db = self.dbuf
        L = "%d" % l
        pid = nc.sync.partition_id()
        tok0 = (pid % 4) * TOK
        X1o = self.dram("X1own", [TOK, D])
        QTo = self.dram("QTown", [128, 4, TOK], BF16)
        KTPo = self.dram("KTPown", [128, (NTL + 2) * 128], BF16)
        VPo = self.dram("VPown", [(NTL + 2) * 128, 128], BF16)
        OLo = self.dram("OLown", [TOK, 256])
        SRo = self.dram("SRown", [TOK, 256], BF16)
        self.dma("sp", X1o, X1[bass.ds(tok0, TOK), :], list(xob0), [db["X1own"]])
        self.dma("sp", QTo, ao["QT"][:, :, bass.ds(tok0, TOK)], [db["QT" + L]], [db["QTown"]])
        self.dma("sp", KTPo, ao["KTP"][:, bass.ds(tok0, (NTL + 2) * 128)], [db["KTP" + L]], [db["KTPown"]])
        self.dma("sp", VPo, ao["VP"][bass.ds(tok0, (NTL + 2) * 128), :], [db["VP" + L]], [db["VPown"]])
        self.dma("sp", OLo, ao["OLOC"][bass.ds(tok0, TOK), :], list(ao["olb"]), [db["OLown"]])
        self.dma("sp", SRo, ao["SR"][bass.ds(tok0, TOK), :], [db["SR" + L]], [db["SRown"]])
        own = dict(ao)
        own.update(QT=QTo, KTP=KTPo, VP=VPo, OLOC=OLo, SR=SRo, olb=[db["OLown"]] * NTL)
        self.aout["own"] = own
        self.ownbufs = dict(QT=db["QTown"], KTP=db["KTPown"], VP=db["VPown"], SR=db["SRown"])
        return X1o, db["X1own"]

    def phase_B_f(self, l, mode, xin, xin_b):
        nc, S, A = self.nc, self.S, self.alloc
        db, cb = self.dbuf, self.cb
        own = mode == "own"
        ao = self.aout["own"] if own else self.aout[l]
        L = "%d" % l
        with_ctx = not own
        ntl = NTL if own else NTLF
        ntiles = ntl + (NTC if with_ctx else 0)
        tiles = list(range(ntiles))
        nvar = 2 if with_ctx else 1
        gtok = lambda t: slice(t * 128, (t + 1) * 128)
        kpad = lambda t: slice(t * 128, (t + 3) * 128)
        ltok = lambda t: slice(t * 128, (t + 1) * 128)
        MODV = self.modv[l]
        QT, KTP, KTC, VP, VC, FUL, FUC, OLOC, SR = [ao[k] for k in ("QT", "KTP", "KTC", "VP", "VC", "FUL", "FUC", "OLOC", "SR")]
        olb = ao["olb"]
        rb = (lambda k: self.ownbufs[k]) if own else (lambda k: db[k + L])
        amask_d = self.inp("amask%d" % (1 if own else 0), [128, 4, 128])
        TAB = self.inp("TABOWN", [2, 4 * 8, 128, 8 * 512], BF16) if own else self.inp("TAB", [2, 16 * 8, 128, 8 * 512], BF16)
        TABC = self.inp("TABC", [2, 1, 128, 2 * 256], BF16)
        CB_d = self.inp("CB", [128, 2, 128])
        wf_d = self.inp("wf" + L, [4, 64, 64])
        glag_d = self.inp("glag" + L, [64])
        sink_d = self.inp("sink" + L, [8])
        wout_d = self.inp("wout" + L, [D, D])
        g2_d = self.inp("g2" + L, [D])
        wffi_d = self.inp("wffi" + L, [D, 2 * HID])
        wffo_d = self.inp("wffo" + L, [HID, D])
        if own:
            XO = self.outp("XOUT", [TOK, D])
        else:
            XO = self.dram("X1", [NTOKF, D])
        FOURT = self.dram("FOURT" + L, [2, 128, ntiles * 128], BF16)
        H2T = self.dram("H2T" + L, [ntiles, 128, 8, 128], BF16)
        fourb = db["FOURT" + L]
        h2tb = [Buf() for _ in range(ntiles)]
        xob = [Buf() for _ in range(ntiles)]
        self.xob = xob
        pmark = self.sb.mark()

        Wi = A("w_ffi", [128, 8, 2 * HID], BF16, high=True)
        Wib = Buf()
        for k in range(8):
            for c0 in range(0, 2 * HID, 1408):
                self.dma("pool", Wi[:, k, c0:c0 + 1408], wffi_d[k * 128:(k + 1) * 128, c0:c0 + 1408], [], [Wib])

        amask = A("amask", [128, 4, 128], BF16)
        self.dma("pool", amask[:], amask_d, [], [cb])
        exs = A("exs", [128, 8], F32)
        self.dma("sp", exs[:], sink_d.partition_broadcast(128), [], [cb])
        self.act(exs[:], exs[:], AF.Exp, [cb], [cb])
        glag = A("glag", [128, 64], F32)
        self.dma("sp", glag[:], glag_d.partition_broadcast(128), [], [cb])
        MODB = A("modbB", [128, nvar, 4, D], F32)
        MODBb = Buf()
        mk = self.sb.mark()
        rows = A("rowsB", [1, 5, D], F32)
        rowsb = Buf()
        for var in range(nvar):
            self.dma("sp", rows[0:1, 0, :], g2_d.rearrange("(o n) -> o n", o=1), [], [rowsb])
            for (ri, c0) in ((1, 2 * D), (2, 4 * D), (3, 3 * D), (4, 5 * D)):
                self.dma("sp", rows[0:1, ri, :], MODV[var:var + 1, c0:c0 + D], [db["MODV" + L]], [rowsb])
            self.stt(rows[0:1, 2, :], rows[0:1, 2, :], 1.0, rows[0:1, 0, :], ALU.add, ALU.mult, [rowsb], [rowsb])
            for (ri, slot) in ((1, 0), (2, 1), (3, 2), (4, 3)):
                for (bk, bb, c0, c1) in self.bcast_row(rows[0:1, ri, :], rowsb, D):
                    self.cp("act", MODB[:, var, slot, c0:c1], bk[:, 0:c1 - c0], [bb], [MODBb])
        S.barrier()
        self.sb.release(mk)

        fmark = self.sb.mark()
        U = A("uall", [128, 64, 256], BF16)
        Ub = Buf()
        for q4 in range(4):
            self.dma("sp", U[:, q4 * 16:(q4 + 1) * 16, :], FUL[q4 * 2048:(q4 + 1) * 2048, :].rearrange("(k p) c -> p k c", p=128), [db["FUL" + L]], [Ub])
        CBs = A("CBs", [128, 2, 128], F32)
        self.dma("sp", CBs[:], CB_d, [], [cb])
        wblk = A("wblk", [128, 2, 128], F32)
        wblkb = Buf()
        S.op("dve", lambda e: e.memset(wblk[:], 0.0), [], [wblkb])
        for g in range(4):
            hf, gi = g // 2, g % 2
            self.dma("sp", wblk[gi * 64:(gi + 1) * 64, hf, gi * 64:(gi + 1) * 64], wf_d[g], [wblkb], [wblkb])
        Mblk = A("Mblk", [128, 2, 2, 128], BF16)
        Mb = Buf()
        for hf in range(2):
            for cs in range(2):
                bk, bb = self.bank()
                self.mm(bk[:, 0:128], CBs[:, cs, :], wblk[:, hf, :], True, True, [cb, wblkb], [bb])
                self.cp("act", Mblk[:, hf, cs, :], bk[:, 0:128], [bb], [Mb])
        tab_r = Ring(A, "tab", [128, 8, 512], BF16, 4)
        abt_r = Ring(A, "abt", [128, 2, 2, 512], BF16, 2)
        ft_r = Ring(A, "ft", [128, 512], BF16, 2)
        tabq = [0]

        def fnet(Usb, Usbb, ntc, tabsrc, nblk, blk, col0):
            tg = min(8, ntc)
            for b in range(nblk):
                abt, abtb = abt_r.get()
                acc = [[self.bank(), self.bank()], [self.bank(), self.bank()]]
                for g0 in range(0, ntc, tg):
                    tabs = []
                    for cs in range(2):
                        tab, tabb = tab_r.get()
                        tabq[0] += 1
                        self.dma("sp" if tabq[0] % 2 else "act", tab[:].rearrange("p k c -> p (k c)")[:, 0:tg * blk], tabsrc(cs, b, g0, tg), [], [tabb])
                        tabs.append((tab, tabb))
                    for tc in range(tg):
                        for hf in range(2):
                            for cs in range(2):
                                tab, tabb = tabs[cs]
                                self.mm(acc[hf][cs][0][:, 0:blk], Usb[:, g0 + tc, hf * 128:(hf + 1) * 128],
                                        tab[:].rearrange("p k c -> p (k c)")[:, tc * blk:(tc + 1) * blk],
                                        g0 + tc == 0, g0 + tc == ntc - 1, [Usbb, tabb], [acc[hf][cs][1]])
                for hf in range(2):
                    for cs in range(2):
                        self.cp("act" if cs == 0 else "dve", abt[:, hf, cs, 0:blk], acc[hf][cs][0][:, 0:blk], [acc[hf][cs][1]], [abtb])
                for hf in range(2):
                    bk, bb = self.bank()
                    for cs in range(2):
                        self.mm(bk[:, 0:blk], Mblk[:, hf, cs, :], abt[:, hf, cs, 0:blk], cs == 0, cs == 1, [Mb, abtb], [bb])
                    ft, ftb = ft_r.get()
                    self.cp("act", ft[:, 0:blk], bk[:, 0:blk], [bb], [ftb])
                    self.dma("sp", FOURT[hf, :, col0 + b * blk:col0 + (b + 1) * blk], ft[:, 0:blk], [ftb], [fourb])

        if own:
            fnet(U, Ub, 64, lambda cs, b, g0, tg: TAB[cs, b * 8 + g0 // 8], 4, 512, 0)
        else:
            fnet(U, Ub, 64, lambda cs, b, g0, tg: TAB[cs, b * 8 + g0 // 8], 16, 512, 0)
            Uc = A("uc", [128, 2, 256], BF16)
            Ucb = Buf()
            self.dma("sp", Uc[:], FUC.rearrange("(k p) c -> p k c", p=128), [db["FUC" + L]], [Ucb])
            fnet(Uc, Ucb, 2, lambda cs, b, g0, tg: TABC[cs, b], 1, 256, SEQ)
        S.barrier()
        self.sb.release(fmark)

        b1mark = self.sb.mark()
        Wo = A("w_out", [128, 8, D], BF16)
        Wob = Buf()
        for k in range(8):
            self.dma("pool", Wo[:, k, :], wout_d[k * 128:(k + 1) * 128, :], [], [Wob])
        KC = A("kc", [128, 2, 128], BF16)
        self.dma("sp", KC[:].rearrange("p k t -> p (k t)"), KTC, [db["KTC" + L]], [cb])
        VCa = A("vca", [128, 2, 2, 65], BF16)
        S.op("pool", lambda e: e.memset(VCa[:, :, :, 64:65], 1.0), [], [cb])
        for g in range(2):
            self.dma("sp", VCa[:, :, g, 0:64], VC[:, g * 64:(g + 1) * 64].rearrange("(k p) d -> p k d", p=128), [db["VC" + L]], [cb])
        def mkringsB(tag):
            kt_r = Ring(A, "ktR" + tag, [128, 3, 128], BF16, 1)
            ve_r = Ring(A, "veR" + tag, [128, 3, 2, 65], BF16, 1)
            xt_r = Ring(A, "xtB" + tag, [128, D], F32, 1)
            qt_r = Ring(A, "qtB" + tag, [128, 4, 128], BF16, 1)
            pT_r = Ring(A, "pT" + tag, [128, 5, 512], BF16, 1)
            den_r = Ring(A, "den" + tag, [128, 8], F32, 1)
            att_r = Ring(A, "att" + tag, [128, 8, 64], BF16, 1)
            mix_r = Ring(A, "mixT" + tag, [128, 8, 128], BF16, 1)
            srr = Ring(A, "srB" + tag, [128, 256], BF16, 1)
            o_r = Ring(A, "oB" + tag, [128, 4, 64], F32, 1)
            sq_r = Ring(A, "sqB" + tag, [128, 4, 64], F32, 1)
            sm_r = Ring(A, "smB" + tag, [128, 3, 4], F32, 1)
            y_r = Ring(A, "yB" + tag, [128, 4, 64], BF16, 1)
            tmp_r = Ring(A, "tmpB" + tag, [128, D], F32, 1)
            xn_r = Ring(A, "xnB" + tag, [128, 8], F32, 1)
            junk_r = Ring(A, "junkB" + tag, [128, 8], BF16, 1)
            st_r = Ring(A, "stB" + tag, [128, 4], F32, 1)
            h2_r = Ring(A, "h2B" + tag, [128, D], BF16, 1)
            h2T_r = Ring(A, "h2TB" + tag, [128, 8, 128], BF16, 1)
            for (ve, veb) in ve_r.items:
                S.op("pool", lambda e: e.memset(ve[:, :, :, 64:65], 1.0), [], [veb])
            return (kt_r, ve_r, xt_r, qt_r, pT_r, den_r, att_r, mix_r, srr, o_r, sq_r, sm_r, y_r, tmp_r, xn_r, junk_r, st_r, h2_r, h2T_r)

        RB = [mkringsB("a"), mkringsB("b")]

        def tileB(t):
            par = t % 2
            self.brange = (0, 4) if par == 0 else (4, 8)
            (kt_r, ve_r, xt_r, qt_r, pT_r, den_r, att_r, mix_r, srr, o_r, sq_r, sm_r, y_r, tmp_r, xn_r, junk_r, st_r, h2_r, h2T_r) = RB[par]
            is_ctx = t >= ntl
            var = 1 if is_ctx else 0
            qt, qtb = qt_r.get()
            self.dma("act", qt[:], QT[:, :, gtok(t)], [rb("QT")], [qtb])
            xt, xtb = xt_r.get()
            self.dma("act", xt[:], xin[gtok(t), :], xin_b(t), [xtb])
            o, ob = o_r.get()
            of = o[:].rearrange("p h e -> p (h e)")
            self.dma("act", of, OLOC[gtok(t), :], [olb[t]], [ob])
            sr, srb = srr.get()
            self.dma("act", sr[:], SR[gtok(t), :], [rb("SR")], [srb])
            mix, mixb = mix_r.get()
            self.dma("act", mix[:, 4:6, :], FOURT[:, :, ltok(t)].rearrange("h p t -> p h t"), [fourb], [mixb])
            keys = []
            if not is_ctx:
                kt, ktb = kt_r.get()
                self.dma("act", kt[:].rearrange("p k t -> p (k t)"), KTP[:, kpad(t)], [rb("KTP")], [ktb])
                ve, veb = ve_r.get()
                for g in range(2):
                    self.dma("act", ve[:, :, g, 0:64], VP[kpad(t), g * 64:(g + 1) * 64].rearrange("(k p) d -> p k d", p=128), [rb("VP")], [veb])
                m0 = 0 if t == 0 else 1
                m2 = 3 if t == ntl - 1 else 2
                keys += [(kt, ve, 0, ktb, veb, m0), (kt, ve, 1, ktb, veb, None), (kt, ve, 2, ktb, veb, m2)]
            keys += [(KC, VCa, 0, cb, cb, None), (KC, VCa, 1, cb, cb, None)]
            nk = len(keys)
            pv = [self.bank(pin=True), self.bank(pin=True)]
            for g in range(2):
                pT, pTb = pT_r.get()
                for i, (kten, vten, ki, kbuf, vbuf, mi) in enumerate(keys):
                    bk, bb = self.bank()
                    self.mm(bk[:, :], kten[g * 64:(g + 1) * 64, ki, :], qt[g * 64:(g + 1) * 64, :, :].rearrange("p j t -> p (j t)"),
                            True, True, [kbuf, qtb], [bb])
                    self.act(pT[:, i, :], bk[:, :], AF.Exp, [bb], [pTb], scale=0.125)
                    if mi is not None:
                        self.tt("pool" if g == 0 else "dve", pT[:, i, :].rearrange("p (j t) -> p j t", j=4), pT[:, i, :].rearrange("p (j t) -> p j t", j=4),
                                amask[:, mi, :].unsqueeze(1).broadcast_to([128, 4, 128]), ALU.mult, [pTb, cb], [pTb])
                for j in range(4):
                    for i, (kten, vten, ki, kbuf, vbuf, mi) in enumerate(keys):
                        self.mm(pv[g][0][:, j * 65:(j + 1) * 65], pT[:, i, j * 128:(j + 1) * 128], vten[:, ki, g, :], i == 0, i == nk - 1,
                                [pTb, vbuf], [pv[g][1]])
            S.rec.append(("mark",))
            den, denb = den_r.get()
            att, attb = att_r.get()
            for g in range(2):
                pvv = pv[g][0][:, 0:260].rearrange("p (j c) -> p j c", c=65)
                self.tt("dve", den[:, g * 4:(g + 1) * 4].unsqueeze(2), pvv[:, :, 64:65], exs[:, g * 4:(g + 1) * 4].unsqueeze(2), ALU.add,
                        [pv[g][1], cb], [denb])
            self.recip(den[:], den[:], [denb], [denb])
            for g in range(2):
                pvv = pv[g][0][:, 0:260].rearrange("p (j c) -> p j c", c=65)
                self.tt("dve", att[:, g * 4:(g + 1) * 4, :], pvv[:, :, 0:64], den[:, g * 4:(g + 1) * 4].unsqueeze(2).broadcast_to([128, 4, 64]),
                        ALU.mult, [pv[g][1], denb], [attb])
            self.unpin(pv[0][0])
            self.unpin(pv[1][0])
            sq, sqb = sq_r.get()
            self.tt("pool", sq[:].rearrange("p h e -> p (h e)"), of, of, ALU.mult, [ob], [sqb])
            sm, smb = sm_r.get()
            self.reduce_x(sm[:, 0, :], sq[:], [sqb], [smb])
            self.rstd(sm[:, 2, :], sm[:, 0, :], 1.0 / 64, [smb, self.eps_b], [smb], sm[:, 1, :], smb)
            self.tt("dve", o[:], o[:], sm[:, 2, :].unsqueeze(2).broadcast_to([128, 4, 64]), ALU.mult, [ob, smb], [ob])
            self.tt("pool", o[:], o[:], glag[:].unsqueeze(1).broadcast_to([128, 4, 64]), ALU.mult, [ob, cb], [ob])
            y, yb = y_r.get()
            self.tt("dve", y[:].rearrange("p h e -> p (h e)"), of, sr[:], ALU.mult, [ob, srb], [yb])
            bk, bb = self.bank()
            bkb = bk[:].bitcast(BF16)
            attf = att[:].rearrange("p h e -> p (h e)")
            yf = y[:].rearrange("p h e -> p (h e)")
            for k in range(4):
                self.tr(bkb[:, k * 128:(k + 1) * 128], attf[:, k * 128:(k + 1) * 128], self.ident_b[:], [attb, cb], [bb])
            for k in range(2):
                self.tr(bkb[:, (4 + k) * 128:(5 + k) * 128], yf[:, k * 128:(k + 1) * 128], self.ident_b[:], [yb, cb], [bb])
            self.cp("dve", mix[:, 0:4, :].rearrange("p k t -> p (k t)"), bkb[:, 0:512], [bb], [mixb])
            self.cp("dve", mix[:, 6:8, :].rearrange("p k t -> p (k t)"), bkb[:, 512:768], [bb], [mixb])
            xn, xnb = xt, xtb
            tmp, tmpb = tmp_r.get()
            for hfc in range(2):
                bk, bb = self.bank()
                for k in range(8):
                    self.mm(bk[:, :], mix[:, k, :], Wo[:, k, hfc * 512:(hfc + 1) * 512], k == 0, k == 7, [mixb, Wob], [bb])
                self.tt("dve", tmp[:, hfc * 512:(hfc + 1) * 512], bk[:, :], MODB[:, var, 0, hfc * 512:(hfc + 1) * 512], ALU.mult, [bb, MODBb], [tmpb])
            self.tt("pool", xn[:], tmp[:], xt[:], ALU.add, [tmpb, xtb], [xnb])
            self.dma("sp", XO[ltok(t), :], xn[:], [xnb], [xob[t]])
            st, stb = st_r.get()
            h2, h2b = h2_r.get()
            self.act(h2[:], xn[:], AF.Square, [xnb], [h2b, stb], accum_out=st[:, 0:1])
            self.rstd(st[:, 2:3], st[:, 0:1], 1.0 / D, [stb, self.eps_b], [stb], st[:, 1:2], stb)
            tmp, tmpb = tmp_r.get()
            self.stt(tmp[:], xn[:], st[:, 2:3], MODB[:, var, 1, :], ALU.mult, ALU.mult, [xnb, stb, MODBb], [tmpb])
            self.tt("pool", h2[:], tmp[:], MODB[:, var, 2, :], ALU.add, [tmpb, MODBb], [h2b])
            bk, bb = self.bank()
            bkb = bk[:].bitcast(BF16)
            for k in range(8):
                self.tr(bkb[:, k * 128:(k + 1) * 128], h2[:, k * 128:(k + 1) * 128], self.ident_b[:], [h2b, cb], [bb])
            h2T, h2Tb = h2T_r.get()
            self.cp("act", h2T[:].rearrange("p k t -> p (k t)"), bkb[:, :], [bb], [h2Tb])
            self.dma("sp", H2T[t], h2T[:], [h2Tb], [h2tb[t]])
        self.pipeline(tileB, tiles)
        self.brange = (0, 8)
        S.barrier()
        self.sb.release(b1mark)

        Wf = A("w_ffo", [128, NCH, D], BF16, high=True)
        Wfb = Buf()
        for j in range(NCH):
            self.dma("pool", Wf[:, j, :], wffo_d[j * 128:(j + 1) * 128, :], [], [Wfb])
        hp_r = Ring(A, "hp", [128, 8, 256], BF16, 2)
        xp_r = Ring(A, "xp", [128, 2, D], F32, 2)
        sg_r = Ring(A, "sgF", [128, 256], BF16, 2)
        ac_r = Ring(A, "acF", [128, 256], BF16, 3)
        tm_r = Ring(A, "tmF", [128, D], F32, 1)
        xo_r = Ring(A, "xoF", [128, D], F32, 2)
        accs = self.banks[0:4]
        self.brange = (4, 8)
        for p in range(ntiles // 2):
            t0 = 2 * p
            var = 1 if t0 >= ntl else 0
            hp, hpb = hp_r.get()
            xp, xpb = xp_r.get()
            for i in range(2):
                self.dma("act", hp[:, :, i * 128:(i + 1) * 128], H2T[t0 + i], [h2tb[t0 + i]], [hpb])
                self.dma("act", xp[:, i, :], XO[ltok(t0 + i), :], [xob[t0 + i]], [xpb])
            def gu(j):
                gk, gbk = self.bank()
                uk, ubk = self.bank()
                for k in range(8):
                    self.mm(gk[:, 0:256], Wi[:, k, j * 128:(j + 1) * 128], hp[:, k, :], k == 0, k == 7, [Wib, hpb], [gbk])
                for k in range(8):
                    self.mm(uk[:, 0:256], Wi[:, k, HID + j * 128:HID + (j + 1) * 128], hp[:, k, :], k == 0, k == 7, [Wib, hpb], [ubk])
                sg, sgb = sg_r.get()
                self.act(sg[:], gk[:, 0:256], AF.Silu, [gbk], [sgb])
                ac, acb = ac_r.get()
                self.tt("dve", ac[:], sg[:], uk[:, 0:256], ALU.mult, [sgb, ubk], [acb])
                return ac, acb

            nxt = gu(0)
            for j in range(NCH):
                ac, acb = nxt
                if j + 1 < NCH:
                    nxt = gu(j + 1)
                for i in range(2):
                    for hfc in range(2):
                        a_, ab_ = accs[i * 2 + hfc]
                        self.mm(a_[:, :], ac[:, i * 128:(i + 1) * 128], Wf[:, j, hfc * 512:(hfc + 1) * 512], j == 0, j == NCH - 1, [acb, Wfb], [ab_])
            for i in range(2):
                tm, tmb = tm_r.get()
                for hfc in range(2):
                    a_, ab_ = accs[i * 2 + hfc]
                    self.tt("dve", tm[:, hfc * 512:(hfc + 1) * 512], a_[:, :], MODB[:, var, 3, hfc * 512:(hfc + 1) * 512], ALU.mult, [ab_, MODBb], [tmb])
                xo, xo_b = xo_r.get()
                self.tt("pool", xo[:], tm[:], xp[:, i, :], ALU.add, [tmb, xpb], [xo_b])
                self.dma("sp", XO[ltok(t0 + i), :], xo[:], [xo_b], [xob[t0 + i]])
        self.brange = (0, 8)
        S.barrier()
        self.sb.release(pmark)
        return XO, xob

    def finish_all(self):
        S = self.S
        deps = [(k, v) for k, v in S.cnt.items() if v > 0]
        S._wait("sp", deps)


def _perm_win(w_in_l):
    idx = []
    for j in range(4):
        idx += list(range(j * 64, (j + 1) * 64)) + list(range((j + 4) * 64, (j + 5) * 64))
    idx += list(range(512, INW))
    return np.ascontiguousarray(w_in_l[:, idx])


def _phaseA_inputs(inp, l, core, consts):
    b, r = core // 4, core % 4
    cT = np.stack([inp["c"][b].reshape(8, 128).T, inp["c_ctx"].reshape(8, 128).T], axis=-1).astype(np.float32)
    wg = np.stack([np.concatenate([inp["gla_w_gate_f"][l], inp["gla_b_gate_f"][l][None]], 0),
                   np.concatenate([inp["gla_w_gate_b"][l], inp["gla_b_gate_b"][l][None]], 0)], 0).astype(np.float32)
    m = {
        "cT": np.ascontiguousarray(cT),
        "wmod_A": inp["w_mod"][l], "bmod_A": inp["b_mod"][l], "g1_A": inp["g_norm1"][l],
        "win_A": _perm_win(inp["w_in"][l]), "qg_A": inp["q_norm_g"][l], "kg_A": inp["k_norm_g"][l],
        "wg_A": np.ascontiguousarray(wg), "rope": _rope_tables(r * TOK, TOK),
        "cm": consts["cm"], "ci": consts["ci"], "gm": consts["gm"], "bd": consts["bd"], "ident_f": consts["ident_f"],
    }
    return m


_TAB_CACHE = {}


def _dft_cached(s0, ns, n):
    key = (s0, ns, n)
    if key not in _TAB_CACHE:
        _TAB_CACHE[key] = _dft_tables(s0, ns, n)
    return _TAB_CACHE[key]


def _cb_const():
    c = np.arange(64)
    ang = 2.0 * np.pi * ((c[:, None] * c[None, :]) % 64) / 64.0
    cc = np.cos(ang) / 8.0
    sc = -np.sin(ang) / 8.0
    cb = np.zeros((128, 2, 128), np.float32)
    for gi in range(2):
        cb[gi * 64:(gi + 1) * 64, 0, gi * 64:(gi + 1) * 64] = cc
        cb[gi * 64:(gi + 1) * 64, 1, gi * 64:(gi + 1) * 64] = sc
    return cb


def _phaseB_inputs(inp, l, core, outs, consts):
    b, r = core // 4, core % 4
    o = outs[core]
    grp = [outs[4 * b + rr] for rr in range(4)]
    m = {}
    m["i_MODV"] = o["MODV_A"]
    m["i_QT"] = o["QT"]
    kfull = np.concatenate([g["KT"][:, :TOK] for g in grp], axis=1)
    vfull = np.concatenate([g["V"][:TOK] for g in grp], axis=0)
    kz = np.zeros((128, 128), kfull.dtype)
    vz = np.zeros((128, 128), vfull.dtype)
    lo, hi = r * TOK, (r + 1) * TOK
    m["i_KTE"] = np.ascontiguousarray(np.concatenate(
        [kfull[:, lo - 128:lo] if r > 0 else kz, kfull[:, lo:hi], kfull[:, hi:hi + 128] if r < 3 else kz, o["KT"][:, TOK:]], axis=1))
    m["i_VE"] = np.ascontiguousarray(np.concatenate(
        [vfull[lo - 128:lo] if r > 0 else vz, vfull[lo:hi], vfull[hi:hi + 128] if r < 3 else vz, o["V"][TOK:]], axis=0))
    m["i_UALL"] = np.ascontiguousarray(np.concatenate([g["FU"][:TOK] for g in grp], axis=0))
    m["i_FUC"] = np.ascontiguousarray(o["FU"][TOK:])
    m["i_OLOC"] = o["OLOC"]
    m["i_QTIL"] = o["QTIL"]
    m["i_SR"] = o["SR"]
    m["i_DSg"] = np.ascontiguousarray(np.stack([g["DS"] for g in grp], axis=1))
    m["i_SLOCg"] = np.ascontiguousarray(np.stack([g["SLOC"] for g in grp], axis=1))
    m["i_SCTX"] = o["SCTX"]
    rm = np.zeros((128, 4, 2), np.float32)
    for rr in range(4):
        rm[:, rr, 0] = 1.0 if rr < r else 0.0
        rm[:, rr, 1] = 1.0 if rr > r else 0.0
    m["i_rmask"] = rm
    z = np.zeros((128, 128), np.float32)
    m["i_amask"] = np.ascontiguousarray(np.stack([consts["mp"] if r > 0 else z, consts["mp"], consts["mn"], consts["mn"] if r < 3 else z], axis=1))
    m["i_TAB"] = _dft_cached(r * TOK, TOK, SEQ)
    m["i_TABC"] = _dft_cached(0, CTX, CTX)
    m["i_CB"] = consts["cbc"]
    m["i_wf"] = inp["w_fourier"][l]
    m["i_glag"] = inp["gla_norm_g"][l]
    m["i_sink"] = inp["attn_sink"][l]
    m["i_wout"] = inp["w_out"][l]
    m["i_g2"] = inp["g_norm2"][l]
    m["i_wffi"] = inp["w_ffn_in"][l]
    m["i_wffo"] = inp["w_ffn_out"][l]
    m["ident_f"] = consts["ident_f"]
    return m


def build_stage(stage):
    P = Prog(stage)
    P.setup_common()
    xin = P.inp("xin", [NTOK, D])
    if stage == 0:
        P.phase_A(0, xin, P.dbuf["xin"], True)
    elif stage == 1:
        P.A_mod()
        P.phase_B(0, xin, P.dbuf["xin"], True)
        P.phase_A(1, P.dout["XO"], P.dbuf["XO"], False)
    else:
        P.phase_B(1, xin, P.dbuf["xin"], False)
    P.finish_all()
    return P


_PROGS = {}


def build_fused():
    P = Prog("f")
    P.setup_common()
    xin = P.inp("xin", [NTOKF, D])
    xb = P.dbuf["xin"]
    P.A_mod_f(0)
    P.A_mod_f(1)
    P.phase_A_f(0, xin, lambda t: [xb])
    X1, xob0 = P.phase_B_f(0, "all", xin, lambda t: [xb])
    if DBG.get("stop_after") == "B0":
        P.finish_all()
        return P
    P.phase_A_f(1, X1, lambda t: [xob0[t]])
    X1o, x1ob = P.own_gather(1, X1, xob0)
    P.phase_B_f(1, "own", X1o, lambda t: [x1ob])
    P.finish_all()
    return P


def _fused_inputs(inp, core, consts, shared):
    b, r = core // 4, core % 4
    m = {}
    m["xin"] = np.ascontiguousarray(np.concatenate([inp["x"][b], inp["ctx"][b]], axis=0))
    m["cT"] = np.ascontiguousarray(np.stack([inp["c"][b].reshape(8, 128).T, inp["c_ctx"].reshape(8, 128).T], axis=-1).astype(np.float32))
    for l in range(DEPTH):
        L = "%d" % l
        m["wmod" + L] = inp["w_mod"][l]
        m["bmod" + L] = inp["b_mod"][l]
        m["g1" + L] = inp["g_norm1"][l]
        m["win" + L] = shared["win"][l]
        m["qg" + L] = inp["q_norm_g"][l]
        m["kg" + L] = inp["k_norm_g"][l]
        m["wg" + L] = shared["wg"][l]
        m["wf" + L] = inp["w_fourier"][l]
        m["glag" + L] = inp["gla_norm_g"][l]
        m["sink" + L] = inp["attn_sink"][l]
        m["wout" + L] = inp["w_out"][l]
        m["g2" + L] = inp["g_norm2"][l]
        m["wffi" + L] = inp["w_ffn_in"][l]
        m["wffo" + L] = inp["w_ffn_out"][l]
    m["rope"] = shared["rope"]
    for k in ("cm", "ci", "gm", "bd", "ident_f"):
        m[k] = consts[k]
    z = np.zeros((128, 128), np.float32)
    m["amask0"] = np.ascontiguousarray(np.stack([z, consts["mp"], consts["mn"], z], axis=1))
    m["amask1"] = np.ascontiguousarray(np.stack([consts["mp"] if r > 0 else z, consts["mp"], consts["mn"], consts["mn"] if r < 3 else z], axis=1))
    m["TAB"] = shared["TAB"]
    m["TABOWN"] = np.ascontiguousarray(shared["TAB"][:, r * 32:(r + 1) * 32])
    m["TABC"] = shared["TABC"]
    m["CB"] = consts["cbc"]
    return m


def _shared(inp):
    sh = {}
    sh["win"] = [_perm_win(inp["w_in"][l]) for l in range(DEPTH)]
    sh["wg"] = [np.ascontiguousarray(np.stack([np.concatenate([inp["gla_w_gate_f"][l], inp["gla_b_gate_f"][l][None]], 0),
                                               np.concatenate([inp["gla_w_gate_b"][l], inp["gla_b_gate_b"][l][None]], 0)], 0).astype(np.float32))
                for l in range(DEPTH)]
    sh["rope"] = _rope_tables(0, SEQ)
    tab = _dft_cached(0, SEQ, SEQ).reshape(2, 16, 8, 8, 128, 512).transpose(0, 1, 2, 4, 3, 5)
    sh["TAB"] = np.ascontiguousarray(tab).reshape(2, 16 * 8, 128, 8 * 512)
    tabc = _dft_cached(0, CTX, CTX).reshape(2, 1, 2, 128, 256).transpose(0, 1, 3, 2, 4)
    sh["TABC"] = np.ascontiguousarray(tabc).reshape(2, 1, 128, 2 * 256)
    return sh


def kernel(**inputs):
    inp = {k: np.asarray(v) for k, v in inputs.items()}
    consts = _host_consts()
    sh = _shared(inp)
    if "f" not in _PROGS:
        _PROGS["f"] = build_fused()
    P = _PROGS["f"]
    maps = [_fused_inputs(inp, core, consts, sh) for core in range(8)]
    res = run_bass_kernel_spmd(P.nc, maps, core_ids=list(range(8)))
    out = np.zeros((NB, SEQ, D), np.float32)
    for core in range(8):
        b, r = core // 4, core % 4
        out[b, r * TOK:(r + 1) * TOK] = np.asarray(res.results[core]["XOUT"])
    return out


DBG_OUT = {}
```
